# Optimizing a Trainium2 kernel written in Bass

```python
import jax
import jax.numpy as jnp
from jax import lax
import numpy as np

D_MODEL = 1024
BATCH = 8
SEQ = 2048
DEPTH = 2
DEC_BATCH = 32
DEC_SEQ = 4
PAST_LEN = 16384
PAGE_SIZE = 128

HEAD_DIM = 64
N_GROUP_HEADS = 4
GROUP_WIDTH = N_GROUP_HEADS * HEAD_DIM
D_MIX = 4 * GROUP_WIDTH
D_FF = 2816
N_FFN = 2
N_NORMS = 6
EPS = 1e-6

GLA_DK = HEAD_DIM // 2
GLA_DV = HEAD_DIM
GLA_RANK = 16
GLA_TAU = 16.0
GLA_CHUNK = 64

GMLP_CHUNK = 128
LN_EPS = 1e-5

RWKV_N = HEAD_DIM
RWKV_W_RANK = 64
RWKV_A_RANK = 64
RWKV_G_RANK = 128
RWKV_LNX_EPS = 64e-5

DIL_PAIRS = ((128, 1), (512, 4), (2048, 16))
DIL_N_GROUPS = len(DIL_PAIRS)
N_DIL_HEADS = DIL_N_GROUPS * N_GROUP_HEADS
DIL_QBLOCK = 128
ALIBI_MAX_EXP = 8.0

GLA_COLS = 2 * N_GROUP_HEADS * GLA_DK + GROUP_WIDTH + GLA_RANK + GROUP_WIDTH
GMLP_COLS = 2 * GROUP_WIDTH
RWKV_COLS = 3 * GROUP_WIDTH + RWKV_W_RANK + RWKV_A_RANK + RWKV_G_RANK
DIL_COLS = DIL_N_GROUPS * 3 * GROUP_WIDTH
N_IN_COLS = GLA_COLS + GMLP_COLS + RWKV_COLS + DIL_COLS

kernel_name = 'hybrid_parallel_groups_decode_step'


def rms_norm(x, g):
    xf = x.astype(jnp.float32)
    y = xf * lax.rsqrt(jnp.mean(xf * xf, axis=-1, keepdims=True) + EPS)
    return (y * g.astype(jnp.float32)).astype(x.dtype)


def swiglu(x, wg, wu, wd):
    return (jax.nn.silu(x @ wg) * (x @ wu)) @ wd


def split_at(x, sizes):
    return jnp.split(x, np.cumsum(sizes)[:-1].tolist(), axis=-1)


def gla_mix(cols, w_alpha2, b_alpha, norm_g, h0):
    B, T, _ = cols.shape
    H = N_GROUP_HEADS
    f32 = jnp.float32
    q, k, v, a_lr, g = split_at(cols.astype(f32), [H * GLA_DK, H * GLA_DK, GROUP_WIDTH, GLA_RANK, GROUP_WIDTH])
    q = q.reshape(B, T, H, GLA_DK) * GLA_DK ** -0.5
    k = k.reshape(B, T, H, GLA_DK)
    v = v.reshape(B, T, H, GLA_DV)
    log_a = (jax.nn.log_sigmoid(a_lr @ w_alpha2.astype(f32) + b_alpha.astype(f32)) / GLA_TAU).reshape(B, T, H, GLA_DK)
    C = min(GLA_CHUNK, T)
    pad = (-T) % C
    n = (T + pad) // C

    def to_chunks(z):
        return jnp.pad(z, ((0, 0), (0, pad), (0, 0), (0, 0))).reshape(B, n, C, H, z.shape[-1])

    q, k, v, log_a = (to_chunks(z) for z in (q, k, v, log_a))
    b = jnp.cumsum(log_a, axis=2)
    b_last = b[:, :, -1:]
    qe = q * jnp.exp(b)
    ke = k * jnp.exp(-b)
    kd = k * jnp.exp(b_last - b)
    causal = jnp.tril(jnp.ones((C, C), dtype=bool))
    att = jnp.where(causal, jnp.einsum('bnthk,bnshk->bnhts', qe, ke), 0.0)
    o_intra = jnp.einsum('bnhts,bnshv->bnthv', att, v)
    incr = jnp.einsum('bnshk,bnshv->bnhkv', kd, v)
    decay = jnp.exp(b_last[:, :, 0])

    def step(S, xs):
        dec, inc = xs
        return dec[..., None] * S + inc, S

    s_final, s_in = lax.scan(step, h0.astype(f32), (jnp.moveaxis(decay, 1, 0), jnp.moveaxis(incr, 1, 0)))
    s_in = jnp.moveaxis(s_in, 0, 1)
    o = o_intra + jnp.einsum('bnthk,bnhkv->bnthv', qe, s_in)
    o = o.reshape(B, n * C, H, GLA_DV)[:, :T]
    o = o * lax.rsqrt(jnp.mean(o * o, axis=-1, keepdims=True) + EPS)
    o = o.reshape(B, T, GROUP_WIDTH) * norm_g.astype(f32) * jax.nn.silu(g)
    return o.astype(cols.dtype), s_final


def gmlp_mix(cols, ln_g, ln_b, w_s, b_s):
    B, T, _ = cols.shape
    f32 = jnp.float32
    u, v = jnp.split(jax.nn.gelu(cols.astype(f32), approximate=False), 2, axis=-1)
    mean = jnp.mean(v, axis=-1, keepdims=True)
    var = jnp.mean(jnp.square(v - mean), axis=-1, keepdims=True)
    v = (v - mean) * lax.rsqrt(var + LN_EPS) * ln_g.astype(f32) + ln_b.astype(f32)
    pad = (-T) % GMLP_CHUNK
    n = (T + pad) // GMLP_CHUNK
    vc = jnp.pad(v, ((0, 0), (0, pad), (0, 0))).reshape(B, n, GMLP_CHUNK, N_GROUP_HEADS, HEAD_DIM)
    causal = jnp.tril(jnp.ones((GMLP_CHUNK, GMLP_CHUNK), dtype=bool))
    w = jnp.where(causal, w_s.astype(f32), 0.0)
    s = jnp.einsum('gts,bnsgc->bntgc', w, vc) + b_s.astype(f32).T[None, None, :, :, None]
    s = s.reshape(B, n * GMLP_CHUNK, GROUP_WIDTH)[:, :T]
    return (u * s).astype(cols.dtype), v.astype(cols.dtype)


def rwkv_mix(cols, shift_prev, s0, mu, w0, w2, a0, a2, g2, k_k, k_a, r_k, lnx_g, lnx_b):
    B, T, _ = cols.shape
    H, N = N_GROUP_HEADS, RWKV_N
    f32 = jnp.float32
    c = cols.astype(f32)
    prev = jnp.concatenate([shift_prev.astype(f32)[:, None], c[:, :-1]], axis=1)
    xm = c + (prev - c) * mu.astype(f32)
    r, k, v, w_lr, a_lr, g_lr = split_at(xm, [GROUP_WIDTH] * 3 + [RWKV_W_RANK, RWKV_A_RANK, RWKV_G_RANK])
    w = -jax.nn.softplus(-(w0.astype(f32) + jnp.tanh(w_lr) @ w2.astype(f32))) - 0.5
    decay = jnp.exp(-jnp.exp(w))
    a = jax.nn.sigmoid(a0.astype(f32) + a_lr @ a2.astype(f32))
    g = jax.nn.sigmoid(g_lr) @ g2.astype(f32)

    def heads(z):
        return z.reshape(B, T, H, N)

    kk = heads(k * k_k.astype(f32))
    kk = kk / jnp.maximum(jnp.linalg.norm(kk, axis=-1, keepdims=True), 1e-12)
    k = k * (1.0 + (a - 1.0) * k_a.astype(f32))
    r, k, v, decay, a = heads(r), heads(k), heads(v), heads(decay), heads(a)

    def step(S, xs):
        r_t, w_t, k_t, v_t, a_t, b_t = xs
        sa = jnp.einsum('bhvk,bhk->bhv', S, a_t)
        S = S * w_t[:, :, None, :] + sa[..., None] * b_t[:, :, None, :] + v_t[..., None] * k_t[:, :, None, :]
        return S, jnp.einsum('bhvk,bhk->bhv', S, r_t)

    def tm(z):
        return jnp.moveaxis(z, 1, 0)

    s_final, y = lax.scan(step, s0.astype(f32), (tm(r), tm(decay), tm(k), tm(v), tm(-kk), tm(kk * a)))
    y = jnp.moveaxis(y, 0, 1)
    mean = jnp.mean(y, axis=-1, keepdims=True)
    var = jnp.mean(jnp.square(y - mean), axis=-1, keepdims=True)
    y = ((y - mean) * lax.rsqrt(var + RWKV_LNX_EPS)).reshape(B, T, GROUP_WIDTH) * lnx_g.astype(f32) + lnx_b.astype(f32)
    bonus = jnp.sum(r * k * r_k.astype(f32).reshape(H, N), axis=-1, keepdims=True) * v
    o = (y + bonus.reshape(B, T, GROUP_WIDTH)) * g
    return o.astype(cols.dtype), s_final, cols[:, -1]


def dilated_group(q, kv_new, past, window, dilation, slopes):
    B, T, H, dh = q.shape
    L = past.shape[1]
    f32 = jnp.float32
    J = window // dilation + 1
    ext = jnp.concatenate([jnp.zeros((B, window - L, 2, H, dh), kv_new.dtype), past.astype(kv_new.dtype), kv_new], axis=1)
    qb = T if T <= DIL_QBLOCK else DIL_QBLOCK
    nb = T // qb
    offs = jnp.arange(J) * dilation
    bias = -slopes[:, None] * offs[None, :].astype(f32)

    def block(args):
        nblk, qblk = args
        i = nblk * qb + jnp.arange(qb)
        idx = window + i[:, None] - offs[None, :]
        kvg = jnp.take(ext, idx, axis=1)
        valid = (i[:, None] - offs[None, :]) >= -L
        s = jnp.einsum('bqhd,bqjhd->bhqj', qblk, kvg[:, :, :, 0]).astype(f32) + bias[None, :, None, :]
        s = jnp.where(valid[None, None], s, -jnp.inf)
        m = jnp.max(s, axis=-1, keepdims=True)
        p = jnp.exp(s - m)
        den = jnp.sum(p, axis=-1)
        o = jnp.einsum('bhqj,bqjhd->bqhd', p, kvg[:, :, :, 1].astype(f32)) / jnp.transpose(den, (0, 2, 1))[..., None]
        lse = jnp.transpose(m[..., 0] + jnp.log(den), (0, 2, 1))
        return o, lse

    qs = jnp.swapaxes(q.reshape(B, nb, qb, H, dh), 0, 1)
    o, lse = lax.map(block, (jnp.arange(nb), qs))
    o = jnp.swapaxes(o, 0, 1).reshape(B, T, H, dh)
    lse = jnp.swapaxes(lse, 0, 1).reshape(B, T, H)
    return o, lse


def dilated_mix(cols, pasts):
    B, T, _ = cols.shape
    qkv = cols.reshape(B, T, DIL_N_GROUPS, 3, N_GROUP_HEADS, HEAD_DIM)
    slopes = jnp.exp2(-ALIBI_MAX_EXP * (jnp.arange(N_DIL_HEADS, dtype=jnp.float32) + 1.0) / N_DIL_HEADS)
    slopes = slopes.reshape(DIL_N_GROUPS, N_GROUP_HEADS)
    outs, lses, new_kv = [], [], []
    for gi, (window, dilation) in enumerate(DIL_PAIRS):
        q = qkv[:, :, gi, 0] * HEAD_DIM ** -0.5
        kv = qkv[:, :, gi, 1:]
        o, lse = dilated_group(q, kv, pasts[gi], window, dilation, slopes[gi])
        outs.append(o)
        lses.append(lse)
        new_kv.append(kv[:, -min(window, T):])
    wts = jax.nn.softmax(jnp.stack(lses, axis=0), axis=0)
    o = jnp.sum(wts[..., None] * jnp.stack(outs, axis=0), axis=0)
    return o.reshape(B, T, GROUP_WIDTH).astype(cols.dtype), new_kv


def trunk_layer(x, p, gla_h0, rwkv_s0, shift0, pasts):
    ng = p['norms']
    h = x + 0.5 * rms_norm(swiglu(rms_norm(x, ng[0]), p['ff_gate'][0], p['ff_up'][0], p['ff_down'][0]), ng[1])
    cols = rms_norm(h, ng[2]) @ p['w_in']
    c_gla, c_gmlp, c_rwkv, c_dil = split_at(cols, [GLA_COLS, GMLP_COLS, RWKV_COLS, DIL_COLS])
    o_gla, gla_s = gla_mix(c_gla, p['gla_w_alpha2'], p['gla_b_alpha'], p['gla_norm'], gla_h0)
    o_gmlp, gmlp_v = gmlp_mix(c_gmlp, p['gmlp_ln_g'], p['gmlp_ln_b'], p['gmlp_ws'], p['gmlp_bs'])
    o_rwkv, rwkv_s, shift = rwkv_mix(c_rwkv, shift0, rwkv_s0, p['rwkv_mu'], p['rwkv_w0'], p['rwkv_w2'],
                                     p['rwkv_a0'], p['rwkv_a2'], p['rwkv_g2'], p['rwkv_kk'], p['rwkv_ka'],
                                     p['rwkv_rk'], p['rwkv_lnx_g'], p['rwkv_lnx_b'])
    o_dil, kv_new = dilated_mix(c_dil, pasts)
    mix = jnp.concatenate([o_gla, o_gmlp, o_rwkv, o_dil], axis=-1) @ p['w_out']
    h = h + rms_norm(mix, ng[3])
    h = h + 0.5 * rms_norm(swiglu(rms_norm(h, ng[4]), p['ff_gate'][1], p['ff_up'][1], p['ff_down'][1]), ng[5])
    return h, gla_s, rwkv_s, shift, gmlp_v, kv_new


def setup_inputs(seed: int = 0) -> dict:
    key = jax.random.key(seed)
    ks = iter(jax.random.split(key, 48))
    f32 = jnp.float32

    def nrm(shape, scale):
        return jax.random.normal(next(ks), shape, f32) * scale

    def unif(shape, lo, hi):
        return jax.random.uniform(next(ks), shape, f32, minval=lo, maxval=hi)

    inp = {}
    inp['x_prompt'] = nrm((BATCH, SEQ, D_MODEL), 1.0)
    inp['x_sample'] = nrm((DEC_BATCH, DEC_SEQ, D_MODEL), 1.0)
    for window, _ in DIL_PAIRS:
        inp['cache_win%d' % window] = nrm((DEPTH, DEC_BATCH, min(window, PAST_LEN), 2, N_GROUP_HEADS, HEAD_DIM), 1.0)
    inp['state_gla'] = nrm((DEPTH, DEC_BATCH, N_GROUP_HEADS, GLA_DK, GLA_DV), 1.0)
    inp['state_rwkv'] = nrm((DEPTH, DEC_BATCH, N_GROUP_HEADS, RWKV_N, RWKV_N), 0.5)
    inp['state_shift'] = nrm((DEPTH, DEC_BATCH, RWKV_COLS), 1.0)
    inp['norm_gains'] = 1.0 + nrm((DEPTH, N_NORMS, D_MODEL), 0.05)
    inp['w_ff_gate'] = nrm((DEPTH, N_FFN, D_MODEL, D_FF), D_MODEL ** -0.5)
    inp['w_ff_up'] = nrm((DEPTH, N_FFN, D_MODEL, D_FF), D_MODEL ** -0.5)
    inp['w_ff_down'] = nrm((DEPTH, N_FFN, D_FF, D_MODEL), D_FF ** -0.5)
    inp['w_in'] = nrm((DEPTH, D_MODEL, N_IN_COLS), D_MODEL ** -0.5)
    inp['w_out'] = nrm((DEPTH, D_MIX, D_MODEL), D_MIX ** -0.5)
    inp['gla_w_alpha2'] = nrm((DEPTH, GLA_RANK, N_GROUP_HEADS * GLA_DK), GLA_RANK ** -0.5)
    inp['gla_b_alpha'] = nrm((DEPTH, N_GROUP_HEADS * GLA_DK), 0.1)
    inp['gla_norm'] = 1.0 + nrm((DEPTH, GROUP_WIDTH), 0.05)
    inp['gmlp_ln_g'] = 1.0 + nrm((DEPTH, GROUP_WIDTH), 0.05)
    inp['gmlp_ln_b'] = nrm((DEPTH, GROUP_WIDTH), 0.05)
    inp['gmlp_ws'] = nrm((DEPTH, N_GROUP_HEADS, GMLP_CHUNK, GMLP_CHUNK), GMLP_CHUNK ** -0.5)
    inp['gmlp_bs'] = 1.0 + nrm((DEPTH, N_GROUP_HEADS, GMLP_CHUNK), 0.05)
    inp['rwkv_mu'] = unif((DEPTH, RWKV_COLS), 0.0, 1.0)
    inp['rwkv_w0'] = unif((DEPTH, GROUP_WIDTH), -6.5, -1.5)
    inp['rwkv_w2'] = nrm((DEPTH, RWKV_W_RANK, GROUP_WIDTH), 0.1)
    inp['rwkv_a0'] = nrm((DEPTH, GROUP_WIDTH), 0.1)
    inp['rwkv_a2'] = nrm((DEPTH, RWKV_A_RANK, GROUP_WIDTH), RWKV_A_RANK ** -0.5)
    inp['rwkv_g2'] = nrm((DEPTH, RWKV_G_RANK, GROUP_WIDTH), RWKV_G_RANK ** -0.5)
    inp['rwkv_kk'] = 0.85 + nrm((DEPTH, GROUP_WIDTH), 0.05)
    inp['rwkv_ka'] = 1.0 + nrm((DEPTH, GROUP_WIDTH), 0.05)
    inp['rwkv_rk'] = nrm((DEPTH, GROUP_WIDTH), 0.1)
    inp['rwkv_lnx_g'] = 1.0 + nrm((DEPTH, GROUP_WIDTH), 0.05)
    inp['rwkv_lnx_b'] = nrm((DEPTH, GROUP_WIDTH), 0.05)
    return inp


def reference(x_prompt, x_sample, cache_win128, cache_win512, cache_win2048, state_gla, state_rwkv, state_shift,
              norm_gains, w_ff_gate, w_ff_up, w_ff_down, w_in, w_out, gla_w_alpha2, gla_b_alpha, gla_norm,
              gmlp_ln_g, gmlp_ln_b, gmlp_ws, gmlp_bs, rwkv_mu, rwkv_w0, rwkv_w2, rwkv_a0, rwkv_a2, rwkv_g2,
              rwkv_kk, rwkv_ka, rwkv_rk, rwkv_lnx_g, rwkv_lnx_b):
    f32 = jnp.float32
    Bp = x_prompt.shape[0]
    caches = (cache_win128, cache_win512, cache_win2048)
    gla0 = jnp.zeros((Bp, N_GROUP_HEADS, GLA_DK, GLA_DV), f32)
    rwkv0 = jnp.zeros((Bp, N_GROUP_HEADS, RWKV_N, RWKV_N), f32)
    shift0 = jnp.zeros((Bp, RWKV_COLS), x_prompt.dtype)
    past0 = jnp.zeros((Bp, 0, 2, N_GROUP_HEADS, HEAD_DIM), x_prompt.dtype)
    yp, ys = x_prompt, x_sample
    pg, pr, psh, pw = [], [], [], [[], [], []]
    sg, sr, ssh, sw, sv = [], [], [], [[], [], []], []
    for l in range(DEPTH):
        p = dict(norms=norm_gains[l], ff_gate=w_ff_gate[l], ff_up=w_ff_up[l], ff_down=w_ff_down[l],
                 w_in=w_in[l], w_out=w_out[l], gla_w_alpha2=gla_w_alpha2[l], gla_b_alpha=gla_b_alpha[l],
                 gla_norm=gla_norm[l], gmlp_ln_g=gmlp_ln_g[l], gmlp_ln_b=gmlp_ln_b[l], gmlp_ws=gmlp_ws[l],
                 gmlp_bs=gmlp_bs[l], rwkv_mu=rwkv_mu[l], rwkv_w0=rwkv_w0[l], rwkv_w2=rwkv_w2[l],
                 rwkv_a0=rwkv_a0[l], rwkv_a2=rwkv_a2[l], rwkv_g2=rwkv_g2[l], rwkv_kk=rwkv_kk[l],
                 rwkv_ka=rwkv_ka[l], rwkv_rk=rwkv_rk[l], rwkv_lnx_g=rwkv_lnx_g[l], rwkv_lnx_b=rwkv_lnx_b[l])
        yp, g_p, r_p, sh_p, _, kv_p = trunk_layer(yp, p, gla0, rwkv0, shift0, [past0, past0, past0])
        ys, g_s, r_s, sh_s, v_s, kv_s = trunk_layer(ys, p, state_gla[l], state_rwkv[l], state_shift[l],
                                                    [c[l] for c in caches])
        pg.append(g_p)
        pr.append(r_p)
        psh.append(sh_p)
        sg.append(g_s)
        sr.append(r_s)
        ssh.append(sh_s)
        sv.append(v_s)
        for gi in range(DIL_N_GROUPS):
            pw[gi].append(kv_p[gi])
            sw[gi].append(kv_s[gi])
    return (yp, ys,
            jnp.stack(pg), jnp.stack(pr), jnp.stack(psh),
            jnp.stack(pw[0]), jnp.stack(pw[1]), jnp.stack(pw[2]),
            jnp.stack(sg), jnp.stack(sr), jnp.stack(ssh),
            jnp.stack(sw[0]), jnp.stack(sw[1]), jnp.stack(sw[2]),
            jnp.stack(sv))
```

```python
import os
import numpy as np
import concourse.bass as bass
import concourse.mybir as mybir
from contextlib import ExitStack
from concourse.bass_utils import run_bass_kernel_spmd

F32 = mybir.dt.float32
BF16 = mybir.dt.bfloat16
ALU = mybir.AluOpType
AF = mybir.ActivationFunctionType
AX = mybir.AxisListType

ENGS = ["pe", "dve", "act", "pool", "sp"]

D = 1024
T = 2048
SLOT = 64
NSEQ = 4
TT = T + NSEQ * SLOT
DFF = 2816
NFFC = DFF // 128
NCOL = 4624
EPS = 1e-6
TILES = [(0, 512), (512, 512), (1024, 512), (1536, 512), (2048, 256)]
C_GLA, C_GMLP, C_RWKV, C_DIL = 0, 784, 1296, 2320
DIL = ((128, 1), (512, 4), (2048, 16))
NEG = -30000.0
STAGE = int(os.environ.get("K_STAGE", "99"))
AW = 50000

V_GAIN = 0
V_BALPHA = 48
V_GLAN = 49
V_LNG = 51
V_LNB = 53
V_MU = 55
V_W0, V_A0, V_KK, V_KA, V_RK, V_LXG, V_LXB = 63, 65, 67, 69, 71, 73, 75
NVEC = 77


class Prog:
    def __init__(self, nc, n_dma_sems=28):
        self.nc = nc
        self.ops = []
        self.n_dma_sems = n_dma_sems
        self.es = ExitStack()
        self.out_finals = []

    def sb(self, name, shape, dt=F32):
        return self.es.enter_context(self.nc.sbuf_tensor(name, list(shape), dt))

    def ps(self, name, shape, dt=F32):
        return self.es.enter_context(self.nc.psum_tensor(name, list(shape), dt))

    def op(self, eng, fn, r=(), w=()):
        extra = tuple(k[0] for k in tuple(r) + tuple(w) if isinstance(k, tuple) and isinstance(k[0], str) and k[0][:2] == "pb" and k[0][2:].isdigit())
        self.ops.append(dict(eng=eng, fn=fn, r=tuple(r) + extra + ("__ph",), w=tuple(w), dma=False, bar=False))

    OUT_KEYS = ("yT", "okv", "o_gla", "o_rwkv", "o_shift", "o_gv", "dbg")

    def dma(self, eng, out, in_, r=(), w=(), **kw):
        def fn(e, out=out, in_=in_, kw=kw):
            return e.dma_start(out=out, in_=in_, **kw)
        w2 = []
        for k in w:
            if isinstance(k, str) and k in self.OUT_KEYS:
                k = (k, len(self.out_finals))
                self.out_finals.append(k)
            w2.append(k)
        self.ops.append(dict(eng=eng, fn=fn, r=tuple(r) + ("__ph",), w=tuple(w2), dma=True, bar=False))

    def barrier(self, fn):
        self.ops.append(dict(eng="dve", fn=fn, r=(), w=("__ph",), dma=False, bar=True))

    def emit(self, final_keys=()):
        nc = self.nc
        ops = self.ops
        n = len(ops)
        last_w = {}
        readers = {}
        deps = [set() for _ in range(n)]
        dma_rr = 0
        dma_rr_sw = 0
        dma_last = {}
        last_eng = {}
        bank_last = {}

        def bank_of(k):
            if isinstance(k, tuple):
                k = k[0]
            if isinstance(k, str) and k[:2] == "pb" and k[2:3].isdigit():
                return k[0:3]
            return None

        for i, o in enumerate(ops):
            d = deps[i]
            o["bankdeps"] = set()
            for bnk in {bank_of(k) for k in o["r"] + o["w"]} - {None}:
                if bnk in bank_last and ops[bank_last[bnk]]["eng"] != o["eng"]:
                    o["bankdeps"].add(bank_last[bnk])
                bank_last[bnk] = i
            if o["bar"]:
                d.update(last_eng.values())
                d.update(dma_last.values())
            else:
                for k in o["r"]:
                    if k in last_w:
                        d.add(last_w[k])
                for k in o["w"]:
                    if k in last_w:
                        d.add(last_w[k])
                    for j in readers.get(k, ()):
                        d.add(j)
            if o["dma"]:
                half = self.n_dma_sems // 2
                if o["eng"] == "pool":
                    s = half + dma_rr_sw % (self.n_dma_sems - half)
                    dma_rr_sw += 1
                else:
                    s = dma_rr % half
                    dma_rr += 1
                o["dsem"] = s
                if s in dma_last:
                    d.add(dma_last[s])
                dma_last[s] = i
            else:
                last_eng[o["eng"]] = i
            d.discard(i)
            if not o["bar"]:
                for k in o["r"]:
                    if k != "__ph":
                        readers.setdefault(k, []).append(i)
            for k in o["w"]:
                last_w[k] = i
                readers[k] = []
        fin = set()
        for k in final_keys:
            if k in last_w:
                fin.add(last_w[k])
        needed = set()
        for i, o in enumerate(ops):
            nd = set()
            for j in deps[i]:
                p = ops[j]
                if (not p["dma"]) and (not o["dma"]) and p["eng"] == o["eng"]:
                    if o["eng"] == "pe":
                        continue
                nd.add(j)
            nd |= o["bankdeps"]
            deps[i] = nd
            needed |= nd
        needed |= fin
        cnt = {e: 0 for e in ENGS}
        dcnt = {}
        for i, o in enumerate(ops):
            if o["dma"]:
                s = o["dsem"]
                dcnt[s] = dcnt.get(s, 0) + 16
                o["sig"] = ("d", s, dcnt[s])
            elif i in needed:
                cnt[o["eng"]] += 1
                o["sig"] = ("e", o["eng"], cnt[o["eng"]])
            else:
                o["sig"] = None
        es = self.es
        esem = {e: es.enter_context(nc.semaphore("sem_" + e)) for e in ENGS}
        dsem = [es.enter_context(nc.semaphore("dsem%d" % s)) for s in range(self.n_dma_sems)]

        def semof(k0, k1):
            return esem[k1] if k0 == "e" else dsem[k1]

        block = es.enter_context(nc.Block())
        by_eng = {e: [i for i, o in enumerate(ops) if o["eng"] == e] for e in ENGS}
        self.stats = {e: len(by_eng[e]) for e in ENGS}

        def run(e_obj, ename):
            seen = {}
            nwait = 0
            for i in by_eng[ename]:
                o = ops[i]
                waits = {}
                for j in deps[i]:
                    sig = ops[j]["sig"]
                    key = (sig[0], sig[1])
                    waits[key] = max(waits.get(key, 0), sig[2])
                for key, v in waits.items():
                    if seen.get(key, 0) >= v:
                        continue
                    seen[key] = v
                    e_obj.wait_ge(semof(*key), v)
                    nwait += 1
                ins = o["fn"](e_obj)
                sig = o["sig"]
                if sig is not None:
                    ins.then_inc(semof(sig[0], sig[1]), 16 if sig[0] == "d" else 1)
            if ename == "sp":
                waits = {}
                for j in fin:
                    sig = ops[j]["sig"]
                    key = (sig[0], sig[1])
                    waits[key] = max(waits.get(key, 0), sig[2])
                for key, v in waits.items():
                    if seen.get(key, 0) >= v:
                        continue
                    e_obj.wait_ge(semof(*key), v)
            self.stats["wait_" + ename] = nwait

        @block.tensor
        def _(e):
            run(e, "pe")

        @block.vector
        def _(e):
            run(e, "dve")

        @block.scalar
        def _(e):
            run(e, "act")

        @block.gpsimd
        def _(e):
            run(e, "pool")

        @block.sync
        def _(e):
            run(e, "sp")

        es.close()


def tk(name, a, n):
    return [(name, q) for q in range(a // 256, (a + n + 255) // 256)]


def _prod(s):
    r = 1
    for v in s:
        r *= v
    return r


class Carve:
    def __init__(self, arena, regions):
        self.arena = arena
        self.regions = [list(r) for r in regions]

    def take(self, shape, dt=F32):
        nel = _prod(shape)
        words = nel if dt == F32 else (nel + 1) // 2
        words = (words + 1) // 2 * 2
        for r in self.regions:
            if r[1] - r[0] >= words:
                off = r[0]
                r[0] += words
                break
        else:
            raise RuntimeError("arena region overflow: need %d words, free %s" % (words, self.regions))
        ap = self.arena[:, off:off + words]
        if dt != F32:
            ap = ap.bitcast(dt)
        ap = ap[:, 0:nel]
        if len(shape) == 2:
            ap = ap.rearrange("p (a b) -> p a b", a=shape[0])
        elif len(shape) == 3:
            ap = ap.rearrange("p (a b c) -> p a b c", a=shape[0], b=shape[1])
        elif len(shape) == 4:
            ap = ap.rearrange("p (a b c d) -> p a b c d", a=shape[0], b=shape[1], c=shape[2])
        return ap


def alibi_slopes():
    s = np.exp2(-8.0 * (np.arange(12, dtype=np.float32) + 1.0) / 12.0).astype(np.float32)
    return s.reshape(3, 4)


CF = {}
CA = {}
CB = {}


def build_consts():
    f_parts, b_parts, a_parts = [], [], []

    def adda(name, arr):
        arr = np.asarray(arr, np.float32).reshape(128, -1)
        CA[name] = (sum(a.shape[1] for a in a_parts), arr.shape[1])
        a_parts.append(arr)

    def addf(name, arr):
        arr = np.asarray(arr, np.float32).reshape(128, -1)
        CF[name] = (sum(a.shape[1] for a in f_parts), arr.shape[1])
        f_parts.append(arr)

    def addb(name, arr):
        arr = np.asarray(arr, np.float32).reshape(128, -1)
        CB[name] = (sum(a.shape[1] for a in b_parts), arr.shape[1])
        b_parts.append(arr)

    p = np.arange(128)
    ident = np.eye(128, dtype=np.float32)
    addf("ident", ident)
    addb("ident", ident)
    addf("ones64", np.ones((128, 64), np.float32))
    blk64 = (p[:, None] // 64 == p[None, :] // 64).astype(np.float32)
    addb("blk64", blk64)
    addf("blk64", blk64)
    tt = np.arange(256)
    addf("valid", np.broadcast_to(((tt % 64) < 4).astype(np.float32), (128, 256)))
    t5 = np.arange(512)
    addf("reset", np.broadcast_to(((t5 % 64) != 0).astype(np.float32), (128, 512)))
    s_ = p % 64
    t_ = np.arange(64)
    addf("tri_incl", (s_[:, None] <= t_[None, :]).astype(np.float32))
    addf("hm4", (p[:, None] // 32 == np.arange(4)[None, :]).astype(np.float32))
    c256 = np.arange(256)
    addf("vmask", (p[:, None] // 64 == (c256[None, :] // 64) % 2).astype(np.float32))
    addf("bdm", (p[:, None] // 32 == c256[None, :] // 64).astype(np.float32))
    c128 = np.arange(128)
    same = (p[:, None] // 64 == c128[None, :] // 64)
    addf("m_strict_bd", (same & (p[:, None] % 64 < c128[None, :] % 64)).astype(np.float32))
    addf("m_strictT_bd", (same & (p[:, None] % 64 > c128[None, :] % 64)).astype(np.float32))
    addf("hm2", (p[:, None] // 64 == np.arange(2)[None, :]).astype(np.float32))
    sl = alibi_slopes()
    k_ = p
    q_ = np.arange(256)
    for gi, (W, d) in enumerate(DIL):
        for h in range(4):
            di = q_[None, :] - k_[:, None]
            ok = (di >= 0) & (di <= 128)
            b = np.where(ok, -sl[gi, h] * d * di, NEG).astype(np.float32)
            adda("biasP%d_%d" % (gi, h), b)
    q1 = np.arange(128)
    for gi, (W, d) in enumerate(DIL):
        for h in range(4):
            sameslot = (k_[:, None] // 64) == (q1[None, :] // 64)
            pk = k_[:, None] % 64
            pq = q1[None, :] % 64
            real = (pk < 4) & (pq < 4)
            if gi == 0:
                ok = sameslot & real & (pk <= pq)
            else:
                ok = sameslot & real & (pk == pq)
            ok = ok | (k_[:, None] == q1[None, :])
            b = np.where(ok, -sl[gi, h] * (pq - pk), NEG).astype(np.float32)
            adda("biasS%d_%d" % (gi, h), b)
    pb0 = np.zeros((128, 16), np.float32)
    for i in range(4):
        for h in range(4):
            pb0[:, i * 4 + h] = np.where(p >= i, -sl[0, h] * (i + 128 - p), NEG)
    addf("pbias0", pb0)
    for gi in (1, 2):
        d = DIL[gi][1]
        addf("pbias%d" % gi, np.stack([-sl[gi, h] * d * (128 - p) for h in range(4)], axis=1))
    rows = [0, 1, 2, 3, 64, 65, 66, 67]
    sel = np.zeros((128, 8, 128), np.float32)
    for r, row in enumerate(rows):
        sel[row, r, :] = 1.0
    addb("sel8", sel)
    hs = np.zeros((128, 2, 4), np.float32)
    for pr in range(2):
        for h in range(4):
            hs[:, pr, h] = (p // 64 == h - 2 * pr)
    addf("hsel", hs)
    addf("tri128", (p[:, None] <= c128[None, :]).astype(np.float32))
    return np.concatenate(f_parts, axis=1), np.concatenate(b_parts, axis=1), np.concatenate(a_parts, axis=1)


_CF_ARR, _CB_ARR, _CA_ARR = build_consts()
NCA = _CA_ARR.shape[1]
NCF = _CF_ARR.shape[1]
NCB = _CB_ARR.shape[1]


def build_program():
    nc = bass.Bass("TRN2", target_bir_lowering=False)
    P = Prog(nc)

    def din(name, shape):
        return nc.dram_tensor(name, list(shape), F32, kind="ExternalInput").ap()

    def dout(name, shape):
        return nc.dram_tensor(name, list(shape), F32, kind="ExternalOutput").ap()

    xT_in = din("xT_in", [D, TT])
    vecs_in = din("vecs", [128, 2 * NVEC])
    cf_in = din("cf", [128, NCF])
    cb_in = din("cb", [128, NCB])
    ca_in = din("ca", [128, NCA])
    w_gate = din("w_ff_gate", [2, 2, D, DFF])
    w_up = din("w_ff_up", [2, 2, D, DFF])
    w_down = din("w_ff_down", [2, 2, DFF, D])
    w_in = din("w_in", [2, D, NCOL])
    w_out = din("w_out", [2, D, D])
    c128_in = din("c128", [2, NSEQ, 128, 512])
    c512_in = din("c512", [2, NSEQ, 512, 512])
    c2048_in = din("c2048", [2, NSEQ, 2048, 512])
    caches = (c128_in, c512_in, c2048_in)
    sgla_in = din("state_gla", [2, NSEQ, 4, 32, 64])
    srwkv_in = din("state_rwkv", [2, NSEQ, 256, 64])
    sshift_in = din("state_shift", [2, NSEQ, D])
    walpha_in = din("gla_w_alpha2", [2, 16, 128])
    lnrow_in = din("gmlp_ln_rows", [2, 2, 256])
    wsT_in = din("gmlp_wsT", [2, 4, 128, 128])
    bs_in = din("gmlp_bs", [2, 4, 128])
    w2_in = din("rwkv_w2", [2, 64, 256])
    a2_in = din("rwkv_a2", [2, 64, 256])
    g2_in = din("rwkv_g2", [2, 128, 256])

    yT_out = dout("yT", [D, TT])
    okv01 = dout("okv01", [2, 2, 2, 256, 768])
    okv2 = dout("okv2", [2, 2, 256, TT])
    o_gla = dout("o_gla", [2, 5, 128, 64])
    o_rwkv = dout("o_rwkv", [2, 5, 256, 64])
    o_shift = dout("o_shift", [2, 5, 128, 8])
    o_gv = dout("o_gv", [2, NSEQ, 4, 256])
    dbg = dout("dbg", [D, TT]) if STAGE < 4 else None
    finals = ["yT", "okv", "o_gla", "o_rwkv", "o_shift", "o_gv", "dbg"]
    xspill = nc.dram_tensor("xspill", [128, 8 * TT], F32).ap()

    vecs = P.sb("vecs_sb", [128, 2 * NVEC], F32)
    P.dma("sp", vecs[:], vecs_in, w=["vecs"])
    cf = P.sb("cf_sb", [128, NCF], F32)
    P.dma("sp", cf[:], cf_in, w=["cf"])
    cbt = P.sb("cb_sb", [128, NCB], BF16)
    P.dma("pool", cbt[:], cb_in, w=["cb"])
    ones_bf = P.sb("ones_bf", [128, 128], BF16)
    P.op("dve", lambda e: e.memset(ones_bf[:], 1.0), w=["ones_bf"])
    gh = P.sb("gh", [128, 2, 2, 8], F32)
    bar_t = P.sb("bar_t", [128, 2], F32)

    def CFv(name, shape=None):
        o, n = CF[name]
        ap = cf[:, o:o + n]
        if shape is not None and len(shape) == 2:
            ap = ap.rearrange("p (a b) -> p a b", a=shape[0])
        return ap

    def CBv(name, shape=None):
        o, n = CB[name]
        ap = cbt[:, o:o + n]
        if shape is not None and len(shape) == 2:
            ap = ap.rearrange("p (a b) -> p a b", a=shape[0])
        return ap

    ident_bf = CBv("ident")
    ident_f = CFv("ident")
    blk64_bf = CBv("blk64")
    blk64_f = CFv("blk64")
    ones64_f = CFv("ones64")

    def vcol(l, j, n=1):
        return vecs[:, l * NVEC + j: l * NVEC + j + n]

    def gain(l, k, c):
        return vcol(l, V_GAIN + k * 8 + c)

    for l in range(2):
        for f, k in ((0, 1), (1, 5)):
            P.op("dve", lambda e, l=l, f=f, k=k: e.tensor_scalar(
                gh[:, l, f, :], vcol(l, V_GAIN + k * 8, 8), 0.5, None, ALU.mult), r=["vecs"], w=["gh"])

    def barrier():
        P.barrier(lambda e: e.memset(bar_t[:], 0.0))

    arena = P.sb("arena", [128, AW], F32)
    XW = 8 * TT
    cvf = Carve(arena, [(0, AW)])
    xT = cvf.take([8, TT], F32)
    GW = 768
    ffo = cvf.take([8, GW], F32)
    xng = cvf.take([8, GW], BF16)
    hbuf = cvf.take([NFFC, GW], BF16)
    wg_sb = [cvf.take([8, 256], BF16) for i in range(2)]
    wu_sb = [cvf.take([8, 256], BF16) for i in range(2)]
    wd_sb = [cvf.take([NFFC, 256], BF16) for i in range(2)]
    sq = cvf.take([4, 512], BF16)
    sd = cvf.take([512], F32)
    rstd = cvf.take([512], F32)
    sg = [cvf.take([512], F32) for i in range(2)]
    tmpa = [cvf.take([512], F32) for i in range(2)]
    xn = arena[:, XW:XW + 4 * TT].bitcast(BF16).rearrange("p (c t) -> p c t", c=8)
    mixT = arena[:, XW + 4 * TT:XW + 8 * TT].bitcast(BF16).rearrange("p (c t) -> p c t", c=8)
    MIXER_REGIONS = [(0, XW), (XW + 8 * TT, AW)]

    for (a, n) in ((0, 768), (768, 768), (1536, 768)):
        for c in range(8):
            P.dma("sp", xT[:, c, a:a + n], xT_in[c * 128:(c + 1) * 128, a:a + n], w=tk("x", a, n))

    pb = [P.ps("pb%d" % i, [128, 512], F32) for i in range(8)]
    cnt = {}

    def rr(name, m=2):
        v = cnt.get(name, 0)
        cnt[name] = v + 1
        return v % m

    def norm_stats(src3, n, rkeys, scale_div, sq_t, sd_t, rstd_t, pbank, lhs=None, eps=EPS, nchunk=8, kp=""):
        lhs = ones_bf[:] if lhs is None else lhs
        hh = nchunk // 2
        for half in range(2):
            P.op("act", lambda e, half=half: e.activation(sq_t[:, 0:hh, 0:n], src3[:, half * hh:(half + 1) * hh, :], AF.Square),
                 r=rkeys, w=["sq" + kp])
            for c in range(hh):
                cc_ = half * hh + c
                P.op("pe", lambda e, c=c, cc_=cc_: e.matmul(pb[pbank][:, 0:n], lhs, sq_t[:, c, 0:n], start=(cc_ == 0), stop=(cc_ == nchunk - 1)),
                     r=["sq" + kp, "ones_bf", "cb"], w=["pb%d" % pbank])
        P.op("act", lambda e: e.activation(sd_t[:, 0:n], pb[pbank][:, 0:n], AF.Sqrt, bias=eps, scale=1.0 / scale_div),
             r=["pb%d" % pbank], w=["sd" + kp])
        P.op("dve", lambda e: e.reciprocal(rstd_t[:, 0:n], sd_t[:, 0:n]), r=["sd" + kp], w=["rstd" + kp])

    def ffn(l, f, k_pre, f_post, write_out=False):
        wg_d = w_gate[l, f].rearrange("(k p) c -> p k c", p=128)
        wu_d = w_up[l, f].rearrange("(k p) c -> p k c", p=128)
        wd_d = w_down[l, f].rearrange("(k p) c -> p k c", p=128)
        tiles = [(0, 512), (512, 256)]

        def prenorm(g):
            t0 = g * GW
            for j, (ra, n) in enumerate(tiles):
                a = t0 + ra
                norm_stats(xT[:, :, a:a + n], n, tk("x", a, n), float(D), sq, sd, rstd, 6)
                for c in range(8):
                    P.op("dve", lambda e, c=c, a=a, ra=ra, n=n: e.scalar_tensor_tensor(
                        xng[:, c, ra:ra + n], xT[:, c, a:a + n], gain(l, k_pre, c), rstd[:, 0:n], ALU.mult, ALU.mult),
                        r=tk("x", a, n) + ["rstd", "vecs"], w=[("xng", j)])

        def phase1(g):
            for fg in range(11):
                b = rr("w1")
                P.dma("pool", wg_sb[b], wg_d[:, :, fg * 256:(fg + 1) * 256], w=[("wg", b)])
                P.dma("pool", wu_sb[b], wu_d[:, :, fg * 256:(fg + 1) * 256], w=[("wu", b)])
                for jj in range(2):
                    ffc = fg * 2 + jj
                    for j, (ra, n) in enumerate(tiles):
                        q = rr("ffn1")
                        pg, pu = pb[q], pb[2 + q]
                        for k in range(8):
                            P.op("pe", lambda e, k=k, pg=pg, b=b, jj=jj, ra=ra, n=n: e.matmul(
                                pg[:, 0:n], wg_sb[b][:, k, jj * 128:(jj + 1) * 128], xng[:, k, ra:ra + n],
                                start=(k == 0), stop=(k == 7)), r=[("wg", b), ("xng", j)], w=["pb%d" % q])
                        for k in range(8):
                            P.op("pe", lambda e, k=k, pu=pu, b=b, jj=jj, ra=ra, n=n: e.matmul(
                                pu[:, 0:n], wu_sb[b][:, k, jj * 128:(jj + 1) * 128], xng[:, k, ra:ra + n],
                                start=(k == 0), stop=(k == 7)), r=[("wu", b), ("xng", j)], w=["pb%d" % (2 + q)])
                        P.op("act", lambda e, pg=pg, q=q, n=n: e.activation(sg[q][:, 0:n], pg[:, 0:n], AF.Silu),
                             r=["pb%d" % q], w=[("sg", q)])
                        P.op("dve", lambda e, pu=pu, q=q, ffc=ffc, ra=ra, n=n: e.tensor_tensor(
                            hbuf[:, ffc, ra:ra + n], sg[q][:, 0:n], pu[:, 0:n], ALU.mult),
                            r=[("sg", q), "pb%d" % (2 + q)], w=[("h", j, ffc)])

        def phase2(g):
            for og in range(4):
                b = rr("w2")
                P.dma("pool", wd_sb[b], wd_d[:, :, og * 256:(og + 1) * 256], w=[("wd", b)])
                for jj in range(2):
                    oc = og * 2 + jj
                    for j, (ra, n) in enumerate(tiles):
                        q = rr("ffn2")
                        po = pb[4 + q]
                        for ffc in range(NFFC):
                            P.op("pe", lambda e, ffc=ffc, po=po, b=b, jj=jj, ra=ra, n=n: e.matmul(
                                po[:, 0:n], wd_sb[b][:, ffc, jj * 128:(jj + 1) * 128], hbuf[:, ffc, ra:ra + n],
                                start=(ffc == 0), stop=(ffc == NFFC - 1)),
                                r=[("wd", b), ("h", j, ffc)], w=["pb%d" % (4 + q)])
                        P.op("act", lambda e, po=po, oc=oc, ra=ra, n=n: e.copy(ffo[:, oc, ra:ra + n], po[:, 0:n]),
                             r=["pb%d" % (4 + q)], w=[("ffo", j)])

        def postnorm(g):
            t0 = g * GW
            for j, (ra, n) in enumerate(tiles):
                a = t0 + ra
                norm_stats(ffo[:, :, ra:ra + n], n, [("ffo", j)], float(D), sq, sd, rstd, 7)
                for c in range(8):
                    q = rr("tmp")
                    P.op("dve", lambda e, c=c, q=q, ra=ra, n=n: e.scalar_tensor_tensor(
                        tmpa[q][:, 0:n], ffo[:, c, ra:ra + n], gh[:, l, f_post, c:c + 1], rstd[:, 0:n], ALU.mult, ALU.mult),
                        r=[("ffo", j), "rstd", "gh"], w=[("tmpa", q)])
                    P.op("dve", lambda e, c=c, q=q, a=a, n=n: e.tensor_tensor(
                        xT[:, c, a:a + n], xT[:, c, a:a + n], tmpa[q][:, 0:n], ALU.add),
                        r=[("tmpa", q)] + tk("x", a, n), w=tk("x", a, n))

        prenorm(0)
        for g in range(3):
            phase1(g)
            if g + 1 < 3:
                prenorm(g + 1)
            phase2(g)
            postnorm(g)
            if write_out:
                for c in range(8):
                    P.dma("sp", yT_out[c * 128:(c + 1) * 128, g * GW:(g + 1) * GW], xT[:, c, g * GW:(g + 1) * GW],
                          r=tk("x", g * GW, GW), w=["yT"])

    def proj(wt, col0, m, a, n, pbank, pkey, wkey):
        for k in range(8):
            P.op("pe", lambda e, k=k: e.matmul(pb[pbank][0:m, 0:n], wt[:, k, col0:col0 + m], xn[:, k, a:a + n],
                                               start=(k == 0), stop=(k == 7)),
                 r=[wkey] + tk("xn", a, n), w=[pkey])

    def attention(l):
        cv = Carve(arena, MIXER_REGIONS[0:1])
        cv2 = Carve(arena, MIXER_REGIONS[1:2])
        cfa = cv.take([NCA], F32)
        P.dma("sp", cfa, ca_in, w=["cfa"])
        wdil = cv.take([8, 768], BF16)
        qT = cv.take([2, TT], BF16)
        kT = cv.take([2, TT], BF16)
        vT = cv.take([2, TT], BF16)
        Vb = cv.take([18, 4, 66], BF16)
        numacc = cv2.take([2, TT], F32)
        denacc = cv2.take([2, TT], F32)
        stg = [cv2.take([512], F32) for i in range(4)]
        qtm = cv2.take([3, 2, 256], BF16)
        st = [cv.take([256], F32) for i in range(2)]
        pT = [cv.take([256], BF16) for i in range(2)]
        kvrows = stg[0:2]
        vrows_bf = [cv.take([256], BF16) for i in range(2)]
        prod = cv.take([256], F32)
        sc = cv.take([16], F32)
        sc2 = cv.take([16], F32)
        pP = cv.take([16], BF16)
        numS = cv2.take([2, 4, 16], F32)
        denS = cv2.take([4, 16], F32)
        tmpS = cv2.take([2, 64], F32)
        redS = cv2.take([2, 16], F32)

        def CAv(name):
            o, n = CA[name]
            return cfa[:, o:o + n]

        P.op("pool", lambda e: e.memset(Vb[:, :, :, 64:65], 1.0), w=["Vb1"])
        P.op("pool", lambda e: e.memset(denacc, 0.0), w=["denacc"])
        P.op("pool", lambda e: e.memset(numS, 0.0), w=["numS"])
        P.op("pool", lambda e: e.memset(denS, 0.0), w=["denS"])

        for gi, (W, d) in enumerate(DIL):
            if str(gi) not in os.environ.get("K_GROUPS", "012"):
                continue
            cb0 = C_DIL + gi * 768
            P.dma("pool", wdil, w_in[l].rearrange("(k p) c -> p k c", p=128)[:, :, cb0:cb0 + 768], w=["wdil"])
            for ti, (a, n) in enumerate(TILES):
                for cc in range(6):
                    bk = rr("apj")
                    proj(wdil, cc * 128, 128, a, n, bk, "pb%d" % bk, "wdil")
                    which, pair = cc // 2, cc % 2
                    dstT = (qT, kT, vT)[which]
                    if a < T and d > 1:
                        dst = dstT[:, pair, 0:T].rearrange("p (r m) -> p r m", r=d)[:, :, a // d:(a + n) // d]
                        srcv = lambda ap, d=d: ap.rearrange("p (m r) -> p r m", r=d)
                    else:
                        dst = dstT[:, pair, a:a + n]
                        srcv = lambda ap: ap
                    wk = [(("qT", "kT", "vT")[which], pair)]
                    if which == 0:
                        P.op("act", lambda e, bk=bk, dst=dst, srcv=srcv, n=n: e.activation(
                            dst, srcv(pb[bk][:, 0:n]), AF.Copy, scale=0.125), r=["pb%d" % bk], w=wk)
                    else:
                        kv = which - 1
                        need_out = (gi == 2) or (a >= 1536)
                        if need_out:
                            sb_ = rr("stg", 4)
                            P.op("dve", lambda e, bk=bk, sb_=sb_, n=n: e.tensor_copy(stg[sb_][:, 0:n], pb[bk][:, 0:n]),
                                 r=["pb%d" % bk], w=[("stg", sb_)])
                            P.op("act", lambda e, bk=bk, dst=dst, srcv=srcv, n=n: e.copy(dst, srcv(pb[bk][:, 0:n])), r=["pb%d" % bk], w=wk)
                            if gi == 2:
                                P.dma("sp", okv2[l, kv, pair * 128:(pair + 1) * 128, a:a + n], stg[sb_][:, 0:n],
                                      r=[("stg", sb_)], w=["okv"])
                            else:
                                P.dma("sp", okv01[l, gi, kv, pair * 128:(pair + 1) * 128, a - 1536:a - 1536 + n], stg[sb_][:, 0:n],
                                      r=[("stg", sb_)], w=["okv"])
                        else:
                            P.op("dve", lambda e, bk=bk, dst=dst, srcv=srcv, n=n: e.tensor_copy(dst, srcv(pb[bk][:, 0:n])), r=["pb%d" % bk], w=wk)
            for blk in range(2):
                bk = rr("apj")
                for k in range(8):
                    P.op("pe", lambda e, k=k, bk=bk, blk=blk: e.matmul(
                        pb[bk][:, 0:256], xn[:, k, T + blk * 128:T + (blk + 1) * 128], wdil[:, k, 0:256],
                        start=(k == 0), stop=(k == 7)), r=["wdil"] + tk("xn", T, 256), w=["pb%d" % bk])
                P.op("act", lambda e, bk=bk, blk=blk, gi=gi: e.activation(qtm[:, gi, blk, :], pb[bk][:, 0:256], AF.Copy, scale=0.125),
                     r=["pb%d" % bk], w=["qtm"])
            for blk in range(18):
                for pair in range(2):
                    bk = rr("apj")
                    pst = pb[bk][:, 0:64].bitcast(BF16)
                    P.op("pe", lambda e, pst=pst, pair=pair, blk=blk: e.transpose(pst, vT[:, pair, blk * 128:(blk + 1) * 128], ident_bf),
                         r=[("vT", pair), "cb"], w=["pb%d" % bk])
                    eng = "act" if (blk + pair) % 2 == 0 else "dve"
                    if eng == "act":
                        P.op("act", lambda e, pst=pst, pair=pair, blk=blk: e.copy(
                            Vb[:, blk, 2 * pair:2 * pair + 2, 0:64], pst.rearrange("p (h d) -> p h d", h=2)),
                            r=["pb%d" % bk], w=[("Vb", blk)])
                    else:
                        P.op("dve", lambda e, pst=pst, pair=pair, blk=blk: e.tensor_copy(
                            Vb[:, blk, 2 * pair:2 * pair + 2, 0:64], pst.rearrange("p (h d) -> p h d", h=2)),
                            r=["pb%d" % bk], w=[("Vb", blk)])
            cbk = 16 // d
            for h in range(4):
                p_, hin = h // 2, h % 2
                pr0 = 64 * hin
                dr = 64 if hin == 0 else 0
                biasP = CAv("biasP%d_%d" % (gi, h))
                biasS = CAv("biasS%d_%d" % (gi, h))
                def kb_info(kb):
                    if kb < 16:
                        return (256 if ((kb + 1) % cbk != 0) else 128), biasP
                    return 128, biasS

                def emit_S(kb):
                    N, bias = kb_info(kb)
                    par = kb % 2
                    P.op("pe", lambda e, par=par, kb=kb, N=N, p_=p_, pr0=pr0: e.matmul(
                        pb[5 + par][:, 0:N], kT[pr0:pr0 + 64, p_, kb * 128:(kb + 1) * 128],
                        qT[pr0:pr0 + 64, p_, kb * 128:kb * 128 + N], start=True, stop=True),
                        r=[("kT", p_), ("qT", p_)], w=["pb%d" % (5 + par)])

                def emit_rest(kb):
                    N, bias = kb_info(kb)
                    par = kb % 2
                    P.op("dve", lambda e, par=par, N=N, bias=bias: e.tensor_tensor(
                        st[par][:, 0:N], pb[5 + par][:, 0:N], bias[:, 0:N], ALU.add),
                        r=["pb%d" % (5 + par), "cfa"], w=[("st", par)])
                    P.op("act", lambda e, par=par, N=N: e.activation(pT[par][:, 0:N], st[par][:, 0:N], AF.Exp),
                         r=[("st", par)], w=[("pT", par)])
                    for half in range(N // 128):
                        qb = kb + half
                        tb, col0 = qb // 4, (qb % 4) * 128
                        if half == 1:
                            start, stop = True, False
                        else:
                            start = (qb >= 16) or (qb % cbk == 0)
                            stop = True
                        if hin == 0:
                            P.op("pe", lambda e, par=par, kb=kb, h=h, tb=tb, col0=col0, half=half, start=start, stop=stop: e.matmul(
                                pb[tb][0:65, col0:col0 + 128], Vb[:, kb, h, 0:65], pT[par][:, half * 128:(half + 1) * 128],
                                start=start, stop=stop), r=[("pT", par), ("Vb", kb), "Vb1"], w=["pb%d" % tb])
                            continue
                        P.op("pe", lambda e, par=par, kb=kb, h=h, tb=tb, col0=col0, half=half, start=start, stop=stop, pr0=pr0: e.matmul(
                            pb[tb][pr0:pr0 + 64, col0:col0 + 128], Vb[:, kb, h, 0:64], pT[par][:, half * 128:(half + 1) * 128],
                            start=start, stop=stop), r=[("pT", par), ("Vb", kb)], w=["pb%d" % tb])
                        P.op("pe", lambda e, par=par, kb=kb, h=h, tb=tb, col0=col0, half=half, start=start, stop=stop, dr=dr: e.matmul(
                            pb[tb][dr:dr + 1, col0:col0 + 128], Vb[:, kb, h, 64:65], pT[par][:, half * 128:(half + 1) * 128],
                            start=start, stop=stop), r=[("pT", par), ("Vb", kb), "Vb1"], w=["pb%d" % tb])

                def evac_bank(tb):
                    a, n = TILES[tb]
                    for (rows, acc, nm) in ((slice(pr0, pr0 + 64), numacc, "numacc"), (slice(dr, dr + 1), denacc, "denacc")):
                        if a < T and d > 1:
                            if d == 4:
                                dst = acc[rows, p_, 0:T].rearrange("p (m r) -> p r m", r=4)[:, tb, :]
                                src = pb[tb][rows, 0:512]
                            else:
                                dst = acc[rows, p_, 0:T].rearrange("p (m r) -> p r m", r=16)[:, 4 * tb:4 * tb + 4, :]
                                src = pb[tb][rows, 0:512].rearrange("p (r m) -> p r m", r=4)
                        else:
                            dst = acc[rows, p_, a:a + n]
                            src = pb[tb][rows, 0:n]
                        kk_ = (nm, p_, "P" if a < T else 4)
                        if str(gi) == os.environ.get("K_GROUPS", "012")[0] and nm == "numacc":
                            P.op("act", lambda e, dst=dst, src=src: e.copy(dst, src), r=["pb%d" % tb], w=[kk_])
                        else:
                            P.op("dve", lambda e, dst=dst, src=src: e.tensor_tensor(dst, dst, src, ALU.add),
                                 r=["pb%d" % tb, kk_, nm], w=[kk_])


                emit_S(0)
                for kb in range(18):
                    if kb + 1 < 18:
                        emit_S(kb + 1)
                    emit_rest(kb)
                    if kb in (3, 7, 11, 15, 17):
                        evac_bank(min(kb // 4, 4))
        hsel = CFv("hsel", [2, 4])
        for j in range(NSEQ):
            blk, rbase = j // 2, (j % 2) * 4
            c0 = T + j * SLOT
            for gi, (W, d) in enumerate(DIL):
                nsets = 1 if gi == 0 else 4
                for si in range(nsets):
                    b = rr("kvr")
                    if gi == 0:
                        src = caches[0][l, j, :, :]
                        qs = [0, 1, 2, 3]
                    else:
                        src = caches[gi][l, j, :, :].rearrange("(m r) c -> r m c", r=d)[si]
                        qs = [si]
                    nq = len(qs)
                    ncol = nq * 4
                    P.dma("sp", kvrows[b], src, w=[("stg", b)])
                    P.op("pool", lambda e, b=b: e.tensor_copy(vrows_bf[b], kvrows[b][:, 256:512]), r=[("stg", b)], w=[("vrb", b)])
                    for qi, i in enumerate(qs):
                        bk = 5 + rr("att")
                        P.op("pe", lambda e, bk=bk, i=i, gi=gi, rbase=rbase, blk=blk: e.matmul(
                            pb[bk][:, 0:256], CBv("sel8", [8, 128])[:, rbase + i, :], qtm[:, gi, blk, :], start=True, stop=True),
                            r=["qtm", "cb"], w=["pb%d" % bk])
                        P.op("dve", lambda e, bk=bk, b=b: e.tensor_tensor(prod, kvrows[b][:, 0:256], pb[bk][:, 0:256], ALU.mult),
                             r=["pb%d" % bk, ("stg", b)], w=["prod"])
                        P.op("dve", lambda e, qi=qi: e.tensor_reduce(
                            sc[:, qi * 4:qi * 4 + 4], prod.rearrange("p (h d) -> p h d", h=4), AX.X, ALU.add),
                            r=["prod"], w=["sc"])
                    pbias = CFv("pbias%d" % gi)
                    P.op("dve", lambda e, ncol=ncol, pbias=pbias: e.tensor_tensor(sc2[:, 0:ncol], sc[:, 0:ncol], pbias[:, 0:ncol], ALU.add),
                         r=["sc", "cf"], w=["sc2"])
                    P.op("act", lambda e, ncol=ncol: e.activation(pP[:, 0:ncol], sc2[:, 0:ncol], AF.Exp), r=["sc2"], w=["pP"])
                    c0s = qs[0] * 4
                    P.op("pe", lambda e, ncol=ncol: e.matmul(pb[7][:, 64:64 + ncol], ones_bf[:], pP[:, 0:ncol], start=True, stop=True),
                         r=["pP", "ones_bf"], w=["pb7"])
                    for p_ in range(2):
                        P.op("pe", lambda e, p_=p_, b=b, ncol=ncol: e.matmul(
                            pb[7][:, p_ * 16:p_ * 16 + ncol], vrows_bf[b][:, p_ * 128:(p_ + 1) * 128], pP[:, 0:ncol], start=True, stop=True),
                            r=["pP", ("vrb", b)], w=["pb7"])
                    P.op("dve", lambda e, j=j, c0s=c0s, ncol=ncol: e.tensor_tensor(
                        numS[:, :, j, c0s:c0s + ncol], numS[:, :, j, c0s:c0s + ncol],
                        pb[7][:, 0:32].rearrange("p (a c) -> p a c", a=2)[:, :, 0:ncol], ALU.add), r=["pb7", "numS"], w=["numS"])
                    P.op("dve", lambda e, j=j, c0s=c0s, ncol=ncol: e.tensor_tensor(
                        denS[:, j, c0s:c0s + ncol], denS[:, j, c0s:c0s + ncol], pb[7][:, 64:64 + ncol], ALU.add), r=["pb7", "denS"], w=["denS"])
        P.op("dve", lambda e: e.tensor_tensor(
            tmpS.rearrange("p a (x h) -> p a x h", h=4), numS.rearrange("p a j (q h) -> p a (j q) h", h=4),
            hsel.unsqueeze(2).broadcast_to([128, 2, 16, 4]), ALU.mult), r=["numS", "cf"], w=["tmpS"])
        P.op("dve", lambda e: e.tensor_reduce(redS, tmpS.rearrange("p a (x h) -> p a x h", h=4), AX.X, ALU.add), r=["tmpS"], w=["redS"])
        nsv = numacc[:, :, T:TT].rearrange("p a (j s) -> p a j s", s=64)[:, :, :, 0:4]
        P.op("dve", lambda e: e.tensor_tensor(nsv, nsv, redS.rearrange("p a (j q) -> p a j q", q=4), ALU.add),
             r=["redS", ("numacc", 0, 4), ("numacc", 1, 4)], w=[("numacc", 0, 4), ("numacc", 1, 4)])
        for p_ in range(2):
            for hin in range(2):
                dr = 64 if hin == 0 else 0
                h = 2 * p_ + hin
                dsv = denacc[dr:dr + 1, p_, T:TT].rearrange("p (j s) -> p j s", s=64)[:, :, 0:4]
                P.op("dve", lambda e, dsv=dsv, dr=dr, h=h: e.tensor_tensor(
                    dsv, dsv, denS[dr:dr + 1, :, :].rearrange("p j (q h) -> p j q h", h=4)[:, :, :, h], ALU.add),
                    r=["denS", ("denacc", p_, 4), "denacc"], w=[("denacc", p_, 4)])
        if STAGE == 2:
            for ci, (srcT, nm) in enumerate(((numacc, "numacc"), (denacc, "denacc"))):
                for p_ in range(2):
                    P.dma("sp", dbg[ci * 256 + p_ * 128: ci * 256 + (p_ + 1) * 128, :], srcT[:, p_, :],
                          r=[(nm, p_, "P"), (nm, p_, 4)], w=["dbg"])
            for ci, srcT in enumerate((qT, kT)):
                for (a, n) in TILES:
                    q = rr("dbgq")
                    P.op("dve", lambda e, q=q, srcT=srcT, a=a, n=n: e.tensor_copy(stg[2 + q][:, 0:n], srcT[:, 0, a:a + n]),
                         r=[("qT", 0), ("kT", 0)], w=[("stg", 2 + q)])
                    P.dma("sp", dbg[512 + ci * 128: 512 + (ci + 1) * 128, a:a + n], stg[2 + q][:, 0:n], r=[("stg", 2 + q)], w=["dbg"])
        for p_ in range(2):
            for hin in range(2):
                pr0 = 64 * hin
                dr = 64 if hin == 0 else 0
                P.op("dve", lambda e, p_=p_, dr=dr: e.reciprocal(denacc[dr:dr + 1, p_, :], denacc[dr:dr + 1, p_, :]),
                     r=[("denacc", p_, "P"), ("denacc", p_, 4), "denacc"], w=[("rden", p_, hin)])
                for tb, (a, n) in enumerate(TILES):
                    bk = rr("apj")
                    P.op("pe", lambda e, bk=bk, p_=p_, dr=dr, pr0=pr0, a=a, n=n: e.matmul(
                        pb[bk][pr0:pr0 + 64, 0:n], ones64_f[dr:dr + 1, 0:64], denacc[dr:dr + 1, p_, a:a + n], start=True, stop=True),
                        r=[("rden", p_, hin), "cf"], w=["pb%d" % bk])
                    P.op("dve", lambda e, bk=bk, p_=p_, pr0=pr0, a=a, n=n: e.tensor_tensor(
                        mixT[pr0:pr0 + 64, 6 + p_, a:a + n], numacc[pr0:pr0 + 64, p_, a:a + n], pb[bk][pr0:pr0 + 64, 0:n], ALU.mult),
                        r=["pb%d" % bk, ("numacc", p_, "P"), ("numacc", p_, 4)], w=tk("mix", a, n))

    def headstat(src, n, lhs, div, eps, sq_t, sd_t, rs_t, pbank, kp):
        P.op("act", lambda e: e.activation(sq_t[:, 0:n], src, AF.Square), r=[kp + "src"], w=[kp + "sq"])
        P.op("pe", lambda e: e.matmul(pb[pbank][:, 0:n], lhs, sq_t[:, 0:n], start=True, stop=True),
             r=[kp + "sq", "cb"], w=["pb%d" % pbank])
        P.op("act", lambda e: e.activation(sd_t[:, 0:n], pb[pbank][:, 0:n], AF.Sqrt, bias=eps, scale=1.0 / div),
             r=["pb%d" % pbank], w=[kp + "sd"])
        P.op("dve", lambda e: e.reciprocal(rs_t[:, 0:n], sd_t[:, 0:n]), r=[kp + "sd"], w=[kp + "rs"])

    def gla(l):
        cv = Carve(arena, MIXER_REGIONS)
        wgla = cv.take([8, 784], BF16)
        P.dma("pool", wgla, w_in[l].rearrange("(k p) c -> p k c", p=128)[:, :, 0:784], w=["wgla"])
        walpha = cv.take([128], BF16)
        P.dma("pool", walpha[0:16, :], walpha_in[l], w=["walpha"])
        nbal = cv.take([2], F32)
        P.op("dve", lambda e: e.tensor_scalar(nbal[:, 0:1], vcol(l, V_BALPHA), -1.0, None, ALU.mult), r=["vecs"], w=["nbal"])
        S_p = cv.take([256], F32)
        S_slot = [cv.take([256], F32) for j in range(NSEQ)]
        S_bf = cv.take([256], BF16)
        Sd = cv.take([64], F32)
        P.op("pool", lambda e: e.memset(S_p, 0.0), w=["S_p"])
        for j in range(NSEQ):
            P.op("pool", lambda e, j=j: e.memset(S_slot[j], 0.0), w=[("S_slot", j)])
            for h in range(4):
                P.dma("sp", S_slot[j][32 * h:32 * h + 32, 64 * h:64 * h + 64], sgla_in[l, j, h], r=[], w=[("S_slot", j)])
        qTt = cv.take([512], F32)
        kTt = cv.take([512], F32)
        alr = cv.take([512], BF16)
        e1 = cv.take([512], F32)
        l1 = cv.take([512], F32)
        la = cv.take([512], F32)
        bcum = cv.take([512], F32)
        eb = cv.take([512], F32)
        enb = cv.take([512], F32)
        ebl = cv.take([8], F32)
        qe = cv.take([512], BF16)
        ke = cv.take([512], BF16)
        kd = cv.take([512], BF16)
        ke_bd = cv.take([8, 4, 64], BF16)
        attm = [cv.take([2, 4, 64], BF16) for i in range(2)]
        Vbd = [cv.take([256], BF16) for i in range(4)]
        Vtm = [cv.take([256], BF16) for i in range(4)]
        kdtm = [cv.take([128], BF16) for i in range(4)]
        t1 = cv.take([256], F32)
        kdtm_all = cv.take([4, 128], BF16)
        t1x = cv.take([4, 256], F32)
        xdup = [cv.take([8, 2, 64], BF16) for i in range(4)]
        oT = cv.take([2, 512], F32)
        sqg = cv.take([512], BF16)
        sdg = cv.take([512], F32)
        rsg = [cv.take([512], F32) for i in range(2)]
        sgl = cv.take([2, 512], F32)
        t2 = cv.take([512], F32)
        tri = CFv("tri_incl")
        hm4 = CFv("hm4")
        vmask = CFv("vmask")
        bdm = CFv("bdm")
        valid = CFv("valid")
        reset = CFv("reset")
        wcols = lambda c0, m: (wgla, c0, m)

        cur = {"S": S_p, "key": "S_p"}
        P.op("act", lambda e: e.copy(S_bf, S_p), r=["S_p"], w=["S_bf"])
        for ti, (a, n) in enumerate(TILES):
            nch = n // 64
            proj(wgla, 0, 128, a, n, 0, "pb0", "wgla")
            P.op("act", lambda e, n=n: e.activation(qTt[:, 0:n], pb[0][:, 0:n], AF.Copy, scale=32 ** -0.5), r=["pb0"], w=["qTt"])
            proj(wgla, 128, 128, a, n, 1, "pb1", "wgla")
            P.op("dve", lambda e, n=n: e.tensor_copy(kTt[:, 0:n], pb[1][:, 0:n]), r=["pb1"], w=["kTt"])
            proj(wgla, 512, 16, a, n, 0, "pb0", "wgla")
            P.op("act", lambda e, n=n: e.copy(alr[0:16, 0:n], pb[0][0:16, 0:n]), r=["pb0"], w=["alr"])
            P.op("pe", lambda e, n=n: e.matmul(pb[2][:, 0:n], walpha[0:16, :], alr[0:16, 0:n], start=True, stop=True),
                 r=["alr", "walpha"], w=["pb2"])
            P.op("act", lambda e, n=n: e.activation(e1[:, 0:n], pb[2][:, 0:n], AF.Exp, bias=nbal[:, 0:1], scale=-1.0),
                 r=["pb2", "nbal"], w=["e1"])
            P.op("act", lambda e, n=n: e.activation(l1[:, 0:n], e1[:, 0:n], AF.Ln, bias=1.0), r=["e1"], w=["l1"])
            if a < T:
                P.op("dve", lambda e, n=n: e.tensor_scalar(la[:, 0:n], l1[:, 0:n], -1.0 / 16.0, None, ALU.mult), r=["l1"], w=["la"])
            else:
                P.op("dve", lambda e, n=n: e.scalar_tensor_tensor(la[:, 0:n], l1[:, 0:n], -1.0 / 16.0, valid[:, 0:n], ALU.mult, ALU.mult),
                     r=["l1", "cf"], w=["la"])
            P.op("dve", lambda e, n=n: e.tensor_tensor_scan(bcum[:, 0:n], reset[:, 0:n], la[:, 0:n], 0.0, ALU.mult, ALU.add),
                 r=["la", "cf"], w=["bcum"])
            P.op("act", lambda e, n=n: e.activation(eb[:, 0:n], bcum[:, 0:n], AF.Exp), r=["bcum"], w=["eb"])
            P.op("act", lambda e, n=n: e.activation(enb[:, 0:n], bcum[:, 0:n], AF.Exp, scale=-1.0), r=["bcum"], w=["enb"])
            P.op("act", lambda e, n=n, nch=nch: e.activation(ebl[:, 0:nch], bcum[:, 63:n:64], AF.Exp), r=["bcum"], w=["ebl"])
            CUT = int(os.environ.get("K_CUT", "99"))
            if CUT <= 1:
                return
            P.op("dve", lambda e, n=n: e.tensor_tensor(qe[:, 0:n], qTt[:, 0:n], eb[:, 0:n], ALU.mult), r=["qTt", "eb"], w=["qe"])
            P.op("pool", lambda e, n=n: e.tensor_tensor(ke[:, 0:n], kTt[:, 0:n], enb[:, 0:n], ALU.mult), r=["kTt", "enb"], w=["ke"])
            P.op("pool", lambda e, n=n, nch=nch: e.tensor_tensor(
                kd[:, 0:n].rearrange("p (c s) -> p c s", s=64), ke[:, 0:n].rearrange("p (c s) -> p c s", s=64),
                ebl[:, 0:nch].unsqueeze(2).broadcast_to([128, nch, 64]), ALU.mult), r=["ke", "ebl"], w=["kd"])
            P.op("dve", lambda e, n=n, nch=nch: e.tensor_tensor(
                ke_bd[:, 0:nch, :, :], ke[:, 0:n].rearrange("p (c s) -> p c s", s=64).unsqueeze(2).broadcast_to([128, nch, 4, 64]),
                hm4.unsqueeze(1).unsqueeze(3).broadcast_to([128, nch, 4, 64]), ALU.mult), r=["ke", "cf"], w=["ke_bd"])
            if CUT <= 2:
                return
            for cg in range(nch // 4):
                ab = rr("attm")
                for cc in range(4):
                    c = cg * 4 + cc
                    for pair in range(2):
                        P.op("pe", lambda e, c=c, cc=cc, pair=pair: e.matmul(
                            pb[4][:, (pair * 4 + cc) * 64:(pair * 4 + cc + 1) * 64],
                            ke_bd[:, c, 2 * pair:2 * pair + 2, :], qe[:, c * 64:(c + 1) * 64], start=True, stop=True),
                            r=["ke_bd", "qe"], w=["pb4"])
                P.op("dve", lambda e, ab=ab: e.tensor_tensor(
                    attm[ab].rearrange("p a b c -> p (a b) c"), pb[4][:, 0:512].rearrange("p (x s) -> p x s", s=64),
                    tri.unsqueeze(1).broadcast_to([128, 8, 64]), ALU.mult), r=["pb4", "cf"], w=[("attm", ab)])
                for cc in range(4):
                    c = cg * 4 + cc
                    vb = cc
                    vbank = 3 if cc % 2 == 0 else 2
                    P.op("dve", lambda e, c=c, a=a, vb=vb: e.tensor_copy(
                        xdup[vb], xn[:, :, a + c * 64:a + (c + 1) * 64].unsqueeze(2).broadcast_to([128, 8, 2, 64])),
                        r=tk("xn", a, n), w=[("xdup", vb)])
                    for k in range(8):
                        P.op("pe", lambda e, k=k, vb=vb, vbank=vbank: e.matmul(
                            pb[vbank][:, 0:256], xdup[vb][:, k, :, :], wgla[:, k, 256:512], start=(k == 0), stop=(k == 7)),
                            r=["wgla", ("xdup", vb)], w=["pb%d" % vbank])
                    P.op("act", lambda e, vb=vb, vbank=vbank: e.copy(Vtm[vb], pb[vbank][:, 0:256]), r=["pb%d" % vbank], w=[("Vtm", vb)])
                    P.op("pool", lambda e, vb=vb: e.tensor_tensor(Vbd[vb], Vtm[vb], vmask, ALU.mult), r=[("Vtm", vb), "cf"], w=[("Vbd", vb)])
                    kt = pb[7][0:64, cc * 64:(cc + 1) * 64].bitcast(BF16)
                    P.op("pe", lambda e, c=c, kt=kt: e.transpose(kt, kd[:, c * 64:(c + 1) * 64], ident_bf), r=["kd", "cb"], w=["pb7"])
                P.op("act", lambda e: e.copy(kdtm_all[0:64, :, :], pb[7][0:64, 0:256].bitcast(BF16).rearrange("p (c k) -> p c k", c=4)), r=["pb7"], w=["kdtm"])
                for cc in range(4):
                    ib = 4 if cc < 2 else 7
                    io = (cc % 2) * 256
                    P.op("pe", lambda e, cc=cc, ib=ib, io=io: e.matmul(pb[ib][:, io:io + 256], kdtm_all[0:64, cc, :], Vtm[cc][0:64, :], start=True, stop=True),
                         r=["kdtm", ("Vtm", cc)], w=["pb%d" % ib])
                for half_ in range(2):
                    ib = 4 if half_ == 0 else 7
                    P.op("dve", lambda e, ib=ib, half_=half_: e.tensor_tensor(
                        t1x[:, 2 * half_:2 * half_ + 2, :], pb[ib][:, 0:512].rearrange("p (c x) -> p c x", c=2),
                        bdm.unsqueeze(1).broadcast_to([128, 2, 256]), ALU.mult), r=["pb%d" % ib, "cf"], w=[("t1x", half_)])
                if CUT <= 3:
                    return
                for cc in range(4):
                    c = cg * 4 + cc
                    vb = cc
                    gc = a // 64 + c
                    if gc >= 32:
                        j = gc - 32
                        cur["S"], cur["key"] = S_slot[j], ("S_slot", j)
                        P.op("act", lambda e, S=cur["S"]: e.copy(S_bf, S), r=[cur["key"]], w=["S_bf"])
                    S, Sk = cur["S"], cur["key"]
                    for pair in range(2):
                        if os.environ.get("K_NOO"):
                            break
                        P.op("pe", lambda e, vb=vb, ab=ab, pair=pair, cc=cc, c=c: e.matmul(
                            pb[5 + pair][:, c * 64:(c + 1) * 64], Vbd[vb][:, pair * 128:(pair + 1) * 128], attm[ab][:, pair, cc, :],
                            start=True, stop=False), r=[("Vbd", vb), ("attm", ab)], w=["pb%d" % (5 + pair)])
                        P.op("pe", lambda e, pair=pair, c=c: e.matmul(
                            pb[5 + pair][:, c * 64:(c + 1) * 64], S_bf[:, pair * 128:(pair + 1) * 128], qe[:, c * 64:(c + 1) * 64],
                            start=False, stop=True), r=["S_bf", "qe"], w=["pb%d" % (5 + pair)])
                    if CUT <= 4:
                        continue
                    P.op("dve", lambda e, S=S, c=c, cc=cc: e.scalar_tensor_tensor(S, S, ebl[:, c:c + 1], t1x[:, cc, :], ALU.mult, ALU.add),
                         r=[("t1x", cc // 2), "ebl", Sk], w=[Sk])
                    P.op("act", lambda e, S=S: e.copy(S_bf, S), r=[Sk], w=["S_bf"])
                    if gc >= 31:
                        slot = gc - 31
                        P.op("dve", lambda e, S=S: e.tensor_reduce(Sd, S.rearrange("p (h v) -> p v h", h=4), AX.X, ALU.add), r=[Sk], w=["Sd"])
                        P.dma("sp", o_gla[l, slot], Sd, r=["Sd"], w=["o_gla"])
            for pair in range(2):
                P.op("act", lambda e, pair=pair, n=n: e.copy(oT[:, pair, 0:n], pb[5 + pair][:, 0:n]), r=["pb%d" % (5 + pair)], w=[("oT", pair)])
                P.op("act", lambda e, pair=pair, n=n: e.activation(sqg[:, 0:n], oT[:, pair, 0:n], AF.Square), r=[("oT", pair)], w=["sqg"])
                P.op("pe", lambda e, n=n: e.matmul(pb[2][:, 0:n], blk64_bf, sqg[:, 0:n], start=True, stop=True), r=["sqg", "cb"], w=["pb2"])
                P.op("act", lambda e, n=n: e.activation(sdg[:, 0:n], pb[2][:, 0:n], AF.Sqrt, bias=EPS, scale=1.0 / 64.0), r=["pb2"], w=["sdg"])
                P.op("dve", lambda e, pair=pair, n=n: e.reciprocal(rsg[pair][:, 0:n], sdg[:, 0:n]), r=["sdg"], w=[("rsg", pair)])
            for pair in range(2):
                bk = rr("apj")
                proj(wgla, 528 + pair * 128, 128, a, n, bk, "pb%d" % bk, "wgla")
                P.op("act", lambda e, bk=bk, pair=pair, n=n: e.activation(sgl[:, pair, 0:n], pb[bk][:, 0:n], AF.Silu), r=["pb%d" % bk], w=[("sgl", pair)])
                P.op("dve", lambda e, pair=pair, n=n: e.scalar_tensor_tensor(
                    t2[:, 0:n], oT[:, pair, 0:n], vcol(l, V_GLAN + pair), rsg[pair][:, 0:n], ALU.mult, ALU.mult),
                    r=[("oT", pair), ("rsg", pair), "vecs"], w=["t2"])
                P.op("pool", lambda e, pair=pair, a=a, n=n: e.tensor_tensor(mixT[:, pair, a:a + n], t2[:, 0:n], sgl[:, pair, 0:n], ALU.mult),
                     r=["t2", ("sgl", pair)], w=tk("mix", a, n))

    def gmlp(l):
        cv = Carve(arena, MIXER_REGIONS)
        wgm = cv.take([8, 512], BF16)
        P.dma("pool", wgm, w_in[l].rearrange("(k p) c -> p k c", p=128)[:, :, C_GMLP:C_GMLP + 512], w=["wgm"])
        wsT_f = cv.take([4, 128], F32)
        P.dma("sp", wsT_f, wsT_in[l].rearrange("g s t -> s g t"), w=["wsT_f"])
        wsS_f = cv.take([4, 128], F32)
        P.op("pool", lambda e: e.memset(wsS_f, 0.0), w=["wsS_f"])
        for g in range(4):
            P.dma("sp", wsS_f[0:4, g, 0:4], wsT_in[l, g, 0:4, 0:4], w=["wsS_f"])
            P.dma("sp", wsS_f[64:68, g, 64:68], wsT_in[l, g, 0:4, 0:4], w=["wsS_f"])
        wsTm = cv.take([4, 128], BF16)
        wsS = cv.take([4, 128], BF16)
        tri128 = CFv("tri128")
        P.op("dve", lambda e: e.tensor_tensor(wsTm, wsT_f, tri128.unsqueeze(1).broadcast_to([128, 4, 128]), ALU.mult), r=["wsT_f", "cf"], w=["wsTm"])
        P.op("dve", lambda e: e.tensor_tensor(wsS, wsS_f, tri128.unsqueeze(1).broadcast_to([128, 4, 128]), ALU.mult), r=["wsS_f", "cf"], w=["wsS"])
        bsT = cv.take([2, 128], F32)
        bsS = cv.take([2, 128], F32)
        for g in range(4):
            P.dma("sp", bsT[(g % 2) * 64:(g % 2) * 64 + 64, g // 2, :], bs_in[l, g:g + 1, :].partition_broadcast(64), w=["bsT"])
        P.op("pool", lambda e: e.tensor_copy(bsS, bsT), r=["bsT"], w=["bsS"])
        P.op("pool", lambda e: e.tensor_copy(bsS[:, :, 64:68], bsT[:, :, 0:4]), r=["bsT", "bsS"], w=["bsS"])
        lng = cv.take([256], F32)
        lnb = cv.take([256], F32)
        P.dma("sp", lng, lnrow_in[l, 0:1, :].partition_broadcast(128), w=["lng"])
        P.dma("sp", lnb, lnrow_in[l, 1:2, :].partition_broadcast(128), w=["lnb"])
        vg4 = [cv.take([256], F32) for i in range(4)]
        mvAll = cv.take([4, 2], F32)
        sdvAll = cv.take([4], F32)
        rsvAll = cv.take([4], F32)
        st6 = cv.take([6], F32)
        mv = cv.take([2], F32)
        sdv = cv.take([2], F32)
        rsv = cv.take([2], F32)
        vn0 = cv.take([256], F32)
        vn1 = cv.take([256], F32)
        vnf = [cv.take([256], F32) for i in range(2)]
        vnb = [cv.take([256], BF16) for i in range(2)]
        uT = cv.take([2, 512], F32)
        ts = cv.take([512], F32)
        for ti, (a, n) in enumerate(TILES):
            nb = n // 128
            bst = bsT if a < T else bsS
            for pair in range(2):
                proj(wgm, pair * 128, 128, a, n, 1, "pb1", "wgm")
                P.op("act", lambda e, pair=pair, n=n: e.activation(uT[:, pair, 0:n], pb[1][:, 0:n], AF.Gelu), r=["pb1"], w=[("uT", pair)])

            for bi in range(nb):
                blk = a // 128 + bi
                vbank = 0 if bi % 2 == 0 else 4
                for k in range(8):
                    P.op("pe", lambda e, k=k, blk=blk, vbank=vbank: e.matmul(pb[vbank][:, 0:256], xn[:, k, blk * 128:(blk + 1) * 128], wgm[:, k, 256:512],
                                                                            start=(k == 0), stop=(k == 7)), r=["wgm"] + tk("xn", a, n), w=["pb%d" % vbank])
                P.op("act", lambda e, bi=bi, vbank=vbank: e.activation(vg4[bi], pb[vbank][:, 0:256], AF.Gelu), r=["pb%d" % vbank], w=[("vg", bi)])
                P.op("dve", lambda e, bi=bi: e.bn_stats(st6, vg4[bi]), r=[("vg", bi)], w=["st6"])
                P.op("dve", lambda e, bi=bi: e.bn_aggr(mvAll[:, bi, :], st6), r=["st6"], w=["mvAll"])
            P.op("act", lambda e, nb=nb: e.activation(sdvAll[:, 0:nb], mvAll[:, 0:nb, 1], AF.Sqrt, bias=1e-5), r=["mvAll"], w=["sdvAll"])
            P.op("dve", lambda e, nb=nb: e.reciprocal(rsvAll[:, 0:nb], sdvAll[:, 0:nb]), r=["sdvAll"], w=["rsvAll"])
            for bi in range(nb):
                blk = a // 128 + bi
                b = bi % 2
                P.op("dve", lambda e, bi=bi: e.tensor_scalar(vn0, vg4[bi], mvAll[:, bi, 0:1], rsvAll[:, bi:bi + 1], ALU.subtract, ALU.mult),
                     r=[("vg", bi), "mvAll", "rsvAll"], w=["vn0"])
                P.op("pool", lambda e: e.tensor_tensor(vn1, vn0, lng, ALU.mult), r=["vn0", "lng"], w=["vn1"])
                P.op("pool", lambda e, b=b: e.tensor_tensor(vnf[b], vn1, lnb, ALU.add), r=["vn1", "lnb"], w=[("vnf", b)])
                P.op("act", lambda e, b=b: e.copy(vnb[b], vnf[b]), r=[("vnf", b)], w=[("vnb", b)])
                if blk >= 16:
                    for jj in range(2):
                        j = (blk - 16) * 2 + jj
                        P.dma("sp", o_gv[l, j], vnf[b][jj * 64:jj * 64 + 4, :], r=[("vnf", b)], w=["o_gv"])
                wmat = wsTm if blk < 16 else wsS
                for g in range(4):
                    P.op("pe", lambda e, g=g, b=b, bi=bi, wmat=wmat: e.matmul(
                        pb[2 + g // 2][(g % 2) * 64:(g % 2) * 64 + 64, bi * 128:(bi + 1) * 128], vnb[b][:, g * 64:(g + 1) * 64], wmat[:, g, :],
                        start=True, stop=True), r=[("vnb", b), "wsTm", "wsS"], w=["pb%d" % (2 + g // 2)])
            for pair in range(2):
                P.op("dve", lambda e, pair=pair, n=n, nb=nb, bst=bst: e.tensor_tensor(
                    ts[:, 0:n].rearrange("p (b t) -> p b t", t=128), pb[2 + pair][:, 0:n].rearrange("p (b t) -> p b t", t=128),
                    bst[:, pair, :].unsqueeze(1).broadcast_to([128, nb, 128]), ALU.add), r=["pb%d" % (2 + pair), "bsT", "bsS"], w=["ts"])
                P.op("pool", lambda e, pair=pair, a=a, n=n: e.tensor_tensor(mixT[:, 2 + pair, a:a + n], uT[:, pair, 0:n], ts[:, 0:n], ALU.mult),
                     r=["ts", ("uT", pair)], w=tk("mix", a, n))

    def rwkv(l):
        cv = Carve(arena, MIXER_REGIONS)
        NT_ = 256
        wrw = cv.take([8, 1024], BF16)
        P.dma("pool", wrw, w_in[l].rearrange("(k p) c -> p k c", p=128)[:, :, C_RWKV:C_RWKV + 1024], w=["wrw"])
        w2b = cv.take([256], BF16)
        a2b = cv.take([256], BF16)
        g2b = cv.take([256], BF16)
        P.dma("pool", w2b[0:64, :], w2_in[l], w=["w2b"])
        P.dma("pool", a2b[64:128, :], a2_in[l], w=["a2b"])
        P.dma("pool", g2b, g2_in[l], w=["g2b"])
        sst = cv.take([8, 4], F32)
        for j in range(NSEQ):
            P.dma("sp", sst[:, :, j], sshift_in[l, j].rearrange("(c p) -> p c", p=128), w=["sst"], allow_slow_non_contiguous=True)
        hm2 = CFv("hm2")
        m_sbd = CFv("m_strict_bd", [2, 64])
        m_sTbd = CFv("m_strictT_bd", [2, 64])
        tri = CFv("tri_incl")
        valid = CFv("valid")
        reset = CFv("reset")
        H_p = [cv.take([128], F32) for p_ in range(2)]
        H_slot = [[cv.take([128], F32) for p_ in range(2)] for j in range(NSEQ)]
        H_bf = [cv.take([128], BF16) for p_ in range(2)]
        snat = cv.take([64], F32)
        sbd = cv.take([2, 64], F32)
        Sx = [cv.take([64], F32) for p_ in range(2)]
        for p_ in range(2):
            P.op("pool", lambda e, p_=p_: e.memset(H_p[p_], 0.0), w=[("H_p", p_)])
            P.op("act", lambda e, p_=p_: e.copy(H_bf[p_], H_p[p_]), r=[("H_p", p_)], w=[("H_bf", p_)])
        for j in range(NSEQ):
            for p_ in range(2):
                P.dma("sp", snat, srwkv_in[l, j, p_ * 128:(p_ + 1) * 128, :], w=["snat"])
                P.op("dve", lambda e: e.tensor_tensor(sbd, snat.unsqueeze(1).broadcast_to([128, 2, 64]),
                                                      hm2.unsqueeze(2).broadcast_to([128, 2, 64]), ALU.mult), r=["snat", "cf"], w=["sbd"])
                P.op("pe", lambda e: e.transpose(pb[7][:, 0:128], sbd.rearrange("p a b -> p (a b)"), ident_f), r=["sbd", "cf"], w=["pb7"])
                P.op("act", lambda e, j=j, p_=p_: e.copy(H_slot[j][p_], pb[7][:, 0:128]), r=["pb7"], w=[("H_slot", j, p_)])
        cext = cv.take([8, NT_ + 1], F32)
        xm = cv.take([8, NT_], F32)
        tw = cv.take([NT_], BF16)
        albf = cv.take([NT_], BF16)
        sgg = cv.take([NT_], BF16)
        gate = [cv.take([NT_], F32) for p_ in range(2)]
        k2 = [cv.take([NT_], F32) for p_ in range(2)]
        AR = [cv.take([4, 2, 64], BF16) for p_ in range(2)]
        btb = [cv.take([NT_], BF16) for p_ in range(2)]
        kbd = [cv.take([4, 2, 64], BF16) for p_ in range(2)]
        bbd = [cv.take([4, 2, 64], BF16) for p_ in range(2)]
        abd = [cv.take([4, 2, 64], BF16) for p_ in range(2)]
        vbd = [cv.take([4, 2, 64], BF16) for p_ in range(2)]
        Bhbd = [cv.take([4, 2, 64], BF16) for p_ in range(2)]
        Khbd = [cv.take([4, 2, 64], BF16) for p_ in range(2)]
        gC = [cv.take([4], F32) for p_ in range(2)]
        tmpf = [{nm: cv.take([NT_], F32) for nm in ("sig", "ar", "kk0", "bv", "t", "gcum", "Ep", "Em", "Epr", "kt", "Bh", "Kh", "nrm")} for p_ in range(2)]
        sqk = [cv.take([NT_], BF16) for p_ in range(2)]
        NU = 8
        XTb = [cv.take([4, 128], F32) for b in range(NU // 4)]
        XT = [XTb[u // 4][:, u % 4, :] for u in range(NU)]
        Qbb = [[cv.take([4, 128], F32) for i in range(2)] for b in range(2)]
        QTbb = [[cv.take([4, 128], F32) for i in range(2)] for b in range(2)]
        Aak = [cv.take([2, 64], BF16) for u in range(NU)]
        Ar = [cv.take([2, 64], BF16) for u in range(NU)]
        TM = [cv.take([3, 128], BF16) for u in range(NU)]
        Vtm = [TM[u][:, 0, :] for u in range(NU)]
        Bhtm = [TM[u][:, 1, :] for u in range(NU)]
        Khtm = [TM[u][:, 2, :] for u in range(NU)]
        Wsb = [cv.take([128], F32) for p_ in range(2)]
        Ubf = [cv.take([128], BF16) for p_ in range(2)]
        yT = [cv.take([NT_], F32) for p_ in range(2)]
        C1 = -float(np.exp(-0.5))

        rtiles = [(a, NT_) for a in range(0, TT, NT_)]
        cur = {"H": H_p, "key": "H_p", "j": None}
        for ti, (a, n) in enumerate(rtiles):
            nch = n // 64
            smp = a >= T
            if ti == 0:
                P.op("pool", lambda e: e.memset(cext[:, :, 0:1], 0.0), w=["cext0"])
            else:
                P.op("pool", lambda e, n=n: e.tensor_copy(cext[:, :, 0:1], cext[:, :, n:n + 1]), r=["cext"], w=["cext0"])
            for cc in range(8):
                bk = rr("apj")
                proj(wrw, cc * 128, 128, a, n, bk, "pb%d" % bk, "wrw")
                if cc % 2 == 0:
                    P.op("act", lambda e, cc=cc, bk=bk, n=n: e.copy(cext[:, cc, 1:n + 1], pb[bk][:, 0:n]), r=["pb%d" % bk, "cext0"], w=["cext"])
                else:
                    P.op("dve", lambda e, cc=cc, bk=bk, n=n: e.tensor_copy(cext[:, cc, 1:n + 1], pb[bk][:, 0:n]), r=["pb%d" % bk, "cext0"], w=["cext"])
            if a + n == T:
                P.dma("sp", o_shift[l, 0], cext[:, :, n], r=["cext"], w=["o_shift"], allow_slow_non_contiguous=True)
            if smp:
                for j in range(NSEQ):
                    P.dma("sp", o_shift[l, 1 + j], cext[:, :, 1 + 64 * j + 3], r=["cext"], w=["o_shift"], allow_slow_non_contiguous=True)
            P.op("pool", lambda e, n=n: e.tensor_tensor(xm[:, 0:4, 0:n], cext[:, 0:4, 0:n], cext[:, 0:4, 1:n + 1], ALU.subtract),
                 r=["cext", "cext0"], w=["xm"])
            P.op("dve", lambda e, n=n: e.tensor_tensor(xm[:, 4:8, 0:n], cext[:, 4:8, 0:n], cext[:, 4:8, 1:n + 1], ALU.subtract),
                 r=["cext", "cext0"], w=["xm"])
            if smp:
                P.op("dve", lambda e, n=n: e.tensor_tensor(xm[:, :, 0:n:64], sst, cext[:, :, 1:n + 1:64], ALU.subtract),
                     r=["cext", "sst", "xm"], w=["xm"])
            for cc in range(8):
                P.op("dve", lambda e, cc=cc, n=n: e.scalar_tensor_tensor(
                    xm[:, cc, 0:n], xm[:, cc, 0:n], vcol(l, V_MU + cc), cext[:, cc, 1:n + 1], ALU.mult, ALU.add),
                    r=["xm", "cext", "vecs"], w=["xm"])
            if smp:
                P.op("pool", lambda e, n=n: e.tensor_tensor(xm[:, :, 0:n], xm[:, :, 0:n], valid[:, 0:n].unsqueeze(1).broadcast_to([128, 8, n]), ALU.mult),
                     r=["xm", "cf"], w=["xm"])
            RCUT = int(os.environ.get("K_RCUT", "99"))
            if RCUT <= 1:
                return
            P.op("act", lambda e, n=n: e.activation(tw[0:64, 0:n], xm[0:64, 6, 0:n], AF.Tanh), r=["xm"], w=["tw"])
            P.op("act", lambda e, n=n: e.copy(albf[64:128, 0:n], xm[64:128, 6, 0:n]), r=["xm"], w=["albf"])
            P.op("act", lambda e, n=n: e.activation(sgg[:, 0:n], xm[:, 7, 0:n], AF.Sigmoid), r=["xm"], w=["sgg"])
            def stageB(p_, n=n, nch=nch, smp=smp):
                PB2, PB3 = (2, 3) if p_ == 0 else (6, 7)
                tf = tmpf[p_]
                r_ = xm[:, p_, 0:n]
                k_ = xm[:, 2 + p_, 0:n]
                v_ = xm[:, 4 + p_, 0:n]
                P.op("pe", lambda e, p_=p_, n=n: e.matmul(pb[PB2][:, 0:n], w2b[0:64, p_ * 128:(p_ + 1) * 128], tw[0:64, 0:n], start=True, stop=True),
                     r=["tw", "w2b"], w=["pb%d" % PB2])
                P.op("act", lambda e, p_=p_, n=n: e.activation(tf["sig"][:, 0:n], pb[PB2][:, 0:n], AF.Sigmoid, bias=vcol(l, V_W0 + p_)),
                     r=["pb%d" % PB2, "vecs"], w=[("sig", p_)])
                if smp:
                    P.op("dve", lambda e, n=n: e.scalar_tensor_tensor(tf["sig"][:, 0:n], tf["sig"][:, 0:n], C1, valid[:, 0:n], ALU.mult, ALU.mult),
                         r=[("sig", p_), "cf"], w=[("sig", p_)])
                else:
                    P.op("dve", lambda e, n=n: e.tensor_scalar(tf["sig"][:, 0:n], tf["sig"][:, 0:n], C1, None, ALU.mult), r=[("sig", p_)], w=[("sig", p_)])
                P.op("pe", lambda e, p_=p_, n=n: e.matmul(pb[PB3][:, 0:n], a2b[64:128, p_ * 128:(p_ + 1) * 128], albf[64:128, 0:n], start=True, stop=True),
                     r=["albf", "a2b"], w=["pb%d" % PB3])
                P.op("act", lambda e, p_=p_, n=n: e.activation(tf["ar"][:, 0:n], pb[PB3][:, 0:n], AF.Sigmoid, bias=vcol(l, V_A0 + p_)),
                     r=["pb%d" % PB3, "vecs"], w=[("ar", p_)])
                P.op("pe", lambda e, p_=p_, n=n: e.matmul(pb[PB2][:, 0:n], g2b[:, p_ * 128:(p_ + 1) * 128], sgg[:, 0:n], start=True, stop=True),
                     r=["sgg", "g2b"], w=["pb%d" % PB2])
                P.op("act", lambda e, p_=p_, n=n: e.copy(gate[p_][:, 0:n], pb[PB2][:, 0:n]), r=["pb%d" % PB2], w=[("gate", p_)])
                yield
                P.op("dve", lambda e, p_=p_, n=n, k_=k_: e.tensor_scalar(tf["kk0"][:, 0:n], k_, vcol(l, V_KK + p_), None, ALU.mult),
                     r=["xm", "vecs"], w=[("kk0", p_)])
                P.op("act", lambda e, n=n: e.activation(sqk[p_][:, 0:n], tf["kk0"][:, 0:n], AF.Square), r=[("kk0", p_)], w=[("sqk", p_)])
                yield
                P.op("pe", lambda e, n=n: e.matmul(pb[PB3][:, 0:n], blk64_bf, sqk[p_][:, 0:n], start=True, stop=True), r=[("sqk", p_), "cb"], w=["pb%d" % PB3])
                yield
                P.op("act", lambda e, n=n: e.activation(tf["nrm"][:, 0:n], pb[PB3][:, 0:n], AF.Sqrt), r=["pb%d" % PB3], w=[("nrm", p_)])
                yield
                P.op("dve", lambda e, n=n: e.tensor_scalar(tf["nrm"][:, 0:n], tf["nrm"][:, 0:n], 1e-12, None, ALU.max), r=[("nrm", p_)], w=[("nrm", p_)])
                yield
                P.op("dve", lambda e, n=n: e.reciprocal(tf["t"][:, 0:n], tf["nrm"][:, 0:n]), r=[("nrm", p_)], w=[("t", p_)])
                yield
                P.op("dve", lambda e, n=n: e.tensor_tensor(tf["kk0"][:, 0:n], tf["kk0"][:, 0:n], tf["t"][:, 0:n], ALU.mult), r=[("kk0", p_), ("t", p_)], w=[("kk0", p_)])
                yield
                P.op("pool", lambda e, n=n: e.tensor_tensor(tf["bv"][:, 0:n], tf["kk0"][:, 0:n], tf["ar"][:, 0:n], ALU.mult), r=[("kk0", p_), ("ar", p_)], w=[("bv", p_)])
                yield
                P.op("dve", lambda e, p_=p_, n=n: e.tensor_scalar(tf["t"][:, 0:n], tf["ar"][:, 0:n], vcol(l, V_KA + p_), vcol(l, V_KA + p_), ALU.mult, ALU.subtract),
                     r=[("ar", p_), "vecs", ("kk0", p_)], w=[("t", p_)])
                P.op("dve", lambda e, p_=p_, n=n, k_=k_: e.scalar_tensor_tensor(k2[p_][:, 0:n], tf["t"][:, 0:n], 1.0, k_, ALU.add, ALU.mult),
                     r=[("t", p_), "xm"], w=[("k2", p_)])
                P.op("dve", lambda e, n=n: e.tensor_tensor_scan(tf["gcum"][:, 0:n], reset[:, 0:n], tf["sig"][:, 0:n], 0.0, ALU.mult, ALU.add),
                     r=[("sig", p_), "cf"], w=[("gcum", p_)])
                P.op("pool", lambda e, n=n: e.tensor_tensor(tf["nrm"][:, 0:n], tf["gcum"][:, 0:n], tf["sig"][:, 0:n], ALU.subtract), r=[("gcum", p_), ("sig", p_)], w=[("nrm", p_)])
                yield
                P.op("act", lambda e, n=n: e.activation(tf["Ep"][:, 0:n], tf["gcum"][:, 0:n], AF.Exp), r=[("gcum", p_)], w=[("Ep", p_)])
                yield
                P.op("act", lambda e, n=n: e.activation(tf["Em"][:, 0:n], tf["gcum"][:, 0:n], AF.Exp, scale=-1.0), r=[("gcum", p_)], w=[("Em", p_)])
                yield
                P.op("act", lambda e, n=n: e.activation(tf["Epr"][:, 0:n], tf["nrm"][:, 0:n], AF.Exp), r=[("nrm", p_)], w=[("Epr", p_)])
                yield
                P.op("act", lambda e, p_=p_, n=n, nch=nch: e.copy(gC[p_][:, 0:nch], tf["Ep"][:, 63:n:64]), r=[("Ep", p_)], w=[("gC", p_)])
                yield
                c3 = lambda ap: ap.rearrange("p (c s) -> p c s", s=64)
                P.op("dve", lambda e, p_=p_, n=n, nch=nch: e.scalar_tensor_tensor(
                    AR[p_][:, 0:nch, 0, :], c3(tf["kk0"][:, 0:n]), -1.0, c3(tf["Epr"][:, 0:n]), ALU.mult, ALU.mult),
                    r=[("kk0", p_), ("Epr", p_)], w=[("AR", p_)])
                P.op("pool", lambda e, p_=p_, n=n, nch=nch, r_=r_: e.tensor_tensor(AR[p_][:, 0:nch, 1, :], c3(r_), c3(tf["Ep"][:, 0:n]), ALU.mult),
                     r=["xm", ("Ep", p_)], w=[("AR", p_)])
                P.op("dve", lambda e, n=n: e.tensor_tensor(tf["bv"][:, 0:n], tf["bv"][:, 0:n], tf["Em"][:, 0:n], ALU.mult), r=[("bv", p_), ("Em", p_)], w=[("bv", p_)])
                yield
                P.op("pool", lambda e, p_=p_, n=n: e.tensor_tensor(tf["kt"][:, 0:n], k2[p_][:, 0:n], tf["Em"][:, 0:n], ALU.mult), r=[("k2", p_), ("Em", p_)], w=[("kt", p_)])
                yield
                P.op("act", lambda e, p_=p_, n=n: e.copy(btb[p_][:, 0:n], tf["bv"][:, 0:n]), r=[("bv", p_)], w=[("btb", p_)])
                yield
                gcb = lambda p_, nch: gC[p_][:, 0:nch].unsqueeze(2).broadcast_to([128, nch, 64])
                P.op("dve", lambda e, p_=p_, n=n, nch=nch: e.tensor_tensor(c3(tf["Bh"][:, 0:n]), c3(tf["bv"][:, 0:n]), gcb(p_, nch), ALU.mult),
                     r=[("bv", p_), ("gC", p_)], w=[("Bh", p_)])
                P.op("pool", lambda e, p_=p_, n=n, nch=nch: e.tensor_tensor(c3(tf["Kh"][:, 0:n]), c3(tf["kt"][:, 0:n]), gcb(p_, nch), ALU.mult),
                     r=[("kt", p_), ("gC", p_)], w=[("Kh", p_)])

                def expand(dst, src3, eng, rk_, wk_, nch=nch):
                    if eng == "act":
                        for hh_ in range(2):
                            P.op("act", lambda e, dst=dst, src3=src3, nch=nch, hh_=hh_: e.activation(
                                dst[:, 0:nch, hh_, :], src3, AF.Copy, scale=hm2[:, hh_:hh_ + 1]), r=rk_ + ["cf"], w=[wk_])
                        return
                    P.op(eng, lambda e, dst=dst, src3=src3, nch=nch: e.tensor_tensor(
                        dst[:, 0:nch, :, :], src3.unsqueeze(2).broadcast_to([128, nch, 2, 64]),
                        hm2.unsqueeze(1).unsqueeze(3).broadcast_to([128, nch, 2, 64]), ALU.mult), r=rk_ + ["cf"], w=[wk_])
                expand(kbd[p_], c3(tf["kt"][:, 0:n]), "act", [("kt", p_)], ("kbd", p_))
                yield
                expand(bbd[p_], c3(tf["bv"][:, 0:n]), "dve", [("bv", p_)], ("bbd", p_))
                yield
                expand(abd[p_], AR[p_][:, 0:nch, 0, :], "act", [("AR", p_)], ("abd", p_))
                yield
                expand(vbd[p_], c3(v_), "dve", ["xm"], ("vbd", p_))
                yield
                expand(Bhbd[p_], c3(tf["Bh"][:, 0:n]), "act", [("Bh", p_)], ("Bhbd", p_))
                yield
                expand(Khbd[p_], c3(tf["Kh"][:, 0:n]), "dve", [("Kh", p_)], ("Khbd", p_))
                yield
            gens = [stageB(0), stageB(1)]
            alive = [True, True]
            while any(alive):
                for gi_ in range(2):
                    if alive[gi_]:
                        try:
                            next(gens[gi_])
                        except StopIteration:
                            alive[gi_] = False
            if RCUT <= 2:
                return
            units = [(c, p_) for c in range(nch) for p_ in range(2)]
            nbat = (len(units) + 3) // 4
            QNB, QTNB, XB = (6, 2), (7, 3), (0, 1)
            for u, (c, p_) in enumerate(units):
                bi_, ui = u // 4, u % 4
                pa = pb[4 + u % 2]
                pak = "pb%d" % (4 + u % 2)
                arc = AR[p_][:, c, :, :].rearrange("p a b -> p (a b)")
                P.op("pe", lambda e, pa=pa, c=c, p_=p_, arc=arc: e.matmul(pa[:, 0:128], kbd[p_][:, c, :, :].rearrange("p a b -> p (a b)"), arc, start=True, stop=True),
                     r=[("kbd", p_), ("AR", p_)], w=[pak])
                P.op("pe", lambda e, pa=pa, c=c, p_=p_, arc=arc: e.matmul(pa[:, 128:256], bbd[p_][:, c, :, :].rearrange("p a b -> p (a b)"), arc, start=True, stop=True),
                     r=[("bbd", p_), ("AR", p_)], w=[pak])
                P.op("pe", lambda e, pa=pa, c=c, p_=p_: e.matmul(pa[:, 256:320], abd[p_][:, c, :, :].rearrange("p a b -> p (a b)"), btb[p_][:, c * 64:(c + 1) * 64], start=True, stop=True),
                     r=[("abd", p_), ("btb", p_)], w=[pak])
                P.op("dve", lambda e, pa=pa, u=u: e.tensor_tensor(Aak[u], pa[:, 0:64].unsqueeze(1).broadcast_to([128, 2, 64]), m_sbd, ALU.mult),
                     r=[pak, "cf"], w=[("Aak", u)])
                P.op("dve", lambda e, pa=pa, u=u: e.tensor_tensor(Ar[u], pa[:, 64:256].rearrange("p (a b) -> p a b", b=64)[:, 0:3:2, :], tri.unsqueeze(1).broadcast_to([128, 2, 64]), ALU.mult),
                     r=[pak, "cf"], w=[("Ar", u)])
                P.op("dve", lambda e, pa=pa, bi_=bi_, ui=ui: e.tensor_tensor(QTbb[bi_][0][:, ui, :].rearrange("p (a b) -> p a b", a=2), pa[:, 128:192].unsqueeze(1).broadcast_to([128, 2, 64]), m_sbd, ALU.mult),
                     r=[pak, "cf"], w=[("QT", bi_, 0)])
                P.op("dve", lambda e, pa=pa, bi_=bi_, ui=ui: e.tensor_tensor(Qbb[bi_][0][:, ui, :].rearrange("p (a b) -> p a b", a=2), pa[:, 256:320].unsqueeze(1).broadcast_to([128, 2, 64]), m_sTbd, ALU.mult),
                     r=[pak, "cf"], w=[("Q", bi_, 0)])
                P.op("pool", lambda e, u=u, bi_=bi_, ui=ui: e.tensor_tensor(XT[u], QTbb[bi_][0][:, ui, :], ident_f, ALU.add), r=[("QT", bi_, 0), "cf"], w=[("XTb", bi_)])
            def inv_sq(jstep):
                src = (jstep - 1) % 2
                for bi_ in range(nbat):
                    nb_ = min(4, len(units) - 4 * bi_)
                    qn, qtn = QNB[bi_], QTNB[bi_]
                    for ui in range(nb_):
                        P.op("pe", lambda e, ui=ui, src=src, bi_=bi_, qn=qn: e.matmul(pb[qn][:, ui * 128:(ui + 1) * 128], QTbb[bi_][src][:, ui, :], Qbb[bi_][src][:, ui, :], start=True, stop=True),
                             r=[("QT", bi_, src), ("Q", bi_, src)], w=["pb%d" % qn])
                    if jstep < 5:
                        for ui in range(nb_):
                            P.op("pe", lambda e, ui=ui, src=src, bi_=bi_, qtn=qtn: e.matmul(pb[qtn][:, ui * 128:(ui + 1) * 128], Qbb[bi_][src][:, ui, :], QTbb[bi_][src][:, ui, :], start=True, stop=True),
                                 r=[("QT", bi_, src), ("Q", bi_, src)], w=["pb%d" % qtn])

            def inv_ev(jstep):
                dst = jstep % 2
                for bi_ in range(nbat):
                    nb_ = min(4, len(units) - 4 * bi_)
                    qn, qtn = QNB[bi_], QTNB[bi_]
                    P.op("act", lambda e, dst=dst, nb_=nb_, bi_=bi_, qn=qn: e.copy(Qbb[bi_][dst][:, 0:nb_, :], pb[qn][:, 0:nb_ * 128].rearrange("p (u c) -> p u c", c=128)),
                         r=["pb%d" % qn], w=[("Q", bi_, dst)])
                    if jstep < 5:
                        P.op("dve", lambda e, dst=dst, nb_=nb_, bi_=bi_, qtn=qtn: e.tensor_copy(QTbb[bi_][dst][:, 0:nb_, :], pb[qtn][:, 0:nb_ * 128].rearrange("p (u c) -> p u c", c=128)),
                             r=["pb%d" % qtn], w=[("QT", bi_, dst)])

            def inv_x(jstep):
                dst = jstep % 2
                for bi_ in range(nbat):
                    nb_ = min(4, len(units) - 4 * bi_)
                    xb = XB[bi_]
                    for ui in range(nb_):
                        u = 4 * bi_ + ui
                        P.op("pe", lambda e, ui=ui, u=u, dst=dst, bi_=bi_, xb=xb: e.matmul(pb[xb][:, ui * 128:(ui + 1) * 128], Qbb[bi_][dst][:, ui, :], XT[u], start=True, stop=True),
                             r=[("Q", bi_, dst), ("XTb", bi_)], w=["pb%d" % xb])

            def inv_add(jstep):
                for bi_ in range(nbat):
                    nb_ = min(4, len(units) - 4 * bi_)
                    xb = XB[bi_]
                    P.op("dve", lambda e, bi_=bi_, nb_=nb_, xb=xb: e.tensor_tensor(XTb[bi_][:, 0:nb_, :], XTb[bi_][:, 0:nb_, :],
                                                                               pb[xb][:, 0:nb_ * 128].rearrange("p (u c) -> p u c", c=128), ALU.add),
                         r=["pb%d" % xb, ("XTb", bi_)], w=[("XTb", bi_)])

            inv_sq(1)
            inv_ev(1)
            for jstep in range(1, 6):
                if jstep < 5:
                    inv_sq(jstep + 1)
                inv_x(jstep)
                if jstep < 5:
                    inv_ev(jstep + 1)
                inv_add(jstep)
            if RCUT <= 3:
                return
            for u, (c, p_) in enumerate(units):
                bk = 6 + u % 2
                for which, srcb in enumerate((vbd, Bhbd, Khbd)):
                    pst = pb[bk][:, which * 64:(which + 1) * 64].bitcast(BF16)
                    P.op("pe", lambda e, pst=pst, srcb=srcb, c=c, p_=p_: e.transpose(pst, srcb[p_][:, c, :, :].rearrange("p a b -> p (a b)"), ident_bf),
                         r=[(("vbd", "Bhbd", "Khbd")[which], p_), "cb"], w=["pb%d" % bk])
                psl = pb[bk][:, 0:192].bitcast(BF16).rearrange("p (w c) -> p w c", w=3)
                if u % 2 == 0:
                    P.op("act", lambda e, psl=psl, u=u: e.copy(TM[u], psl), r=["pb%d" % bk], w=[("TM", u)])
                else:
                    P.op("dve", lambda e, psl=psl, u=u: e.tensor_copy(TM[u], psl), r=["pb%d" % bk], w=[("TM", u)])
            if RCUT <= 4:
                return
            for c in range(nch):
                gc = a // 64 + c
                if gc >= 32:
                    j = gc - 32
                    cur["H"], cur["key"], cur["j"] = H_slot[j], "H_slot", j
                    for p_ in range(2):
                        P.op("act", lambda e, p_=p_, j=j: e.copy(H_bf[p_], H_slot[j][p_]), r=[("H_slot", j, p_)], w=[("H_bf", p_)])
                WB, UB, HB = (1, 0), (2, 6), (3, 7)
                us = [c * 2 + p_ for p_ in range(2)]
                Hfs = [cur["H"][p_] for p_ in range(2)]
                Hks = [("H_p", p_) if cur["j"] is None else ("H_slot", cur["j"], p_) for p_ in range(2)]
                for p_ in range(2):
                    u = us[p_]
                    P.op("pe", lambda e, c=c, p_=p_: e.matmul(pb[WB[p_]][:, 0:128], abd[p_][:, c, :, :].rearrange("p a b -> p (a b)"), H_bf[p_], start=True, stop=False),
                         r=[("abd", p_), ("H_bf", p_)], w=["pb%d" % WB[p_]])
                    P.op("pe", lambda e, u=u, p_=p_: e.matmul(pb[WB[p_]][:, 0:128], Aak[u].rearrange("p a b -> p (a b)"), Vtm[u], start=False, stop=True),
                         r=[("Aak", u), ("TM", u)], w=["pb%d" % WB[p_]])
                for p_ in range(2):
                    if p_ == 0:
                        P.op("act", lambda e, p_=p_: e.copy(Wsb[p_], pb[WB[p_]][:, 0:128]), r=["pb%d" % WB[p_]], w=[("Wsb", p_)])
                    else:
                        P.op("dve", lambda e, p_=p_: e.tensor_copy(Wsb[p_], pb[WB[p_]][:, 0:128]), r=["pb%d" % WB[p_]], w=[("Wsb", p_)])
                for p_ in range(2):
                    u = us[p_]
                    P.op("pe", lambda e, u=u, p_=p_: e.matmul(pb[UB[p_]][:, 0:128], XT[u], Wsb[p_], start=True, stop=True),
                         r=[("XTb", u // 4), ("Wsb", p_)], w=["pb%d" % UB[p_]])
                for p_ in range(2):
                    if p_ == 0:
                        P.op("dve", lambda e, p_=p_: e.tensor_copy(Ubf[p_], pb[UB[p_]][:, 0:128]), r=["pb%d" % UB[p_]], w=[("Ubf", p_)])
                    else:
                        P.op("act", lambda e, p_=p_: e.copy(Ubf[p_], pb[UB[p_]][:, 0:128]), r=["pb%d" % UB[p_]], w=[("Ubf", p_)])
                for p_ in range(2):
                    u = us[p_]
                    yps = pb[4 + p_][:, c * 64:(c + 1) * 64]
                    P.op("pe", lambda e, yps=yps, c=c, p_=p_: e.matmul(yps, H_bf[p_], AR[p_][:, c, 1, :], start=True, stop=False),
                         r=[("H_bf", p_), ("AR", p_)], w=["pb%d" % (4 + p_)])
                    P.op("pe", lambda e, yps=yps, u=u, p_=p_: e.matmul(yps, Ubf[p_], Ar[u][:, 1, :], start=False, stop=False),
                         r=[("Ubf", p_), ("Ar", u)], w=["pb%d" % (4 + p_)])
                    P.op("pe", lambda e, yps=yps, u=u: e.matmul(yps, Vtm[u], Ar[u][:, 0, :], start=False, stop=True),
                         r=[("TM", u), ("Ar", u)], w=["pb%d" % (4 + p_)])
                    P.op("pe", lambda e, u=u, p_=p_: e.matmul(pb[HB[p_]][:, 0:128], Bhtm[u], Ubf[p_], start=True, stop=False),
                         r=[("TM", u), ("Ubf", p_)], w=["pb%d" % HB[p_]])
                    P.op("pe", lambda e, u=u, p_=p_: e.matmul(pb[HB[p_]][:, 0:128], Khtm[u], Vtm[u], start=False, stop=True),
                         r=[("TM", u)], w=["pb%d" % HB[p_]])
                for p_ in range(2):
                    Hf, Hk = Hfs[p_], Hks[p_]
                    P.op("dve", lambda e, Hf=Hf, p_=p_, c=c: e.scalar_tensor_tensor(Hf, Hf, gC[p_][:, c:c + 1], pb[HB[p_]][:, 0:128], ALU.mult, ALU.add),
                         r=["pb%d" % HB[p_], ("gC", p_), Hk], w=[Hk])
                    P.op("act", lambda e, Hf=Hf, p_=p_: e.copy(H_bf[p_], Hf), r=[Hk], w=[("H_bf", p_)])
                if gc >= 31:
                    slot = gc - 31
                    for p_ in range(2):
                        Hf, Hk = Hfs[p_], Hks[p_]
                        P.op("pe", lambda e, Hf=Hf, p_=p_: e.transpose(pb[UB[p_]][:, 0:128], Hf, ident_f), r=[Hk, "cf"], w=["pb%d" % UB[p_]])
                        P.op("dve", lambda e, p_=p_: e.tensor_reduce(Sx[p_], pb[UB[p_]][:, 0:128].rearrange("p (h k) -> p k h", h=2), AX.X, ALU.add),
                             r=["pb%d" % UB[p_]], w=[("Sx", p_)])
                        P.dma("sp", o_rwkv[l, slot, p_ * 128:(p_ + 1) * 128, :], Sx[p_], r=[("Sx", p_)], w=["o_rwkv"])
            if RCUT <= 5:
                return
            def stageD(p_, a=a, n=n):
                PB2, PB3 = (2, 3) if p_ == 0 else (6, 7)
                tf = tmpf[p_]
                P.op("act", lambda e, p_=p_, n=n, tf=tf: e.copy(yT[p_][:, 0:n], pb[4 + p_][:, 0:n]), r=["pb%d" % (4 + p_)], w=[("yT", p_)])
                yield
                P.op("pe", lambda e, p_=p_, n=n, tf=tf: e.matmul(pb[PB2][:, 0:n], blk64_f, yT[p_][:, 0:n], start=True, stop=True), r=[("yT", p_), "cf"], w=["pb%d" % PB2])
                yield
                P.op("dve", lambda e, p_=p_, n=n, tf=tf: e.scalar_tensor_tensor(tf["Ep"][:, 0:n], pb[PB2][:, 0:n], -1.0 / 64.0, yT[p_][:, 0:n], ALU.mult, ALU.add),
                     r=["pb%d" % PB2, ("yT", p_)], w=[("Ep", p_)])
                yield
                P.op("act", lambda e, n=n, p_=p_, tf=tf: e.activation(sqk[p_][:, 0:n], tf["Ep"][:, 0:n], AF.Square), r=[("Ep", p_)], w=[("sqk", p_)])
                yield
                P.op("pe", lambda e, n=n, p_=p_, tf=tf: e.matmul(pb[PB3][:, 0:n], blk64_bf, sqk[p_][:, 0:n], start=True, stop=True), r=[("sqk", p_), "cb"], w=["pb%d" % PB3])
                yield
                P.op("act", lambda e, n=n, p_=p_, tf=tf: e.activation(tf["nrm"][:, 0:n], pb[PB3][:, 0:n], AF.Sqrt, bias=64e-5, scale=1.0 / 64.0), r=["pb%d" % PB3], w=[("nrm", p_)])
                yield
                P.op("dve", lambda e, n=n, p_=p_, tf=tf: e.reciprocal(tf["t"][:, 0:n], tf["nrm"][:, 0:n]), r=[("nrm", p_)], w=[("t", p_)])
                yield
                P.op("dve", lambda e, p_=p_, n=n, tf=tf: e.scalar_tensor_tensor(tf["Em"][:, 0:n], tf["Ep"][:, 0:n], vcol(l, V_LXG + p_), tf["t"][:, 0:n], ALU.mult, ALU.mult),
                     r=[("Ep", p_), ("t", p_), "vecs"], w=[("Em", p_)])
                yield
                P.op("dve", lambda e, p_=p_, n=n, tf=tf: e.scalar_tensor_tensor(tf["Epr"][:, 0:n], xm[:, p_, 0:n], vcol(l, V_RK + p_), k2[p_][:, 0:n], ALU.mult, ALU.mult),
                     r=["xm", ("k2", p_), "vecs"], w=[("Epr", p_)])
                yield
                P.op("pe", lambda e, n=n, p_=p_, tf=tf: e.matmul(pb[PB2][:, 0:n], blk64_f, tf["Epr"][:, 0:n], start=True, stop=True), r=[("Epr", p_), "cf"], w=["pb%d" % PB2])
                yield
                P.op("dve", lambda e, p_=p_, n=n, tf=tf: e.tensor_tensor(tf["kt"][:, 0:n], pb[PB2][:, 0:n], xm[:, 4 + p_, 0:n], ALU.mult), r=["pb%d" % PB2, "xm"], w=[("kt", p_)])
                yield
                P.op("dve", lambda e, p_=p_, n=n, tf=tf: e.scalar_tensor_tensor(tf["Em"][:, 0:n], tf["Em"][:, 0:n], vcol(l, V_LXB + p_), tf["kt"][:, 0:n], ALU.add, ALU.add),
                     r=[("Em", p_), ("kt", p_), "vecs"], w=[("Em", p_)])
                yield
                P.op("pool", lambda e, p_=p_, a=a, n=n, tf=tf: e.tensor_tensor(mixT[:, 4 + p_, a:a + n], tf["Em"][:, 0:n], gate[p_][:, 0:n], ALU.mult),
                     r=[("Em", p_), ("gate", p_)], w=tk("mix", a, n))
                yield


            gensD = [stageD(0), stageD(1)]
            aliveD = [True, True]
            while any(aliveD):
                for gi_ in range(2):
                    if aliveD[gi_]:
                        try:
                            next(gensD[gi_])
                        except StopIteration:
                            aliveD[gi_] = False

    def mixer_phase(l):
        for ti, (a, n) in enumerate(TILES):
            norm_stats(xT[:, :, a:a + n], n, tk("x", a, n), float(D), sq, sd, rstd, 6)
            for c in range(8):
                P.op("dve", lambda e, c=c, a=a, n=n: e.scalar_tensor_tensor(
                    xn[:, c, a:a + n], xT[:, c, a:a + n], gain(l, 2, c), rstd[:, 0:n], ALU.mult, ALU.mult),
                    r=tk("x", a, n) + ["rstd", "vecs"], w=tk("xn", a, n))
        for c in range(8):
            P.dma("sp", xspill[:, c * TT:(c + 1) * TT], xT[:, c, :], r=tk("x", 0, TT), w=["xspill"])
        barrier()
        if STAGE != 3:
            attention(l)
        if STAGE <= 2:
            return
        barrier()
        if "g" in os.environ.get("K_MIX", "gmr"):
            gla(l)
            barrier()
        if "m" in os.environ.get("K_MIX", "gmr"):
            gmlp(l)
            barrier()
        if "r" in os.environ.get("K_MIX", "gmr"):
            rwkv(l)
        if STAGE <= 3:
            return
        barrier()
        for j in range(NSEQ):
            P.op("pool", lambda e, j=j: e.memset(mixT[:, :, T + 64 * j + 4:T + 64 * (j + 1)], 0.0), r=tk("mix", T, 256), w=tk("mix", T, 256))
        for c in range(8):
            P.dma("sp", xT[:, c, :], xspill[:, c * TT:(c + 1) * TT], r=["xspill"], w=tk("x", 0, TT))
        wout = arena[:, XW + 8 * TT:XW + 8 * TT + 4096].bitcast(BF16).rearrange("p (k c) -> p k c", k=8)
        ao = arena[:, XW:XW + 4096].rearrange("p (c t) -> p c t", c=8)
        P.dma("pool", wout, w_out[l].rearrange("(k p) c -> p k c", p=128), w=["wout"])
        for ti, (a, n) in enumerate(TILES):
            for oc in range(8):
                bk = rr("apj")
                for k in range(8):
                    P.op("pe", lambda e, k=k, bk=bk, oc=oc, a=a, n=n: e.matmul(
                        pb[bk][:, 0:n], wout[:, k, oc * 128:(oc + 1) * 128], mixT[:, k, a:a + n], start=(k == 0), stop=(k == 7)),
                        r=["wout"] + tk("mix", a, n), w=["pb%d" % bk])
                P.op("act", lambda e, bk=bk, oc=oc, n=n: e.copy(ao[:, oc, 0:n], pb[bk][:, 0:n]), r=["pb%d" % bk], w=["ao"])
            norm_stats(ao[:, :, 0:n], n, ["ao"], float(D), sq, sd, rstd, 6)
            for c in range(8):
                q = rr("tmp")
                P.op("dve", lambda e, c=c, q=q, n=n: e.scalar_tensor_tensor(
                    tmpa[q][:, 0:n], ao[:, c, 0:n], gain(l, 3, c), rstd[:, 0:n], ALU.mult, ALU.mult),
                    r=["ao", "rstd", "vecs"], w=[("tmpa", q)])
                P.op("pool" if c % 2 == 0 else "dve", lambda e, c=c, q=q, a=a, n=n: e.tensor_tensor(
                    xT[:, c, a:a + n], xT[:, c, a:a + n], tmpa[q][:, 0:n], ALU.add),
                    r=[("tmpa", q)] + tk("x", a, n), w=tk("x", a, n))
        barrier()
        if STAGE <= 3:
            return
        barrier()

    for l in range(2):
        ffn(l, 0, 0, 0)
        if STAGE <= 1:
            break
        mixer_phase(l)
        if STAGE >= 5:
            ffn(l, 1, 4, 1, write_out=False)
            continue
        if STAGE == 4:
            break
        if STAGE <= 3:
            barrier()
            dtmp = [arena[:, i * 512:(i + 1) * 512] for i in range(2)]
            for c in (range(6, 8) if STAGE == 2 else range(0, 6)):
                for ti, (a, n) in enumerate(TILES):
                    q = rr("dbg")
                    P.op("dve", lambda e, q=q, c=c, a=a, n=n: e.tensor_copy(dtmp[q][:, 0:n], mixT[:, c, a:a + n]),
                         r=tk("mix", a, n), w=[("dtmp", q)])
                    P.dma("sp", dbg[c * 128:(c + 1) * 128, a:a + n], dtmp[q][:, 0:n], r=[("dtmp", q)], w=["dbg"])
            break

    if STAGE <= 1 or STAGE >= 4:
        for c in range(8):
            for (a, n) in ((0, 1024), (1024, 1280)):
                P.dma("sp", yT_out[c * 128:(c + 1) * 128, a:a + n], xT[:, c, a:a + n], r=tk("x", a, n), w=["yT"])
    P.emit(final_keys=list(finals) + list(P.out_finals))
    return nc, P


def pack_vecs(inp):
    v = np.zeros((128, 2 * NVEC), np.float32)
    for l in range(2):
        cols = []
        for k in range(6):
            cols.append(inp["norm_gains"][l, k].reshape(8, 128).T)
        cols.append(inp["gla_b_alpha"][l].reshape(1, 128).T)
        for nm in ("gla_norm", "gmlp_ln_g", "gmlp_ln_b"):
            cols.append(inp[nm][l].reshape(2, 128).T)
        cols.append(inp["rwkv_mu"][l].reshape(8, 128).T)
        for nm in ("rwkv_w0", "rwkv_a0", "rwkv_kk", "rwkv_ka", "rwkv_rk", "rwkv_lnx_g", "rwkv_lnx_b"):
            cols.append(inp[nm][l].reshape(2, 128).T)
        m = np.concatenate(cols, axis=1)
        assert m.shape[1] == NVEC, m.shape
        v[:, l * NVEC:(l + 1) * NVEC] = m
    return v


def core_inputs(inp, i, shared):
    x = np.zeros((TT, D), np.float32)
    x[:T] = inp["x_prompt"][i]
    for j in range(NSEQ):
        x[T + j * SLOT: T + j * SLOT + 4] = inp["x_sample"][4 * i + j]
    sl = slice(4 * i, 4 * i + 4)
    m = dict(shared)
    m["xT_in"] = np.ascontiguousarray(x.T)
    m["c128"] = np.ascontiguousarray(inp["cache_win128"][:, sl].reshape(2, NSEQ, 128, 512))
    m["c512"] = np.ascontiguousarray(inp["cache_win512"][:, sl].reshape(2, NSEQ, 512, 512))
    m["c2048"] = np.ascontiguousarray(inp["cache_win2048"][:, sl].reshape(2, NSEQ, 2048, 512))
    m["state_gla"] = np.ascontiguousarray(inp["state_gla"][:, sl])
    m["state_rwkv"] = np.ascontiguousarray(inp["state_rwkv"][:, sl].reshape(2, NSEQ, 256, 64))
    m["state_shift"] = np.ascontiguousarray(inp["state_shift"][:, sl])
    return m


def shared_inputs(inp):
    return {
        "vecs": pack_vecs(inp), "cf": _CF_ARR, "cb": _CB_ARR, "ca": _CA_ARR,
        "w_ff_gate": inp["w_ff_gate"], "w_ff_up": inp["w_ff_up"], "w_ff_down": inp["w_ff_down"],
        "w_in": inp["w_in"], "w_out": inp["w_out"],
        "gla_w_alpha2": inp["gla_w_alpha2"],
        "gmlp_ln_rows": np.ascontiguousarray(np.stack([inp["gmlp_ln_g"], inp["gmlp_ln_b"]], axis=1)),
        "gmlp_wsT": np.ascontiguousarray(np.swapaxes(inp["gmlp_ws"], 2, 3)),
        "gmlp_bs": inp["gmlp_bs"],
        "rwkv_w2": inp["rwkv_w2"], "rwkv_a2": inp["rwkv_a2"], "rwkv_g2": inp["rwkv_g2"],
    }


_CACHE = {}


def run_cores(inp, cores):
    if "nc" not in _CACHE:
        _CACHE["nc"] = build_program()
    nc, P = _CACHE["nc"]
    shared = shared_inputs(inp)
    in_maps = [core_inputs(inp, i, shared) for i in cores]
    res = run_bass_kernel_spmd(nc, in_maps, core_ids=list(range(len(cores))))
    return res.results


def assemble(res):
    nco = len(res)
    y_p = np.stack([r["yT"].T[:T] for r in res])
    y_s = np.stack([r["yT"].T[T + j * SLOT: T + j * SLOT + 4] for r in res for j in range(NSEQ)])

    def per_prompt(fn):
        return np.stack([np.stack([fn(r, l) for r in res]) for l in range(2)])

    def per_sample(fn):
        return np.stack([np.stack([fn(r, l, j) for r in res for j in range(NSEQ)]) for l in range(2)])

    def kvrows(arr, lo, n):
        return np.ascontiguousarray(arr[:, :, lo:lo + n].transpose(2, 0, 1)).reshape(n, 2, 4, 64)

    p_gla = per_prompt(lambda r, l: r["o_gla"][l, 0].reshape(4, 32, 64))
    p_rwkv = per_prompt(lambda r, l: r["o_rwkv"][l, 0].reshape(4, 64, 64))
    p_shift = per_prompt(lambda r, l: r["o_shift"][l, 0].T.reshape(1024))
    p_w128 = per_prompt(lambda r, l: kvrows(r["okv01"][l, 0], 512 - 128, 128))
    p_w512 = per_prompt(lambda r, l: kvrows(r["okv01"][l, 1], 0, 512))
    p_w2048 = per_prompt(lambda r, l: kvrows(r["okv2"][l], 0, 2048))
    s_gla = per_sample(lambda r, l, j: r["o_gla"][l, 1 + j].reshape(4, 32, 64))
    s_rwkv = per_sample(lambda r, l, j: r["o_rwkv"][l, 1 + j].reshape(4, 64, 64))
    s_shift = per_sample(lambda r, l, j: r["o_shift"][l, 1 + j].T.reshape(1024))
    s_w128 = per_sample(lambda r, l, j: kvrows(r["okv01"][l, 0], 512 + 64 * j, 4))
    s_w512 = per_sample(lambda r, l, j: kvrows(r["okv01"][l, 1], 512 + 64 * j, 4))
    s_w2048 = per_sample(lambda r, l, j: kvrows(r["okv2"][l], T + 64 * j, 4))
    s_gv = per_sample(lambda r, l, j: r["o_gv"][l, j])
    outs = (y_p, y_s, p_gla, p_rwkv, p_shift, p_w128, p_w512, p_w2048, s_gla, s_rwkv, s_shift, s_w128, s_w512, s_w2048, s_gv)
    return tuple(np.ascontiguousarray(o, dtype=np.float32) for o in outs)


def kernel(**inp):
    inp = {k: np.asarray(v) for k, v in inp.items()}
    res = run_cores(inp, list(range(8)))
    return assemble(res)
```

```python
import os
import numpy as np
import concourse.bass as bass
import concourse.mybir as mybir
from contextlib import ExitStack
from concourse.bass_utils import run_bass_kernel_spmd

F32 = mybir.dt.float32
BF16 = mybir.dt.bfloat16
ALU = mybir.AluOpType
AF = mybir.ActivationFunctionType
AX = mybir.AxisListType

ENGS = ["pe", "dve", "act", "pool", "sp"]

D = 1024
T = 2048
SLOT = 64
NSEQ = 4
TT = T + NSEQ * SLOT
DFF = 2816
NFFC = DFF // 128
NCOL = 4624
EPS = 1e-6
TILES = [(0, 512), (512, 512), (1024, 512), (1536, 512), (2048, 256)]
C_GLA, C_GMLP, C_RWKV, C_DIL = 0, 784, 1296, 2320
DIL = ((128, 1), (512, 4), (2048, 16))
NEG = -30000.0
STAGE = int(os.environ.get("K_STAGE", "99"))
AW = 50000

V_GAIN = 0
V_BALPHA = 48
V_GLAN = 49
V_LNG = 51
V_LNB = 53
V_MU = 55
V_W0, V_A0, V_KK, V_KA, V_RK, V_LXG, V_LXB = 63, 65, 67, 69, 71, 73, 75
NVEC = 77


class Prog:
    def __init__(self, nc, n_dma_sems=28):
        self.nc = nc
        self.ops = []
        self.n_dma_sems = n_dma_sems
        self.es = ExitStack()

    def sb(self, name, shape, dt=F32):
        return self.es.enter_context(self.nc.sbuf_tensor(name, list(shape), dt))

    def ps(self, name, shape, dt=F32):
        return self.es.enter_context(self.nc.psum_tensor(name, list(shape), dt))

    def op(self, eng, fn, r=(), w=()):
        extra = tuple(k[0] for k in tuple(r) + tuple(w) if isinstance(k, tuple) and isinstance(k[0], str) and k[0][:2] == "pb" and k[0][2:].isdigit())
        self.ops.append(dict(eng=eng, fn=fn, r=tuple(r) + extra + ("__ph",), w=tuple(w), dma=False, bar=False))

    def dma(self, eng, out, in_, r=(), w=(), **kw):
        def fn(e, out=out, in_=in_, kw=kw):
            return e.dma_start(out=out, in_=in_, **kw)
        self.ops.append(dict(eng=eng, fn=fn, r=tuple(r) + ("__ph",), w=tuple(w), dma=True, bar=False))

    def barrier(self, fn):
        self.ops.append(dict(eng="dve", fn=fn, r=(), w=("__ph",), dma=False, bar=True))

    def emit(self, final_keys=()):
        nc = self.nc
        ops = self.ops
        n = len(ops)
        last_w = {}
        readers = {}
        deps = [set() for _ in range(n)]
        dma_rr = 0
        dma_rr_sw = 0
        dma_last = {}
        last_eng = {}
        bank_last = {}

        def bank_of(k):
            if isinstance(k, tuple):
                k = k[0]
            if isinstance(k, str) and k[:2] == "pb" and k[2:3].isdigit():
                return k[0:3]
            return None

        for i, o in enumerate(ops):
            d = deps[i]
            o["bankdeps"] = set()
            for bnk in {bank_of(k) for k in o["r"] + o["w"]} - {None}:
                if bnk in bank_last and ops[bank_last[bnk]]["eng"] != o["eng"]:
                    o["bankdeps"].add(bank_last[bnk])
                bank_last[bnk] = i
            if o["bar"]:
                d.update(last_eng.values())
                d.update(dma_last.values())
            else:
                for k in o["r"]:
                    if k in last_w:
                        d.add(last_w[k])
                for k in o["w"]:
                    if k in last_w:
                        d.add(last_w[k])
                    for j in readers.get(k, ()):
                        d.add(j)
            if o["dma"]:
                half = self.n_dma_sems // 2
                if o["eng"] == "pool":
                    s = half + dma_rr_sw % (self.n_dma_sems - half)
                    dma_rr_sw += 1
                else:
                    s = dma_rr % half
                    dma_rr += 1
                o["dsem"] = s
                if s in dma_last:
                    d.add(dma_last[s])
                dma_last[s] = i
            else:
                last_eng[o["eng"]] = i
            d.discard(i)
            if not o["bar"]:
                for k in o["r"]:
                    if k != "__ph":
                        readers.setdefault(k, []).append(i)
            for k in o["w"]:
                last_w[k] = i
                readers[k] = []
        fin = set()
        for k in final_keys:
            if k in last_w:
                fin.add(last_w[k])
        needed = set()
        for i, o in enumerate(ops):
            nd = set()
            for j in deps[i]:
                p = ops[j]
                if (not p["dma"]) and (not o["dma"]) and p["eng"] == o["eng"]:
                    if o["eng"] == "pe":
                        continue
                nd.add(j)
            nd |= o["bankdeps"]
            deps[i] = nd
            needed |= nd
        needed |= fin
        cnt = {e: 0 for e in ENGS}
        dcnt = {}
        for i, o in enumerate(ops):
            if o["dma"]:
                s = o["dsem"]
                dcnt[s] = dcnt.get(s, 0) + 16
                o["sig"] = ("d", s, dcnt[s])
            elif i in needed:
                cnt[o["eng"]] += 1
                o["sig"] = ("e", o["eng"], cnt[o["eng"]])
            else:
                o["sig"] = None
        es = self.es
        esem = {e: es.enter_context(nc.semaphore("sem_" + e)) for e in ENGS}
        dsem = [es.enter_context(nc.semaphore("dsem%d" % s)) for s in range(self.n_dma_sems)]

        def semof(k0, k1):
            return esem[k1] if k0 == "e" else dsem[k1]

        block = es.enter_context(nc.Block())
        by_eng = {e: [i for i, o in enumerate(ops) if o["eng"] == e] for e in ENGS}
        self.stats = {e: len(by_eng[e]) for e in ENGS}

        def run(e_obj, ename):
            seen = {}
            nwait = 0
            for i in by_eng[ename]:
                o = ops[i]
                waits = {}
                for j in deps[i]:
                    sig = ops[j]["sig"]
                    key = (sig[0], sig[1])
                    waits[key] = max(waits.get(key, 0), sig[2])
                for key, v in waits.items():
                    if seen.get(key, 0) >= v:
                        continue
                    seen[key] = v
                    e_obj.wait_ge(semof(*key), v)
                    nwait += 1
                ins = o["fn"](e_obj)
                sig = o["sig"]
                if sig is not None:
                    ins.then_inc(semof(sig[0], sig[1]), 16 if sig[0] == "d" else 1)
            if ename == "sp":
                waits = {}
                for j in fin:
                    sig = ops[j]["sig"]
                    key = (sig[0], sig[1])
                    waits[key] = max(waits.get(key, 0), sig[2])
                for key, v in waits.items():
                    if seen.get(key, 0) >= v:
                        continue
                    e_obj.wait_ge(semof(*key), v)
            self.stats["wait_" + ename] = nwait

        @block.tensor
        def _(e):
            run(e, "pe")

        @block.vector
        def _(e):
            run(e, "dve")

        @block.scalar
        def _(e):
            run(e, "act")

        @block.gpsimd
        def _(e):
            run(e, "pool")

        @block.sync
        def _(e):
            run(e, "sp")

        es.close()


def tk(name, a, n):
    return [(name, q) for q in range(a // 256, (a + n + 255) // 256)]


def _prod(s):
    r = 1
    for v in s:
        r *= v
    return r


class Carve:
    def __init__(self, arena, regions):
        self.arena = arena
        self.regions = [list(r) for r in regions]

    def take(self, shape, dt=F32):
        nel = _prod(shape)
        words = nel if dt == F32 else (nel + 1) // 2
        words = (words + 1) // 2 * 2
        for r in self.regions:
            if r[1] - r[0] >= words:
                off = r[0]
                r[0] += words
                break
        else:
            raise RuntimeError("arena region overflow: need %d words, free %s" % (words, self.regions))
        ap = self.arena[:, off:off + words]
        if dt != F32:
            ap = ap.bitcast(dt)
        ap = ap[:, 0:nel]
        if len(shape) == 2:
            ap = ap.rearrange("p (a b) -> p a b", a=shape[0])
        elif len(shape) == 3:
            ap = ap.rearrange("p (a b c) -> p a b c", a=shape[0], b=shape[1])
        elif len(shape) == 4:
            ap = ap.rearrange("p (a b c d) -> p a b c d", a=shape[0], b=shape[1], c=shape[2])
        return ap


def alibi_slopes():
    s = np.exp2(-8.0 * (np.arange(12, dtype=np.float32) + 1.0) / 12.0).astype(np.float32)
    return s.reshape(3, 4)


CF = {}
CA = {}
CB = {}


def build_consts():
    f_parts, b_parts, a_parts = [], [], []

    def adda(name, arr):
        arr = np.asarray(arr, np.float32).reshape(128, -1)
        CA[name] = (sum(a.shape[1] for a in a_parts), arr.shape[1])
        a_parts.append(arr)

    def addf(name, arr):
        arr = np.asarray(arr, np.float32).reshape(128, -1)
        CF[name] = (sum(a.shape[1] for a in f_parts), arr.shape[1])
        f_parts.append(arr)

    def addb(name, arr):
        arr = np.asarray(arr, np.float32).reshape(128, -1)
        CB[name] = (sum(a.shape[1] for a in b_parts), arr.shape[1])
        b_parts.append(arr)

    p = np.arange(128)
    ident = np.eye(128, dtype=np.float32)
    addf("ident", ident)
    addb("ident", ident)
    addf("ones64", np.ones((128, 64), np.float32))
    blk64 = (p[:, None] // 64 == p[None, :] // 64).astype(np.float32)
    addb("blk64", blk64)
    addf("blk64", blk64)
    tt = np.arange(256)
    addf("valid", np.broadcast_to(((tt % 64) < 4).astype(np.float32), (128, 256)))
    t5 = np.arange(512)
    addf("reset", np.broadcast_to(((t5 % 64) != 0).astype(np.float32), (128, 512)))
    s_ = p % 64
    t_ = np.arange(64)
    addf("tri_incl", (s_[:, None] <= t_[None, :]).astype(np.float32))
    addf("hm4", (p[:, None] // 32 == np.arange(4)[None, :]).astype(np.float32))
    c256 = np.arange(256)
    addf("vmask", (p[:, None] // 64 == (c256[None, :] // 64) % 2).astype(np.float32))
    addf("bdm", (p[:, None] // 32 == c256[None, :] // 64).astype(np.float32))
    c128 = np.arange(128)
    same = (p[:, None] // 64 == c128[None, :] // 64)
    addf("m_strict_bd", (same & (p[:, None] % 64 < c128[None, :] % 64)).astype(np.float32))
    addf("m_strictT_bd", (same & (p[:, None] % 64 > c128[None, :] % 64)).astype(np.float32))
    addf("hm2", (p[:, None] // 64 == np.arange(2)[None, :]).astype(np.float32))
    sl = alibi_slopes()
    k_ = p
    q_ = np.arange(256)
    for gi, (W, d) in enumerate(DIL):
        for h in range(4):
            di = q_[None, :] - k_[:, None]
            ok = (di >= 0) & (di <= 128)
            b = np.where(ok, -sl[gi, h] * d * di, NEG).astype(np.float32)
            adda("biasP%d_%d" % (gi, h), b)
    q1 = np.arange(128)
    for gi, (W, d) in enumerate(DIL):
        for h in range(4):
            sameslot = (k_[:, None] // 64) == (q1[None, :] // 64)
            pk = k_[:, None] % 64
            pq = q1[None, :] % 64
            real = (pk < 4) & (pq < 4)
            if gi == 0:
                ok = sameslot & real & (pk <= pq)
            else:
                ok = sameslot & real & (pk == pq)
            ok = ok | (k_[:, None] == q1[None, :])
            b = np.where(ok, -sl[gi, h] * (pq - pk), NEG).astype(np.float32)
            adda("biasS%d_%d" % (gi, h), b)
    pb0 = np.zeros((128, 16), np.float32)
    for i in range(4):
        for h in range(4):
            pb0[:, i * 4 + h] = np.where(p >= i, -sl[0, h] * (i + 128 - p), NEG)
    addf("pbias0", pb0)
    for gi in (1, 2):
        d = DIL[gi][1]
        addf("pbias%d" % gi, np.stack([-sl[gi, h] * d * (128 - p) for h in range(4)], axis=1))
    rows = [0, 1, 2, 3, 64, 65, 66, 67]
    sel = np.zeros((128, 8, 128), np.float32)
    for r, row in enumerate(rows):
        sel[row, r, :] = 1.0
    addb("sel8", sel)
    hs = np.zeros((128, 2, 4), np.float32)
    for pr in range(2):
        for h in range(4):
            hs[:, pr, h] = (p // 64 == h - 2 * pr)
    addf("hsel", hs)
    addf("tri128", (p[:, None] <= c128[None, :]).astype(np.float32))
    return np.concatenate(f_parts, axis=1), np.concatenate(b_parts, axis=1), np.concatenate(a_parts, axis=1)


_CF_ARR, _CB_ARR, _CA_ARR = build_consts()
NCA = _CA_ARR.shape[1]
NCF = _CF_ARR.shape[1]
NCB = _CB_ARR.shape[1]


def build_program():
    nc = bass.Bass("TRN2", target_bir_lowering=False)
    P = Prog(nc)

    def din(name, shape):
        return nc.dram_tensor(name, list(shape), F32, kind="ExternalInput").ap()

    def dout(name, shape):
        return nc.dram_tensor(name, list(shape), F32, kind="ExternalOutput").ap()

    xT_in = din("xT_in", [D, TT])
    vecs_in = din("vecs", [128, 2 * NVEC])
    cf_in = din("cf", [128, NCF])
    cb_in = din("cb", [128, NCB])
    ca_in = din("ca", [128, NCA])
    w_gate = din("w_ff_gate", [2, 2, D, DFF])
    w_up = din("w_ff_up", [2, 2, D, DFF])
    w_down = din("w_ff_down", [2, 2, DFF, D])
    w_in = din("w_in", [2, D, NCOL])
    w_out = din("w_out", [2, D, D])
    c128_in = din("c128", [2, NSEQ, 128, 512])
    c512_in = din("c512", [2, NSEQ, 512, 512])
    c2048_in = din("c2048", [2, NSEQ, 2048, 512])
    caches = (c128_in, c512_in, c2048_in)
    sgla_in = din("state_gla", [2, NSEQ, 4, 32, 64])
    srwkv_in = din("state_rwkv", [2, NSEQ, 256, 64])
    sshift_in = din("state_shift", [2, NSEQ, D])
    walpha_in = din("gla_w_alpha2", [2, 16, 128])
    lnrow_in = din("gmlp_ln_rows", [2, 2, 256])
    wsT_in = din("gmlp_wsT", [2, 4, 128, 128])
    bs_in = din("gmlp_bs", [2, 4, 128])
    w2_in = din("rwkv_w2", [2, 64, 256])
    a2_in = din("rwkv_a2", [2, 64, 256])
    g2_in = din("rwkv_g2", [2, 128, 256])

    yT_out = dout("yT", [D, TT])
    okv01 = dout("okv01", [2, 2, 2, 256, 768])
    okv2 = dout("okv2", [2, 2, 256, TT])
    o_gla = dout("o_gla", [2, 5, 128, 64])
    o_rwkv = dout("o_rwkv", [2, 5, 256, 64])
    o_shift = dout("o_shift", [2, 5, 128, 8])
    o_gv = dout("o_gv", [2, NSEQ, 4, 256])
    dbg = dout("dbg", [D, TT]) if STAGE < 4 else None
    finals = ["yT", "okv", "o_gla", "o_rwkv", "o_shift", "o_gv", "dbg"]
    xspill = nc.dram_tensor("xspill", [128, 8 * TT], F32).ap()

    vecs = P.sb("vecs_sb", [128, 2 * NVEC], F32)
    P.dma("sp", vecs[:], vecs_in, w=["vecs"])
    cf = P.sb("cf_sb", [128, NCF], F32)
    P.dma("sp", cf[:], cf_in, w=["cf"])
    cbt = P.sb("cb_sb", [128, NCB], BF16)
    P.dma("pool", cbt[:], cb_in, w=["cb"])
    ones_bf = P.sb("ones_bf", [128, 128], BF16)
    P.op("dve", lambda e: e.memset(ones_bf[:], 1.0), w=["ones_bf"])
    gh = P.sb("gh", [128, 2, 2, 8], F32)
    bar_t = P.sb("bar_t", [128, 2], F32)

    def CFv(name, shape=None):
        o, n = CF[name]
        ap = cf[:, o:o + n]
        if shape is not None and len(shape) == 2:
            ap = ap.rearrange("p (a b) -> p a b", a=shape[0])
        return ap

    def CBv(name, shape=None):
        o, n = CB[name]
        ap = cbt[:, o:o + n]
        if shape is not None and len(shape) == 2:
            ap = ap.rearrange("p (a b) -> p a b", a=shape[0])
        return ap

    ident_bf = CBv("ident")
    ident_f = CFv("ident")
    blk64_bf = CBv("blk64")
    blk64_f = CFv("blk64")
    ones64_f = CFv("ones64")

    def vcol(l, j, n=1):
        return vecs[:, l * NVEC + j: l * NVEC + j + n]

    def gain(l, k, c):
        return vcol(l, V_GAIN + k * 8 + c)

    for l in range(2):
        for f, k in ((0, 1), (1, 5)):
            P.op("dve", lambda e, l=l, f=f, k=k: e.tensor_scalar(
                gh[:, l, f, :], vcol(l, V_GAIN + k * 8, 8), 0.5, None, ALU.mult), r=["vecs"], w=["gh"])

    def barrier():
        P.barrier(lambda e: e.memset(bar_t[:], 0.0))

    arena = P.sb("arena", [128, AW], F32)
    XW = 8 * TT
    cvf = Carve(arena, [(0, AW)])
    xT = cvf.take([8, TT], F32)
    GW = 768
    ffo = cvf.take([8, GW], F32)
    xng = cvf.take([8, GW], BF16)
    hbuf = cvf.take([NFFC, GW], BF16)
    wg_sb = [cvf.take([8, 256], BF16) for i in range(2)]
    wu_sb = [cvf.take([8, 256], BF16) for i in range(2)]
    wd_sb = [cvf.take([NFFC, 256], BF16) for i in range(2)]
    sq = cvf.take([4, 512], BF16)
    sd = cvf.take([512], F32)
    rstd = cvf.take([512], F32)
    sg = [cvf.take([512], F32) for i in range(2)]
    tmpa = [cvf.take([512], F32) for i in range(2)]
    xn = arena[:, XW:XW + 4 * TT].bitcast(BF16).rearrange("p (c t) -> p c t", c=8)
    mixT = arena[:, XW + 4 * TT:XW + 8 * TT].bitcast(BF16).rearrange("p (c t) -> p c t", c=8)
    MIXER_REGIONS = [(0, XW), (XW + 8 * TT, AW)]

    for (a, n) in ((0, 768), (768, 768), (1536, 768)):
        for c in range(8):
            P.dma("sp", xT[:, c, a:a + n], xT_in[c * 128:(c + 1) * 128, a:a + n], w=tk("x", a, n))

    pb = [P.ps("pb%d" % i, [128, 512], F32) for i in range(8)]
    cnt = {}

    def rr(name, m=2):
        v = cnt.get(name, 0)
        cnt[name] = v + 1
        return v % m

    def norm_stats(src3, n, rkeys, scale_div, sq_t, sd_t, rstd_t, pbank, lhs=None, eps=EPS, nchunk=8, kp=""):
        lhs = ones_bf[:] if lhs is None else lhs
        hh = nchunk // 2
        for half in range(2):
            P.op("act", lambda e, half=half: e.activation(sq_t[:, 0:hh, 0:n], src3[:, half * hh:(half + 1) * hh, :], AF.Square),
                 r=rkeys, w=["sq" + kp])
            for c in range(hh):
                cc_ = half * hh + c
                P.op("pe", lambda e, c=c, cc_=cc_: e.matmul(pb[pbank][:, 0:n], lhs, sq_t[:, c, 0:n], start=(cc_ == 0), stop=(cc_ == nchunk - 1)),
                     r=["sq" + kp, "ones_bf", "cb"], w=["pb%d" % pbank])
        P.op("act", lambda e: e.activation(sd_t[:, 0:n], pb[pbank][:, 0:n], AF.Sqrt, bias=eps, scale=1.0 / scale_div),
             r=["pb%d" % pbank], w=["sd" + kp])
        P.op("dve", lambda e: e.reciprocal(rstd_t[:, 0:n], sd_t[:, 0:n]), r=["sd" + kp], w=["rstd" + kp])

    def ffn(l, f, k_pre, f_post):
        wg_d = w_gate[l, f].rearrange("(k p) c -> p k c", p=128)
        wu_d = w_up[l, f].rearrange("(k p) c -> p k c", p=128)
        wd_d = w_down[l, f].rearrange("(k p) c -> p k c", p=128)
        tiles = [(0, 512), (512, 256)]

        def prenorm(g):
            t0 = g * GW
            for j, (ra, n) in enumerate(tiles):
                a = t0 + ra
                norm_stats(xT[:, :, a:a + n], n, tk("x", a, n), float(D), sq, sd, rstd, 6)
                for c in range(8):
                    P.op("dve", lambda e, c=c, a=a, ra=ra, n=n: e.scalar_tensor_tensor(
                        xng[:, c, ra:ra + n], xT[:, c, a:a + n], gain(l, k_pre, c), rstd[:, 0:n], ALU.mult, ALU.mult),
                        r=tk("x", a, n) + ["rstd", "vecs"], w=[("xng", j)])

        def phase1(g):
            for fg in range(11):
                b = rr("w1")
                P.dma("pool", wg_sb[b], wg_d[:, :, fg * 256:(fg + 1) * 256], w=[("wg", b)])
                P.dma("pool", wu_sb[b], wu_d[:, :, fg * 256:(fg + 1) * 256], w=[("wu", b)])
                for jj in range(2):
                    ffc = fg * 2 + jj
                    for j, (ra, n) in enumerate(tiles):
                        q = rr("ffn1")
                        pg, pu = pb[q], pb[2 + q]
                        for k in range(8):
                            P.op("pe", lambda e, k=k, pg=pg, b=b, jj=jj, ra=ra, n=n: e.matmul(
                                pg[:, 0:n], wg_sb[b][:, k, jj * 128:(jj + 1) * 128], xng[:, k, ra:ra + n],
                                start=(k == 0), stop=(k == 7)), r=[("wg", b), ("xng", j)], w=["pb%d" % q])
                        for k in range(8):
                            P.op("pe", lambda e, k=k, pu=pu, b=b, jj=jj, ra=ra, n=n: e.matmul(
                                pu[:, 0:n], wu_sb[b][:, k, jj * 128:(jj + 1) * 128], xng[:, k, ra:ra + n],
                                start=(k == 0), stop=(k == 7)), r=[("wu", b), ("xng", j)], w=["pb%d" % (2 + q)])
                        P.op("act", lambda e, pg=pg, q=q, n=n: e.activation(sg[q][:, 0:n], pg[:, 0:n], AF.Silu),
                             r=["pb%d" % q], w=[("sg", q)])
                        P.op("dve", lambda e, pu=pu, q=q, ffc=ffc, ra=ra, n=n: e.tensor_tensor(
                            hbuf[:, ffc, ra:ra + n], sg[q][:, 0:n], pu[:, 0:n], ALU.mult),
                            r=[("sg", q), "pb%d" % (2 + q)], w=[("h", j, ffc)])

        def phase2(g):
            for og in range(4):
                b = rr("w2")
                P.dma("pool", wd_sb[b], wd_d[:, :, og * 256:(og + 1) * 256], w=[("wd", b)])
                for jj in range(2):
                    oc = og * 2 + jj
                    for j, (ra, n) in enumerate(tiles):
                        q = rr("ffn2")
                        po = pb[4 + q]
                        for ffc in range(NFFC):
                            P.op("pe", lambda e, ffc=ffc, po=po, b=b, jj=jj, ra=ra, n=n: e.matmul(
                                po[:, 0:n], wd_sb[b][:, ffc, jj * 128:(jj + 1) * 128], hbuf[:, ffc, ra:ra + n],
                                start=(ffc == 0), stop=(ffc == NFFC - 1)),
                                r=[("wd", b), ("h", j, ffc)], w=["pb%d" % (4 + q)])
                        P.op("act", lambda e, po=po, oc=oc, ra=ra, n=n: e.copy(ffo[:, oc, ra:ra + n], po[:, 0:n]),
                             r=["pb%d" % (4 + q)], w=[("ffo", j)])

        def postnorm(g):
            t0 = g * GW
            for j, (ra, n) in enumerate(tiles):
                a = t0 + ra
                norm_stats(ffo[:, :, ra:ra + n], n, [("ffo", j)], float(D), sq, sd, rstd, 7)
                for c in range(8):
                    q = rr("tmp")
                    P.op("dve", lambda e, c=c, q=q, ra=ra, n=n: e.scalar_tensor_tensor(
                        tmpa[q][:, 0:n], ffo[:, c, ra:ra + n], gh[:, l, f_post, c:c + 1], rstd[:, 0:n], ALU.mult, ALU.mult),
                        r=[("ffo", j), "rstd", "gh"], w=[("tmpa", q)])
                    P.op("dve", lambda e, c=c, q=q, a=a, n=n: e.tensor_tensor(
                        xT[:, c, a:a + n], xT[:, c, a:a + n], tmpa[q][:, 0:n], ALU.add),
                        r=[("tmpa", q)] + tk("x", a, n), w=tk("x", a, n))

        prenorm(0)
        for g in range(3):
            phase1(g)
            if g + 1 < 3:
                prenorm(g + 1)
            phase2(g)
            postnorm(g)

    def proj(wt, col0, m, a, n, pbank, pkey, wkey):
        for k in range(8):
            P.op("pe", lambda e, k=k: e.matmul(pb[pbank][0:m, 0:n], wt[:, k, col0:col0 + m], xn[:, k, a:a + n],
                                               start=(k == 0), stop=(k == 7)),
                 r=[wkey] + tk("xn", a, n), w=[pkey])

    def attention(l):
        cv = Carve(arena, MIXER_REGIONS[0:1])
        cv2 = Carve(arena, MIXER_REGIONS[1:2])
        cfa = cv.take([NCA], F32)
        P.dma("sp", cfa, ca_in, w=["cfa"])
        wdil = cv.take([8, 768], BF16)
        qT = cv.take([2, TT], BF16)
        kT = cv.take([2, TT], BF16)
        vT = cv.take([2, TT], BF16)
        Vb = cv.take([18, 4, 66], BF16)
        numacc = cv2.take([2, TT], F32)
        denacc = cv2.take([2, TT], F32)
        stg = [cv2.take([512], F32) for i in range(4)]
        qtm = cv2.take([3, 2, 256], BF16)
        st = [cv.take([256], F32) for i in range(2)]
        pT = [cv.take([256], BF16) for i in range(2)]
        kvrows = stg[0:2]
        vrows_bf = [cv.take([256], BF16) for i in range(2)]
        prod = cv.take([256], F32)
        sc = cv.take([16], F32)
        sc2 = cv.take([16], F32)
        pP = cv.take([16], BF16)
        numS = cv2.take([2, 4, 16], F32)
        denS = cv2.take([4, 16], F32)
        tmpS = cv2.take([2, 64], F32)
        redS = cv2.take([2, 16], F32)

        def CAv(name):
            o, n = CA[name]
            return cfa[:, o:o + n]

        P.op("pool", lambda e: e.memset(Vb[:, :, :, 64:65], 1.0), w=["Vb1"])
        P.op("pool", lambda e: e.memset(denacc, 0.0), w=["denacc"])
        P.op("pool", lambda e: e.memset(numS, 0.0), w=["numS"])
        P.op("pool", lambda e: e.memset(denS, 0.0), w=["denS"])

        for gi, (W, d) in enumerate(DIL):
            if str(gi) not in os.environ.get("K_GROUPS", "012"):
                continue
            cb0 = C_DIL + gi * 768
            P.dma("pool", wdil, w_in[l].rearrange("(k p) c -> p k c", p=128)[:, :, cb0:cb0 + 768], w=["wdil"])
            for ti, (a, n) in enumerate(TILES):
                for cc in range(6):
                    bk = rr("apj")
                    proj(wdil, cc * 128, 128, a, n, bk, "pb%d" % bk, "wdil")
                    which, pair = cc // 2, cc % 2
                    dstT = (qT, kT, vT)[which]
                    if a < T and d > 1:
                        dst = dstT[:, pair, 0:T].rearrange("p (r m) -> p r m", r=d)[:, :, a // d:(a + n) // d]
                        srcv = lambda ap, d=d: ap.rearrange("p (m r) -> p r m", r=d)
                    else:
                        dst = dstT[:, pair, a:a + n]
                        srcv = lambda ap: ap
                    wk = [(("qT", "kT", "vT")[which], pair)]
                    if which == 0:
                        P.op("act", lambda e, bk=bk, dst=dst, srcv=srcv, n=n: e.activation(
                            dst, srcv(pb[bk][:, 0:n]), AF.Copy, scale=0.125), r=["pb%d" % bk], w=wk)
                    else:
                        kv = which - 1
                        need_out = (gi == 2) or (a >= 1536)
                        if need_out:
                            sb_ = rr("stg", 4)
                            P.op("dve", lambda e, bk=bk, sb_=sb_, n=n: e.tensor_copy(stg[sb_][:, 0:n], pb[bk][:, 0:n]),
                                 r=["pb%d" % bk], w=[("stg", sb_)])
                            P.op("act", lambda e, bk=bk, dst=dst, srcv=srcv, n=n: e.copy(dst, srcv(pb[bk][:, 0:n])), r=["pb%d" % bk], w=wk)
                            if gi == 2:
                                P.dma("sp", okv2[l, kv, pair * 128:(pair + 1) * 128, a:a + n], stg[sb_][:, 0:n],
                                      r=[("stg", sb_)], w=["okv"])
                            else:
                                P.dma("sp", okv01[l, gi, kv, pair * 128:(pair + 1) * 128, a - 1536:a - 1536 + n], stg[sb_][:, 0:n],
                                      r=[("stg", sb_)], w=["okv"])
                        else:
                            P.op("dve", lambda e, bk=bk, dst=dst, srcv=srcv, n=n: e.tensor_copy(dst, srcv(pb[bk][:, 0:n])), r=["pb%d" % bk], w=wk)
            for blk in range(2):
                bk = rr("apj")
                for k in range(8):
                    P.op("pe", lambda e, k=k, bk=bk, blk=blk: e.matmul(
                        pb[bk][:, 0:256], xn[:, k, T + blk * 128:T + (blk + 1) * 128], wdil[:, k, 0:256],
                        start=(k == 0), stop=(k == 7)), r=["wdil"] + tk("xn", T, 256), w=["pb%d" % bk])
                P.op("act", lambda e, bk=bk, blk=blk, gi=gi: e.activation(qtm[:, gi, blk, :], pb[bk][:, 0:256], AF.Copy, scale=0.125),
                     r=["pb%d" % bk], w=["qtm"])
            for blk in range(18):
                for pair in range(2):
                    bk = rr("apj")
                    pst = pb[bk][:, 0:64].bitcast(BF16)
                    P.op("pe", lambda e, pst=pst, pair=pair, blk=blk: e.transpose(pst, vT[:, pair, blk * 128:(blk + 1) * 128], ident_bf),
                         r=[("vT", pair), "cb"], w=["pb%d" % bk])
                    eng = "act" if (blk + pair) % 2 == 0 else "dve"
                    if eng == "act":
                        P.op("act", lambda e, pst=pst, pair=pair, blk=blk: e.copy(
                            Vb[:, blk, 2 * pair:2 * pair + 2, 0:64], pst.rearrange("p (h d) -> p h d", h=2)),
                            r=["pb%d" % bk], w=[("Vb", blk)])
                    else:
                        P.op("dve", lambda e, pst=pst, pair=pair, blk=blk: e.tensor_copy(
                            Vb[:, blk, 2 * pair:2 * pair + 2, 0:64], pst.rearrange("p (h d) -> p h d", h=2)),
                            r=["pb%d" % bk], w=[("Vb", blk)])
            cbk = 16 // d
            for h in range(4):
                p_, hin = h // 2, h % 2
                pr0 = 64 * hin
                dr = 64 if hin == 0 else 0
                biasP = CAv("biasP%d_%d" % (gi, h))
                biasS = CAv("biasS%d_%d" % (gi, h))
                def kb_info(kb):
                    if kb < 16:
                        return (256 if ((kb + 1) % cbk != 0) else 128), biasP
                    return 128, biasS

                def emit_S(kb):
                    N, bias = kb_info(kb)
                    par = kb % 2
                    P.op("pe", lambda e, par=par, kb=kb, N=N, p_=p_, pr0=pr0: e.matmul(
                        pb[5 + par][:, 0:N], kT[pr0:pr0 + 64, p_, kb * 128:(kb + 1) * 128],
                        qT[pr0:pr0 + 64, p_, kb * 128:kb * 128 + N], start=True, stop=True),
                        r=[("kT", p_), ("qT", p_)], w=["pb%d" % (5 + par)])

                def emit_rest(kb):
                    N, bias = kb_info(kb)
                    par = kb % 2
                    P.op("dve", lambda e, par=par, N=N, bias=bias: e.tensor_tensor(
                        st[par][:, 0:N], pb[5 + par][:, 0:N], bias[:, 0:N], ALU.add),
                        r=["pb%d" % (5 + par), "cfa"], w=[("st", par)])
                    P.op("act", lambda e, par=par, N=N: e.activation(pT[par][:, 0:N], st[par][:, 0:N], AF.Exp),
                         r=[("st", par)], w=[("pT", par)])
                    for half in range(N // 128):
                        qb = kb + half
                        tb, col0 = qb // 4, (qb % 4) * 128
                        if half == 1:
                            start, stop = True, False
                        else:
                            start = (qb >= 16) or (qb % cbk == 0)
                            stop = True
                        if hin == 0:
                            P.op("pe", lambda e, par=par, kb=kb, h=h, tb=tb, col0=col0, half=half, start=start, stop=stop: e.matmul(
                                pb[tb][0:65, col0:col0 + 128], Vb[:, kb, h, 0:65], pT[par][:, half * 128:(half + 1) * 128],
                                start=start, stop=stop), r=[("pT", par), ("Vb", kb), "Vb1"], w=["pb%d" % tb])
                            continue
                        P.op("pe", lambda e, par=par, kb=kb, h=h, tb=tb, col0=col0, half=half, start=start, stop=stop, pr0=pr0: e.matmul(
                            pb[tb][pr0:pr0 + 64, col0:col0 + 128], Vb[:, kb, h, 0:64], pT[par][:, half * 128:(half + 1) * 128],
                            start=start, stop=stop), r=[("pT", par), ("Vb", kb)], w=["pb%d" % tb])
                        P.op("pe", lambda e, par=par, kb=kb, h=h, tb=tb, col0=col0, half=half, start=start, stop=stop, dr=dr: e.matmul(
                            pb[tb][dr:dr + 1, col0:col0 + 128], Vb[:, kb, h, 64:65], pT[par][:, half * 128:(half + 1) * 128],
                            start=start, stop=stop), r=[("pT", par), ("Vb", kb), "Vb1"], w=["pb%d" % tb])

                def evac_bank(tb):
                    a, n = TILES[tb]
                    for (rows, acc, nm) in ((slice(pr0, pr0 + 64), numacc, "numacc"), (slice(dr, dr + 1), denacc, "denacc")):
                        if a < T and d > 1:
                            if d == 4:
                                dst = acc[rows, p_, 0:T].rearrange("p (m r) -> p r m", r=4)[:, tb, :]
                                src = pb[tb][rows, 0:512]
                            else:
                                dst = acc[rows, p_, 0:T].rearrange("p (m r) -> p r m", r=16)[:, 4 * tb:4 * tb + 4, :]
                                src = pb[tb][rows, 0:512].rearrange("p (r m) -> p r m", r=4)
                        else:
                            dst = acc[rows, p_, a:a + n]
                            src = pb[tb][rows, 0:n]
                        kk_ = (nm, p_, "P" if a < T else 4)
                        if str(gi) == os.environ.get("K_GROUPS", "012")[0] and nm == "numacc":
                            P.op("act", lambda e, dst=dst, src=src: e.copy(dst, src), r=["pb%d" % tb], w=[kk_])
                        else:
                            P.op("dve", lambda e, dst=dst, src=src: e.tensor_tensor(dst, dst, src, ALU.add),
                                 r=["pb%d" % tb, kk_, nm], w=[kk_])


                emit_S(0)
                for kb in range(18):
                    if kb + 1 < 18:
                        emit_S(kb + 1)
                    emit_rest(kb)
                    if kb in (3, 7, 11, 15, 17):
                        evac_bank(min(kb // 4, 4))
        hsel = CFv("hsel", [2, 4])
        for j in range(NSEQ):
            blk, rbase = j // 2, (j % 2) * 4
            c0 = T + j * SLOT
            for gi, (W, d) in enumerate(DIL):
                nsets = 1 if gi == 0 else 4
                for si in range(nsets):
                    b = rr("kvr")
                    if gi == 0:
                        src = caches[0][l, j, :, :]
                        qs = [0, 1, 2, 3]
                    else:
                        src = caches[gi][l, j, :, :].rearrange("(m r) c -> r m c", r=d)[si]
                        qs = [si]
                    nq = len(qs)
                    ncol = nq * 4
                    P.dma("sp", kvrows[b], src, w=[("stg", b)])
                    P.op("pool", lambda e, b=b: e.tensor_copy(vrows_bf[b], kvrows[b][:, 256:512]), r=[("stg", b)], w=[("vrb", b)])
                    for qi, i in enumerate(qs):
                        bk = 5 + rr("att")
                        P.op("pe", lambda e, bk=bk, i=i, gi=gi, rbase=rbase, blk=blk: e.matmul(
                            pb[bk][:, 0:256], CBv("sel8", [8, 128])[:, rbase + i, :], qtm[:, gi, blk, :], start=True, stop=True),
                            r=["qtm", "cb"], w=["pb%d" % bk])
                        P.op("dve", lambda e, bk=bk, b=b: e.tensor_tensor(prod, kvrows[b][:, 0:256], pb[bk][:, 0:256], ALU.mult),
                             r=["pb%d" % bk, ("stg", b)], w=["prod"])
                        P.op("dve", lambda e, qi=qi: e.tensor_reduce(
                            sc[:, qi * 4:qi * 4 + 4], prod.rearrange("p (h d) -> p h d", h=4), AX.X, ALU.add),
                            r=["prod"], w=["sc"])
                    pbias = CFv("pbias%d" % gi)
                    P.op("dve", lambda e, ncol=ncol, pbias=pbias: e.tensor_tensor(sc2[:, 0:ncol], sc[:, 0:ncol], pbias[:, 0:ncol], ALU.add),
                         r=["sc", "cf"], w=["sc2"])
                    P.op("act", lambda e, ncol=ncol: e.activation(pP[:, 0:ncol], sc2[:, 0:ncol], AF.Exp), r=["sc2"], w=["pP"])
                    c0s = qs[0] * 4
                    P.op("pe", lambda e, ncol=ncol: e.matmul(pb[7][:, 64:64 + ncol], ones_bf[:], pP[:, 0:ncol], start=True, stop=True),
                         r=["pP", "ones_bf"], w=["pb7"])
                    for p_ in range(2):
                        P.op("pe", lambda e, p_=p_, b=b, ncol=ncol: e.matmul(
                            pb[7][:, p_ * 16:p_ * 16 + ncol], vrows_bf[b][:, p_ * 128:(p_ + 1) * 128], pP[:, 0:ncol], start=True, stop=True),
                            r=["pP", ("vrb", b)], w=["pb7"])
                    P.op("dve", lambda e, j=j, c0s=c0s, ncol=ncol: e.tensor_tensor(
                        numS[:, :, j, c0s:c0s + ncol], numS[:, :, j, c0s:c0s + ncol],
                        pb[7][:, 0:32].rearrange("p (a c) -> p a c", a=2)[:, :, 0:ncol], ALU.add), r=["pb7", "numS"], w=["numS"])
                    P.op("dve", lambda e, j=j, c0s=c0s, ncol=ncol: e.tensor_tensor(
                        denS[:, j, c0s:c0s + ncol], denS[:, j, c0s:c0s + ncol], pb[7][:, 64:64 + ncol], ALU.add), r=["pb7", "denS"], w=["denS"])
        P.op("dve", lambda e: e.tensor_tensor(
            tmpS.rearrange("p a (x h) -> p a x h", h=4), numS.rearrange("p a j (q h) -> p a (j q) h", h=4),
            hsel.unsqueeze(2).broadcast_to([128, 2, 16, 4]), ALU.mult), r=["numS", "cf"], w=["tmpS"])
        P.op("dve", lambda e: e.tensor_reduce(redS, tmpS.rearrange("p a (x h) -> p a x h", h=4), AX.X, ALU.add), r=["tmpS"], w=["redS"])
        nsv = numacc[:, :, T:TT].rearrange("p a (j s) -> p a j s", s=64)[:, :, :, 0:4]
        P.op("dve", lambda e: e.tensor_tensor(nsv, nsv, redS.rearrange("p a (j q) -> p a j q", q=4), ALU.add),
             r=["redS", ("numacc", 0, 4), ("numacc", 1, 4)], w=[("numacc", 0, 4), ("numacc", 1, 4)])
        for p_ in range(2):
            for hin in range(2):
                dr = 64 if hin == 0 else 0
                h = 2 * p_ + hin
                dsv = denacc[dr:dr + 1, p_, T:TT].rearrange("p (j s) -> p j s", s=64)[:, :, 0:4]
                P.op("dve", lambda e, dsv=dsv, dr=dr, h=h: e.tensor_tensor(
                    dsv, dsv, denS[dr:dr + 1, :, :].rearrange("p j (q h) -> p j q h", h=4)[:, :, :, h], ALU.add),
                    r=["denS", ("denacc", p_, 4), "denacc"], w=[("denacc", p_, 4)])
        if STAGE == 2:
            for ci, (srcT, nm) in enumerate(((numacc, "numacc"), (denacc, "denacc"))):
                for p_ in range(2):
                    P.dma("sp", dbg[ci * 256 + p_ * 128: ci * 256 + (p_ + 1) * 128, :], srcT[:, p_, :],
                          r=[(nm, p_, "P"), (nm, p_, 4)], w=["dbg"])
            for ci, srcT in enumerate((qT, kT)):
                for (a, n) in TILES:
                    q = rr("dbgq")
                    P.op("dve", lambda e, q=q, srcT=srcT, a=a, n=n: e.tensor_copy(stg[2 + q][:, 0:n], srcT[:, 0, a:a + n]),
                         r=[("qT", 0), ("kT", 0)], w=[("stg", 2 + q)])
                    P.dma("sp", dbg[512 + ci * 128: 512 + (ci + 1) * 128, a:a + n], stg[2 + q][:, 0:n], r=[("stg", 2 + q)], w=["dbg"])
        for p_ in range(2):
            for hin in range(2):
                pr0 = 64 * hin
                dr = 64 if hin == 0 else 0
                P.op("dve", lambda e, p_=p_, dr=dr: e.reciprocal(denacc[dr:dr + 1, p_, :], denacc[dr:dr + 1, p_, :]),
                     r=[("denacc", p_, "P"), ("denacc", p_, 4), "denacc"], w=[("rden", p_, hin)])
                for tb, (a, n) in enumerate(TILES):
                    bk = rr("apj")
                    P.op("pe", lambda e, bk=bk, p_=p_, dr=dr, pr0=pr0, a=a, n=n: e.matmul(
                        pb[bk][pr0:pr0 + 64, 0:n], ones64_f[dr:dr + 1, 0:64], denacc[dr:dr + 1, p_, a:a + n], start=True, stop=True),
                        r=[("rden", p_, hin), "cf"], w=["pb%d" % bk])
                    P.op("dve", lambda e, bk=bk, p_=p_, pr0=pr0, a=a, n=n: e.tensor_tensor(
                        mixT[pr0:pr0 + 64, 6 + p_, a:a + n], numacc[pr0:pr0 + 64, p_, a:a + n], pb[bk][pr0:pr0 + 64, 0:n], ALU.mult),
                        r=["pb%d" % bk, ("numacc", p_, "P"), ("numacc", p_, 4)], w=tk("mix", a, n))

    def headstat(src, n, lhs, div, eps, sq_t, sd_t, rs_t, pbank, kp):
        P.op("act", lambda e: e.activation(sq_t[:, 0:n], src, AF.Square), r=[kp + "src"], w=[kp + "sq"])
        P.op("pe", lambda e: e.matmul(pb[pbank][:, 0:n], lhs, sq_t[:, 0:n], start=True, stop=True),
             r=[kp + "sq", "cb"], w=["pb%d" % pbank])
        P.op("act", lambda e: e.activation(sd_t[:, 0:n], pb[pbank][:, 0:n], AF.Sqrt, bias=eps, scale=1.0 / div),
             r=["pb%d" % pbank], w=[kp + "sd"])
        P.op("dve", lambda e: e.reciprocal(rs_t[:, 0:n], sd_t[:, 0:n]), r=[kp + "sd"], w=[kp + "rs"])

    def gla(l):
        cv = Carve(arena, MIXER_REGIONS)
        wgla = cv.take([8, 784], BF16)
        P.dma("pool", wgla, w_in[l].rearrange("(k p) c -> p k c", p=128)[:, :, 0:784], w=["wgla"])
        walpha = cv.take([128], BF16)
        P.dma("pool", walpha[0:16, :], walpha_in[l], w=["walpha"])
        nbal = cv.take([2], F32)
        P.op("dve", lambda e: e.tensor_scalar(nbal[:, 0:1], vcol(l, V_BALPHA), -1.0, None, ALU.mult), r=["vecs"], w=["nbal"])
        S_p = cv.take([256], F32)
        S_slot = [cv.take([256], F32) for j in range(NSEQ)]
        S_bf = cv.take([256], BF16)
        Sd = cv.take([64], F32)
        P.op("pool", lambda e: e.memset(S_p, 0.0), w=["S_p"])
        for j in range(NSEQ):
            P.op("pool", lambda e, j=j: e.memset(S_slot[j], 0.0), w=[("S_slot", j)])
            for h in range(4):
                P.dma("sp", S_slot[j][32 * h:32 * h + 32, 64 * h:64 * h + 64], sgla_in[l, j, h], r=[], w=[("S_slot", j)])
        qTt = cv.take([512], F32)
        kTt = cv.take([512], F32)
        alr = cv.take([512], BF16)
        e1 = cv.take([512], F32)
        l1 = cv.take([512], F32)
        la = cv.take([512], F32)
        bcum = cv.take([512], F32)
        eb = cv.take([512], F32)
        enb = cv.take([512], F32)
        ebl = cv.take([8], F32)
        qe = cv.take([512], BF16)
        ke = cv.take([512], BF16)
        kd = cv.take([512], BF16)
        ke_bd = cv.take([8, 4, 64], BF16)
        attm = [cv.take([2, 4, 64], BF16) for i in range(2)]
        Vbd = [cv.take([256], BF16) for i in range(4)]
        Vtm = [cv.take([256], BF16) for i in range(4)]
        kdtm = [cv.take([128], BF16) for i in range(4)]
        t1 = cv.take([256], F32)
        kdtm_all = cv.take([4, 128], BF16)
        t1x = cv.take([4, 256], F32)
        xdup = [cv.take([8, 2, 64], BF16) for i in range(4)]
        oT = cv.take([2, 512], F32)
        sqg = cv.take([512], BF16)
        sdg = cv.take([512], F32)
        rsg = [cv.take([512], F32) for i in range(2)]
        sgl = cv.take([2, 512], F32)
        t2 = cv.take([512], F32)
        tri = CFv("tri_incl")
        hm4 = CFv("hm4")
        vmask = CFv("vmask")
        bdm = CFv("bdm")
        valid = CFv("valid")
        reset = CFv("reset")
        wcols = lambda c0, m: (wgla, c0, m)

        cur = {"S": S_p, "key": "S_p"}
        P.op("act", lambda e: e.copy(S_bf, S_p), r=["S_p"], w=["S_bf"])
        for ti, (a, n) in enumerate(TILES):
            nch = n // 64
            proj(wgla, 0, 128, a, n, 0, "pb0", "wgla")
            P.op("act", lambda e, n=n: e.activation(qTt[:, 0:n], pb[0][:, 0:n], AF.Copy, scale=32 ** -0.5), r=["pb0"], w=["qTt"])
            proj(wgla, 128, 128, a, n, 1, "pb1", "wgla")
            P.op("dve", lambda e, n=n: e.tensor_copy(kTt[:, 0:n], pb[1][:, 0:n]), r=["pb1"], w=["kTt"])
            proj(wgla, 512, 16, a, n, 0, "pb0", "wgla")
            P.op("act", lambda e, n=n: e.copy(alr[0:16, 0:n], pb[0][0:16, 0:n]), r=["pb0"], w=["alr"])
            P.op("pe", lambda e, n=n: e.matmul(pb[2][:, 0:n], walpha[0:16, :], alr[0:16, 0:n], start=True, stop=True),
                 r=["alr", "walpha"], w=["pb2"])
            P.op("act", lambda e, n=n: e.activation(e1[:, 0:n], pb[2][:, 0:n], AF.Exp, bias=nbal[:, 0:1], scale=-1.0),
                 r=["pb2", "nbal"], w=["e1"])
            P.op("act", lambda e, n=n: e.activation(l1[:, 0:n], e1[:, 0:n], AF.Ln, bias=1.0), r=["e1"], w=["l1"])
            if a < T:
                P.op("dve", lambda e, n=n: e.tensor_scalar(la[:, 0:n], l1[:, 0:n], -1.0 / 16.0, None, ALU.mult), r=["l1"], w=["la"])
            else:
                P.op("dve", lambda e, n=n: e.scalar_tensor_tensor(la[:, 0:n], l1[:, 0:n], -1.0 / 16.0, valid[:, 0:n], ALU.mult, ALU.mult),
                     r=["l1", "cf"], w=["la"])
            P.op("dve", lambda e, n=n: e.tensor_tensor_scan(bcum[:, 0:n], reset[:, 0:n], la[:, 0:n], 0.0, ALU.mult, ALU.add),
                 r=["la", "cf"], w=["bcum"])
            P.op("act", lambda e, n=n: e.activation(eb[:, 0:n], bcum[:, 0:n], AF.Exp), r=["bcum"], w=["eb"])
            P.op("act", lambda e, n=n: e.activation(enb[:, 0:n], bcum[:, 0:n], AF.Exp, scale=-1.0), r=["bcum"], w=["enb"])
            P.op("act", lambda e, n=n, nch=nch: e.activation(ebl[:, 0:nch], bcum[:, 63:n:64], AF.Exp), r=["bcum"], w=["ebl"])
            CUT = int(os.environ.get("K_CUT", "99"))
            if CUT <= 1:
                return
            P.op("dve", lambda e, n=n: e.tensor_tensor(qe[:, 0:n], qTt[:, 0:n], eb[:, 0:n], ALU.mult), r=["qTt", "eb"], w=["qe"])
            P.op("pool", lambda e, n=n: e.tensor_tensor(ke[:, 0:n], kTt[:, 0:n], enb[:, 0:n], ALU.mult), r=["kTt", "enb"], w=["ke"])
            P.op("pool", lambda e, n=n, nch=nch: e.tensor_tensor(
                kd[:, 0:n].rearrange("p (c s) -> p c s", s=64), ke[:, 0:n].rearrange("p (c s) -> p c s", s=64),
                ebl[:, 0:nch].unsqueeze(2).broadcast_to([128, nch, 64]), ALU.mult), r=["ke", "ebl"], w=["kd"])
            P.op("dve", lambda e, n=n, nch=nch: e.tensor_tensor(
                ke_bd[:, 0:nch, :, :], ke[:, 0:n].rearrange("p (c s) -> p c s", s=64).unsqueeze(2).broadcast_to([128, nch, 4, 64]),
                hm4.unsqueeze(1).unsqueeze(3).broadcast_to([128, nch, 4, 64]), ALU.mult), r=["ke", "cf"], w=["ke_bd"])
            if CUT <= 2:
                return
            for cg in range(nch // 4):
                ab = rr("attm")
                for cc in range(4):
                    c = cg * 4 + cc
                    for pair in range(2):
                        P.op("pe", lambda e, c=c, cc=cc, pair=pair: e.matmul(
                            pb[4][:, (pair * 4 + cc) * 64:(pair * 4 + cc + 1) * 64],
                            ke_bd[:, c, 2 * pair:2 * pair + 2, :], qe[:, c * 64:(c + 1) * 64], start=True, stop=True),
                            r=["ke_bd", "qe"], w=["pb4"])
                P.op("dve", lambda e, ab=ab: e.tensor_tensor(
                    attm[ab].rearrange("p a b c -> p (a b) c"), pb[4][:, 0:512].rearrange("p (x s) -> p x s", s=64),
                    tri.unsqueeze(1).broadcast_to([128, 8, 64]), ALU.mult), r=["pb4", "cf"], w=[("attm", ab)])
                for cc in range(4):
                    c = cg * 4 + cc
                    vb = cc
                    vbank = 3 if cc % 2 == 0 else 2
                    P.op("dve", lambda e, c=c, a=a, vb=vb: e.tensor_copy(
                        xdup[vb], xn[:, :, a + c * 64:a + (c + 1) * 64].unsqueeze(2).broadcast_to([128, 8, 2, 64])),
                        r=tk("xn", a, n), w=[("xdup", vb)])
                    for k in range(8):
                        P.op("pe", lambda e, k=k, vb=vb, vbank=vbank: e.matmul(
                            pb[vbank][:, 0:256], xdup[vb][:, k, :, :], wgla[:, k, 256:512], start=(k == 0), stop=(k == 7)),
                            r=["wgla", ("xdup", vb)], w=["pb%d" % vbank])
                    P.op("act", lambda e, vb=vb, vbank=vbank: e.copy(Vtm[vb], pb[vbank][:, 0:256]), r=["pb%d" % vbank], w=[("Vtm", vb)])
                    P.op("pool", lambda e, vb=vb: e.tensor_tensor(Vbd[vb], Vtm[vb], vmask, ALU.mult), r=[("Vtm", vb), "cf"], w=[("Vbd", vb)])
                    kt = pb[7][0:64, cc * 64:(cc + 1) * 64].bitcast(BF16)
                    P.op("pe", lambda e, c=c, kt=kt: e.transpose(kt, kd[:, c * 64:(c + 1) * 64], ident_bf), r=["kd", "cb"], w=["pb7"])
                P.op("act", lambda e: e.copy(kdtm_all[0:64, :, :], pb[7][0:64, 0:256].bitcast(BF16).rearrange("p (c k) -> p c k", c=4)), r=["pb7"], w=["kdtm"])
                for cc in range(4):
                    ib = 4 if cc < 2 else 7
                    io = (cc % 2) * 256
                    P.op("pe", lambda e, cc=cc, ib=ib, io=io: e.matmul(pb[ib][:, io:io + 256], kdtm_all[0:64, cc, :], Vtm[cc][0:64, :], start=True, stop=True),
                         r=["kdtm", ("Vtm", cc)], w=["pb%d" % ib])
                for half_ in range(2):
                    ib = 4 if half_ == 0 else 7
                    P.op("dve", lambda e, ib=ib, half_=half_: e.tensor_tensor(
                        t1x[:, 2 * half_:2 * half_ + 2, :], pb[ib][:, 0:512].rearrange("p (c x) -> p c x", c=2),
                        bdm.unsqueeze(1).broadcast_to([128, 2, 256]), ALU.mult), r=["pb%d" % ib, "cf"], w=[("t1x", half_)])
                if CUT <= 3:
                    return
                for cc in range(4):
                    c = cg * 4 + cc
                    vb = cc
                    gc = a // 64 + c
                    if gc >= 32:
                        j = gc - 32
                        cur["S"], cur["key"] = S_slot[j], ("S_slot", j)
                        P.op("act", lambda e, S=cur["S"]: e.copy(S_bf, S), r=[cur["key"]], w=["S_bf"])
                    S, Sk = cur["S"], cur["key"]
                    for pair in range(2):
                        if os.environ.get("K_NOO"):
                            break
                        P.op("pe", lambda e, vb=vb, ab=ab, pair=pair, cc=cc, c=c: e.matmul(
                            pb[5 + pair][:, c * 64:(c + 1) * 64], Vbd[vb][:, pair * 128:(pair + 1) * 128], attm[ab][:, pair, cc, :],
                            start=True, stop=False), r=[("Vbd", vb), ("attm", ab)], w=["pb%d" % (5 + pair)])
                        P.op("pe", lambda e, pair=pair, c=c: e.matmul(
                            pb[5 + pair][:, c * 64:(c + 1) * 64], S_bf[:, pair * 128:(pair + 1) * 128], qe[:, c * 64:(c + 1) * 64],
                            start=False, stop=True), r=["S_bf", "qe"], w=["pb%d" % (5 + pair)])
                    if CUT <= 4:
                        continue
                    P.op("dve", lambda e, S=S, c=c, cc=cc: e.scalar_tensor_tensor(S, S, ebl[:, c:c + 1], t1x[:, cc, :], ALU.mult, ALU.add),
                         r=[("t1x", cc // 2), "ebl", Sk], w=[Sk])
                    P.op("act", lambda e, S=S: e.copy(S_bf, S), r=[Sk], w=["S_bf"])
                    if gc >= 31:
                        slot = gc - 31
                        P.op("dve", lambda e, S=S: e.tensor_reduce(Sd, S.rearrange("p (h v) -> p v h", h=4), AX.X, ALU.add), r=[Sk], w=["Sd"])
                        P.dma("sp", o_gla[l, slot], Sd, r=["Sd"], w=["o_gla"])
            for pair in range(2):
                P.op("act", lambda e, pair=pair, n=n: e.copy(oT[:, pair, 0:n], pb[5 + pair][:, 0:n]), r=["pb%d" % (5 + pair)], w=[("oT", pair)])
                P.op("act", lambda e, pair=pair, n=n: e.activation(sqg[:, 0:n], oT[:, pair, 0:n], AF.Square), r=[("oT", pair)], w=["sqg"])
                P.op("pe", lambda e, n=n: e.matmul(pb[2][:, 0:n], blk64_bf, sqg[:, 0:n], start=True, stop=True), r=["sqg", "cb"], w=["pb2"])
                P.op("act", lambda e, n=n: e.activation(sdg[:, 0:n], pb[2][:, 0:n], AF.Sqrt, bias=EPS, scale=1.0 / 64.0), r=["pb2"], w=["sdg"])
                P.op("dve", lambda e, pair=pair, n=n: e.reciprocal(rsg[pair][:, 0:n], sdg[:, 0:n]), r=["sdg"], w=[("rsg", pair)])
            for pair in range(2):
                bk = rr("apj")
                proj(wgla, 528 + pair * 128, 128, a, n, bk, "pb%d" % bk, "wgla")
                P.op("act", lambda e, bk=bk, pair=pair, n=n: e.activation(sgl[:, pair, 0:n], pb[bk][:, 0:n], AF.Silu), r=["pb%d" % bk], w=[("sgl", pair)])
                P.op("dve", lambda e, pair=pair, n=n: e.scalar_tensor_tensor(
                    t2[:, 0:n], oT[:, pair, 0:n], vcol(l, V_GLAN + pair), rsg[pair][:, 0:n], ALU.mult, ALU.mult),
                    r=[("oT", pair), ("rsg", pair), "vecs"], w=["t2"])
                P.op("pool", lambda e, pair=pair, a=a, n=n: e.tensor_tensor(mixT[:, pair, a:a + n], t2[:, 0:n], sgl[:, pair, 0:n], ALU.mult),
                     r=["t2", ("sgl", pair)], w=tk("mix", a, n))

    def gmlp(l):
        cv = Carve(arena, MIXER_REGIONS)
        wgm = cv.take([8, 512], BF16)
        P.dma("pool", wgm, w_in[l].rearrange("(k p) c -> p k c", p=128)[:, :, C_GMLP:C_GMLP + 512], w=["wgm"])
        wsT_f = cv.take([4, 128], F32)
        P.dma("sp", wsT_f, wsT_in[l].rearrange("g s t -> s g t"), w=["wsT_f"])
        wsS_f = cv.take([4, 128], F32)
        P.op("pool", lambda e: e.memset(wsS_f, 0.0), w=["wsS_f"])
        for g in range(4):
            P.dma("sp", wsS_f[0:4, g, 0:4], wsT_in[l, g, 0:4, 0:4], w=["wsS_f"])
            P.dma("sp", wsS_f[64:68, g, 64:68], wsT_in[l, g, 0:4, 0:4], w=["wsS_f"])
        wsTm = cv.take([4, 128], BF16)
        wsS = cv.take([4, 128], BF16)
        tri128 = CFv("tri128")
        P.op("dve", lambda e: e.tensor_tensor(wsTm, wsT_f, tri128.unsqueeze(1).broadcast_to([128, 4, 128]), ALU.mult), r=["wsT_f", "cf"], w=["wsTm"])
        P.op("dve", lambda e: e.tensor_tensor(wsS, wsS_f, tri128.unsqueeze(1).broadcast_to([128, 4, 128]), ALU.mult), r=["wsS_f", "cf"], w=["wsS"])
        bsT = cv.take([2, 128], F32)
        bsS = cv.take([2, 128], F32)
        for g in range(4):
            P.dma("sp", bsT[(g % 2) * 64:(g % 2) * 64 + 64, g // 2, :], bs_in[l, g:g + 1, :].partition_broadcast(64), w=["bsT"])
        P.op("pool", lambda e: e.tensor_copy(bsS, bsT), r=["bsT"], w=["bsS"])
        P.op("pool", lambda e: e.tensor_copy(bsS[:, :, 64:68], bsT[:, :, 0:4]), r=["bsT", "bsS"], w=["bsS"])
        lng = cv.take([256], F32)
        lnb = cv.take([256], F32)
        P.dma("sp", lng, lnrow_in[l, 0:1, :].partition_broadcast(128), w=["lng"])
        P.dma("sp", lnb, lnrow_in[l, 1:2, :].partition_broadcast(128), w=["lnb"])
        vg4 = [cv.take([256], F32) for i in range(4)]
        mvAll = cv.take([4, 2], F32)
        sdvAll = cv.take([4], F32)
        rsvAll = cv.take([4], F32)
        st6 = cv.take([6], F32)
        mv = cv.take([2], F32)
        sdv = cv.take([2], F32)
        rsv = cv.take([2], F32)
        vn0 = cv.take([256], F32)
        vn1 = cv.take([256], F32)
        vnf = [cv.take([256], F32) for i in range(2)]
        vnb = [cv.take([256], BF16) for i in range(2)]
        uT = cv.take([2, 512], F32)
        ts = cv.take([512], F32)
        for ti, (a, n) in enumerate(TILES):
            nb = n // 128
            bst = bsT if a < T else bsS
            for pair in range(2):
                proj(wgm, pair * 128, 128, a, n, 1, "pb1", "wgm")
                P.op("act", lambda e, pair=pair, n=n: e.activation(uT[:, pair, 0:n], pb[1][:, 0:n], AF.Gelu), r=["pb1"], w=[("uT", pair)])

            for bi in range(nb):
                blk = a // 128 + bi
                vbank = 0 if bi % 2 == 0 else 4
                for k in range(8):
                    P.op("pe", lambda e, k=k, blk=blk, vbank=vbank: e.matmul(pb[vbank][:, 0:256], xn[:, k, blk * 128:(blk + 1) * 128], wgm[:, k, 256:512],
                                                                            start=(k == 0), stop=(k == 7)), r=["wgm"] + tk("xn", a, n), w=["pb%d" % vbank])
                P.op("act", lambda e, bi=bi, vbank=vbank: e.activation(vg4[bi], pb[vbank][:, 0:256], AF.Gelu), r=["pb%d" % vbank], w=[("vg", bi)])
                P.op("dve", lambda e, bi=bi: e.bn_stats(st6, vg4[bi]), r=[("vg", bi)], w=["st6"])
                P.op("dve", lambda e, bi=bi: e.bn_aggr(mvAll[:, bi, :], st6), r=["st6"], w=["mvAll"])
            P.op("act", lambda e, nb=nb: e.activation(sdvAll[:, 0:nb], mvAll[:, 0:nb, 1], AF.Sqrt, bias=1e-5), r=["mvAll"], w=["sdvAll"])
            P.op("dve", lambda e, nb=nb: e.reciprocal(rsvAll[:, 0:nb], sdvAll[:, 0:nb]), r=["sdvAll"], w=["rsvAll"])
            for bi in range(nb):
                blk = a // 128 + bi
                b = bi % 2
                P.op("dve", lambda e, bi=bi: e.tensor_scalar(vn0, vg4[bi], mvAll[:, bi, 0:1], rsvAll[:, bi:bi + 1], ALU.subtract, ALU.mult),
                     r=[("vg", bi), "mvAll", "rsvAll"], w=["vn0"])
                P.op("pool", lambda e: e.tensor_tensor(vn1, vn0, lng, ALU.mult), r=["vn0", "lng"], w=["vn1"])
                P.op("pool", lambda e, b=b: e.tensor_tensor(vnf[b], vn1, lnb, ALU.add), r=["vn1", "lnb"], w=[("vnf", b)])
                P.op("act", lambda e, b=b: e.copy(vnb[b], vnf[b]), r=[("vnf", b)], w=[("vnb", b)])
                if blk >= 16:
                    for jj in range(2):
                        j = (blk - 16) * 2 + jj
                        P.dma("sp", o_gv[l, j], vnf[b][jj * 64:jj * 64 + 4, :], r=[("vnf", b)], w=["o_gv"])
                wmat = wsTm if blk < 16 else wsS
                for g in range(4):
                    P.op("pe", lambda e, g=g, b=b, bi=bi, wmat=wmat: e.matmul(
                        pb[2 + g // 2][(g % 2) * 64:(g % 2) * 64 + 64, bi * 128:(bi + 1) * 128], vnb[b][:, g * 64:(g + 1) * 64], wmat[:, g, :],
                        start=True, stop=True), r=[("vnb", b), "wsTm", "wsS"], w=["pb%d" % (2 + g // 2)])
            for pair in range(2):
                P.op("dve", lambda e, pair=pair, n=n, nb=nb, bst=bst: e.tensor_tensor(
                    ts[:, 0:n].rearrange("p (b t) -> p b t", t=128), pb[2 + pair][:, 0:n].rearrange("p (b t) -> p b t", t=128),
                    bst[:, pair, :].unsqueeze(1).broadcast_to([128, nb, 128]), ALU.add), r=["pb%d" % (2 + pair), "bsT", "bsS"], w=["ts"])
                P.op("pool", lambda e, pair=pair, a=a, n=n: e.tensor_tensor(mixT[:, 2 + pair, a:a + n], uT[:, pair, 0:n], ts[:, 0:n], ALU.mult),
                     r=["ts", ("uT", pair)], w=tk("mix", a, n))

    def rwkv(l):
        cv = Carve(arena, MIXER_REGIONS)
        NT_ = 256
        wrw = cv.take([8, 1024], BF16)
        P.dma("pool", wrw, w_in[l].rearrange("(k p) c -> p k c", p=128)[:, :, C_RWKV:C_RWKV + 1024], w=["wrw"])
        w2b = cv.take([256], BF16)
        a2b = cv.take([256], BF16)
        g2b = cv.take([256], BF16)
        P.dma("pool", w2b[0:64, :], w2_in[l], w=["w2b"])
        P.dma("pool", a2b[64:128, :], a2_in[l], w=["a2b"])
        P.dma("pool", g2b, g2_in[l], w=["g2b"])
        sst = cv.take([8, 4], F32)
        for j in range(NSEQ):
            P.dma("sp", sst[:, :, j], sshift_in[l, j].rearrange("(c p) -> p c", p=128), w=["sst"], allow_slow_non_contiguous=True)
        hm2 = CFv("hm2")
        m_sbd = CFv("m_strict_bd", [2, 64])
        m_sTbd = CFv("m_strictT_bd", [2, 64])
        tri = CFv("tri_incl")
        valid = CFv("valid")
        reset = CFv("reset")
        H_p = [cv.take([128], F32) for p_ in range(2)]
        H_slot = [[cv.take([128], F32) for p_ in range(2)] for j in range(NSEQ)]
        H_bf = [cv.take([128], BF16) for p_ in range(2)]
        snat = cv.take([64], F32)
        sbd = cv.take([2, 64], F32)
        Sx = [cv.take([64], F32) for p_ in range(2)]
        for p_ in range(2):
            P.op("pool", lambda e, p_=p_: e.memset(H_p[p_], 0.0), w=[("H_p", p_)])
            P.op("act", lambda e, p_=p_: e.copy(H_bf[p_], H_p[p_]), r=[("H_p", p_)], w=[("H_bf", p_)])
        for j in range(NSEQ):
            for p_ in range(2):
                P.dma("sp", snat, srwkv_in[l, j, p_ * 128:(p_ + 1) * 128, :], w=["snat"])
                P.op("dve", lambda e: e.tensor_tensor(sbd, snat.unsqueeze(1).broadcast_to([128, 2, 64]),
                                                      hm2.unsqueeze(2).broadcast_to([128, 2, 64]), ALU.mult), r=["snat", "cf"], w=["sbd"])
                P.op("pe", lambda e: e.transpose(pb[7][:, 0:128], sbd.rearrange("p a b -> p (a b)"), ident_f), r=["sbd", "cf"], w=["pb7"])
                P.op("act", lambda e, j=j, p_=p_: e.copy(H_slot[j][p_], pb[7][:, 0:128]), r=["pb7"], w=[("H_slot", j, p_)])
        cext = cv.take([8, NT_ + 1], F32)
        xm = cv.take([8, NT_], F32)
        tw = cv.take([NT_], BF16)
        albf = cv.take([NT_], BF16)
        sgg = cv.take([NT_], BF16)
        gate = [cv.take([NT_], F32) for p_ in range(2)]
        k2 = [cv.take([NT_], F32) for p_ in range(2)]
        AR = [cv.take([4, 2, 64], BF16) for p_ in range(2)]
        btb = [cv.take([NT_], BF16) for p_ in range(2)]
        kbd = [cv.take([4, 2, 64], BF16) for p_ in range(2)]
        bbd = [cv.take([4, 2, 64], BF16) for p_ in range(2)]
        abd = [cv.take([4, 2, 64], BF16) for p_ in range(2)]
        vbd = [cv.take([4, 2, 64], BF16) for p_ in range(2)]
        Bhbd = [cv.take([4, 2, 64], BF16) for p_ in range(2)]
        Khbd = [cv.take([4, 2, 64], BF16) for p_ in range(2)]
        gC = [cv.take([4], F32) for p_ in range(2)]
        tmpf = [{nm: cv.take([NT_], F32) for nm in ("sig", "ar", "kk0", "bv", "t", "gcum", "Ep", "Em", "Epr", "kt", "Bh", "Kh", "nrm")} for p_ in range(2)]
        sqk = [cv.take([NT_], BF16) for p_ in range(2)]
        NU = 8
        XTb = [cv.take([4, 128], F32) for b in range(NU // 4)]
        XT = [XTb[u // 4][:, u % 4, :] for u in range(NU)]
        Qbb = [[cv.take([4, 128], F32) for i in range(2)] for b in range(2)]
        QTbb = [[cv.take([4, 128], F32) for i in range(2)] for b in range(2)]
        Aak = [cv.take([2, 64], BF16) for u in range(NU)]
        Ar = [cv.take([2, 64], BF16) for u in range(NU)]
        TM = [cv.take([3, 128], BF16) for u in range(NU)]
        Vtm = [TM[u][:, 0, :] for u in range(NU)]
        Bhtm = [TM[u][:, 1, :] for u in range(NU)]
        Khtm = [TM[u][:, 2, :] for u in range(NU)]
        Wsb = [cv.take([128], F32) for p_ in range(2)]
        Ubf = [cv.take([128], BF16) for p_ in range(2)]
        yT = [cv.take([NT_], F32) for p_ in range(2)]
        C1 = -float(np.exp(-0.5))

        rtiles = [(a, NT_) for a in range(0, TT, NT_)]
        cur = {"H": H_p, "key": "H_p", "j": None}
        for ti, (a, n) in enumerate(rtiles):
            nch = n // 64
            smp = a >= T
            if ti == 0:
                P.op("pool", lambda e: e.memset(cext[:, :, 0:1], 0.0), w=["cext0"])
            else:
                P.op("pool", lambda e, n=n: e.tensor_copy(cext[:, :, 0:1], cext[:, :, n:n + 1]), r=["cext"], w=["cext0"])
            for cc in range(8):
                bk = rr("apj")
                proj(wrw, cc * 128, 128, a, n, bk, "pb%d" % bk, "wrw")
                if cc % 2 == 0:
                    P.op("act", lambda e, cc=cc, bk=bk, n=n: e.copy(cext[:, cc, 1:n + 1], pb[bk][:, 0:n]), r=["pb%d" % bk, "cext0"], w=["cext"])
                else:
                    P.op("dve", lambda e, cc=cc, bk=bk, n=n: e.tensor_copy(cext[:, cc, 1:n + 1], pb[bk][:, 0:n]), r=["pb%d" % bk, "cext0"], w=["cext"])
            if a + n == T:
                P.dma("sp", o_shift[l, 0], cext[:, :, n], r=["cext"], w=["o_shift"], allow_slow_non_contiguous=True)
            if smp:
                for j in range(NSEQ):
                    P.dma("sp", o_shift[l, 1 + j], cext[:, :, 1 + 64 * j + 3], r=["cext"], w=["o_shift"], allow_slow_non_contiguous=True)
            P.op("pool", lambda e, n=n: e.tensor_tensor(xm[:, 0:4, 0:n], cext[:, 0:4, 0:n], cext[:, 0:4, 1:n + 1], ALU.subtract),
                 r=["cext", "cext0"], w=["xm"])
            P.op("dve", lambda e, n=n: e.tensor_tensor(xm[:, 4:8, 0:n], cext[:, 4:8, 0:n], cext[:, 4:8, 1:n + 1], ALU.subtract),
                 r=["cext", "cext0"], w=["xm"])
            if smp:
                P.op("dve", lambda e, n=n: e.tensor_tensor(xm[:, :, 0:n:64], sst, cext[:, :, 1:n + 1:64], ALU.subtract),
                     r=["cext", "sst", "xm"], w=["xm"])
            for cc in range(8):
                P.op("dve", lambda e, cc=cc, n=n: e.scalar_tensor_tensor(
                    xm[:, cc, 0:n], xm[:, cc, 0:n], vcol(l, V_MU + cc), cext[:, cc, 1:n + 1], ALU.mult, ALU.add),
                    r=["xm", "cext", "vecs"], w=["xm"])
            if smp:
                P.op("pool", lambda e, n=n: e.tensor_tensor(xm[:, :, 0:n], xm[:, :, 0:n], valid[:, 0:n].unsqueeze(1).broadcast_to([128, 8, n]), ALU.mult),
                     r=["xm", "cf"], w=["xm"])
            RCUT = int(os.environ.get("K_RCUT", "99"))
            if RCUT <= 1:
                return
            P.op("act", lambda e, n=n: e.activation(tw[0:64, 0:n], xm[0:64, 6, 0:n], AF.Tanh), r=["xm"], w=["tw"])
            P.op("act", lambda e, n=n: e.copy(albf[64:128, 0:n], xm[64:128, 6, 0:n]), r=["xm"], w=["albf"])
            P.op("act", lambda e, n=n: e.activation(sgg[:, 0:n], xm[:, 7, 0:n], AF.Sigmoid), r=["xm"], w=["sgg"])
            def stageB(p_, n=n, nch=nch, smp=smp):
                PB2, PB3 = (2, 3) if p_ == 0 else (6, 7)
                tf = tmpf[p_]
                r_ = xm[:, p_, 0:n]
                k_ = xm[:, 2 + p_, 0:n]
                v_ = xm[:, 4 + p_, 0:n]
                P.op("pe", lambda e, p_=p_, n=n: e.matmul(pb[PB2][:, 0:n], w2b[0:64, p_ * 128:(p_ + 1) * 128], tw[0:64, 0:n], start=True, stop=True),
                     r=["tw", "w2b"], w=["pb%d" % PB2])
                P.op("act", lambda e, p_=p_, n=n: e.activation(tf["sig"][:, 0:n], pb[PB2][:, 0:n], AF.Sigmoid, bias=vcol(l, V_W0 + p_)),
                     r=["pb%d" % PB2, "vecs"], w=[("sig", p_)])
                if smp:
                    P.op("dve", lambda e, n=n: e.scalar_tensor_tensor(tf["sig"][:, 0:n], tf["sig"][:, 0:n], C1, valid[:, 0:n], ALU.mult, ALU.mult),
                         r=[("sig", p_), "cf"], w=[("sig", p_)])
                else:
                    P.op("dve", lambda e, n=n: e.tensor_scalar(tf["sig"][:, 0:n], tf["sig"][:, 0:n], C1, None, ALU.mult), r=[("sig", p_)], w=[("sig", p_)])
                P.op("pe", lambda e, p_=p_, n=n: e.matmul(pb[PB3][:, 0:n], a2b[64:128, p_ * 128:(p_ + 1) * 128], albf[64:128, 0:n], start=True, stop=True),
                     r=["albf", "a2b"], w=["pb%d" % PB3])
                P.op("act", lambda e, p_=p_, n=n: e.activation(tf["ar"][:, 0:n], pb[PB3][:, 0:n], AF.Sigmoid, bias=vcol(l, V_A0 + p_)),
                     r=["pb%d" % PB3, "vecs"], w=[("ar", p_)])
                P.op("pe", lambda e, p_=p_, n=n: e.matmul(pb[PB2][:, 0:n], g2b[:, p_ * 128:(p_ + 1) * 128], sgg[:, 0:n], start=True, stop=True),
                     r=["sgg", "g2b"], w=["pb%d" % PB2])
                P.op("act", lambda e, p_=p_, n=n: e.copy(gate[p_][:, 0:n], pb[PB2][:, 0:n]), r=["pb%d" % PB2], w=[("gate", p_)])
                yield
                P.op("dve", lambda e, p_=p_, n=n, k_=k_: e.tensor_scalar(tf["kk0"][:, 0:n], k_, vcol(l, V_KK + p_), None, ALU.mult),
                     r=["xm", "vecs"], w=[("kk0", p_)])
                P.op("act", lambda e, n=n: e.activation(sqk[p_][:, 0:n], tf["kk0"][:, 0:n], AF.Square), r=[("kk0", p_)], w=[("sqk", p_)])
                yield
                P.op("pe", lambda e, n=n: e.matmul(pb[PB3][:, 0:n], blk64_bf, sqk[p_][:, 0:n], start=True, stop=True), r=[("sqk", p_), "cb"], w=["pb%d" % PB3])
                yield
                P.op("act", lambda e, n=n: e.activation(tf["nrm"][:, 0:n], pb[PB3][:, 0:n], AF.Sqrt), r=["pb%d" % PB3], w=[("nrm", p_)])
                yield
                P.op("dve", lambda e, n=n: e.tensor_scalar(tf["nrm"][:, 0:n], tf["nrm"][:, 0:n], 1e-12, None, ALU.max), r=[("nrm", p_)], w=[("nrm", p_)])
                yield
                P.op("dve", lambda e, n=n: e.reciprocal(tf["t"][:, 0:n], tf["nrm"][:, 0:n]), r=[("nrm", p_)], w=[("t", p_)])
                yield
                P.op("dve", lambda e, n=n: e.tensor_tensor(tf["kk0"][:, 0:n], tf["kk0"][:, 0:n], tf["t"][:, 0:n], ALU.mult), r=[("kk0", p_), ("t", p_)], w=[("kk0", p_)])
                yield
                P.op("pool", lambda e, n=n: e.tensor_tensor(tf["bv"][:, 0:n], tf["kk0"][:, 0:n], tf["ar"][:, 0:n], ALU.mult), r=[("kk0", p_), ("ar", p_)], w=[("bv", p_)])
                yield
                P.op("dve", lambda e, p_=p_, n=n: e.tensor_scalar(tf["t"][:, 0:n], tf["ar"][:, 0:n], vcol(l, V_KA + p_), vcol(l, V_KA + p_), ALU.mult, ALU.subtract),
                     r=[("ar", p_), "vecs", ("kk0", p_)], w=[("t", p_)])
                P.op("dve", lambda e, p_=p_, n=n, k_=k_: e.scalar_tensor_tensor(k2[p_][:, 0:n], tf["t"][:, 0:n], 1.0, k_, ALU.add, ALU.mult),
                     r=[("t", p_), "xm"], w=[("k2", p_)])
                P.op("dve", lambda e, n=n: e.tensor_tensor_scan(tf["gcum"][:, 0:n], reset[:, 0:n], tf["sig"][:, 0:n], 0.0, ALU.mult, ALU.add),
                     r=[("sig", p_), "cf"], w=[("gcum", p_)])
                P.op("pool", lambda e, n=n: e.tensor_tensor(tf["nrm"][:, 0:n], tf["gcum"][:, 0:n], tf["sig"][:, 0:n], ALU.subtract), r=[("gcum", p_), ("sig", p_)], w=[("nrm", p_)])
                yield
                P.op("act", lambda e, n=n: e.activation(tf["Ep"][:, 0:n], tf["gcum"][:, 0:n], AF.Exp), r=[("gcum", p_)], w=[("Ep", p_)])
                yield
                P.op("act", lambda e, n=n: e.activation(tf["Em"][:, 0:n], tf["gcum"][:, 0:n], AF.Exp, scale=-1.0), r=[("gcum", p_)], w=[("Em", p_)])
                yield
                P.op("act", lambda e, n=n: e.activation(tf["Epr"][:, 0:n], tf["nrm"][:, 0:n], AF.Exp), r=[("nrm", p_)], w=[("Epr", p_)])
                yield
                P.op("act", lambda e, p_=p_, n=n, nch=nch: e.copy(gC[p_][:, 0:nch], tf["Ep"][:, 63:n:64]), r=[("Ep", p_)], w=[("gC", p_)])
                yield
                c3 = lambda ap: ap.rearrange("p (c s) -> p c s", s=64)
                P.op("dve", lambda e, p_=p_, n=n, nch=nch: e.scalar_tensor_tensor(
                    AR[p_][:, 0:nch, 0, :], c3(tf["kk0"][:, 0:n]), -1.0, c3(tf["Epr"][:, 0:n]), ALU.mult, ALU.mult),
                    r=[("kk0", p_), ("Epr", p_)], w=[("AR", p_)])
                P.op("pool", lambda e, p_=p_, n=n, nch=nch, r_=r_: e.tensor_tensor(AR[p_][:, 0:nch, 1, :], c3(r_), c3(tf["Ep"][:, 0:n]), ALU.mult),
                     r=["xm", ("Ep", p_)], w=[("AR", p_)])
                P.op("dve", lambda e, n=n: e.tensor_tensor(tf["bv"][:, 0:n], tf["bv"][:, 0:n], tf["Em"][:, 0:n], ALU.mult), r=[("bv", p_), ("Em", p_)], w=[("bv", p_)])
                yield
                P.op("pool", lambda e, p_=p_, n=n: e.tensor_tensor(tf["kt"][:, 0:n], k2[p_][:, 0:n], tf["Em"][:, 0:n], ALU.mult), r=[("k2", p_), ("Em", p_)], w=[("kt", p_)])
                yield
                P.op("act", lambda e, p_=p_, n=n: e.copy(btb[p_][:, 0:n], tf["bv"][:, 0:n]), r=[("bv", p_)], w=[("btb", p_)])
                yield
                gcb = lambda p_, nch: gC[p_][:, 0:nch].unsqueeze(2).broadcast_to([128, nch, 64])
                P.op("dve", lambda e, p_=p_, n=n, nch=nch: e.tensor_tensor(c3(tf["Bh"][:, 0:n]), c3(tf["bv"][:, 0:n]), gcb(p_, nch), ALU.mult),
                     r=[("bv", p_), ("gC", p_)], w=[("Bh", p_)])
                P.op("pool", lambda e, p_=p_, n=n, nch=nch: e.tensor_tensor(c3(tf["Kh"][:, 0:n]), c3(tf["kt"][:, 0:n]), gcb(p_, nch), ALU.mult),
                     r=[("kt", p_), ("gC", p_)], w=[("Kh", p_)])

                def expand(dst, src3, eng, rk_, wk_, nch=nch):
                    if eng == "act":
                        for hh_ in range(2):
                            P.op("act", lambda e, dst=dst, src3=src3, nch=nch, hh_=hh_: e.activation(
                                dst[:, 0:nch, hh_, :], src3, AF.Copy, scale=hm2[:, hh_:hh_ + 1]), r=rk_ + ["cf"], w=[wk_])
                        return
                    P.op(eng, lambda e, dst=dst, src3=src3, nch=nch: e.tensor_tensor(
                        dst[:, 0:nch, :, :], src3.unsqueeze(2).broadcast_to([128, nch, 2, 64]),
                        hm2.unsqueeze(1).unsqueeze(3).broadcast_to([128, nch, 2, 64]), ALU.mult), r=rk_ + ["cf"], w=[wk_])
                expand(kbd[p_], c3(tf["kt"][:, 0:n]), "act", [("kt", p_)], ("kbd", p_))
                yield
                expand(bbd[p_], c3(tf["bv"][:, 0:n]), "dve", [("bv", p_)], ("bbd", p_))
                yield
                expand(abd[p_], AR[p_][:, 0:nch, 0, :], "act", [("AR", p_)], ("abd", p_))
                yield
                expand(vbd[p_], c3(v_), "dve", ["xm"], ("vbd", p_))
                yield
                expand(Bhbd[p_], c3(tf["Bh"][:, 0:n]), "act", [("Bh", p_)], ("Bhbd", p_))
                yield
                expand(Khbd[p_], c3(tf["Kh"][:, 0:n]), "dve", [("Kh", p_)], ("Khbd", p_))
                yield
            gens = [stageB(0), stageB(1)]
            alive = [True, True]
            while any(alive):
                for gi_ in range(2):
                    if alive[gi_]:
                        try:
                            next(gens[gi_])
                        except StopIteration:
                            alive[gi_] = False
            if RCUT <= 2:
                return
            units = [(c, p_) for c in range(nch) for p_ in range(2)]
            nbat = (len(units) + 3) // 4
            QNB, QTNB, XB = (6, 2), (7, 3), (0, 1)
            for u, (c, p_) in enumerate(units):
                bi_, ui = u // 4, u % 4
                pa = pb[4 + u % 2]
                pak = "pb%d" % (4 + u % 2)
                arc = AR[p_][:, c, :, :].rearrange("p a b -> p (a b)")
                P.op("pe", lambda e, pa=pa, c=c, p_=p_, arc=arc: e.matmul(pa[:, 0:128], kbd[p_][:, c, :, :].rearrange("p a b -> p (a b)"), arc, start=True, stop=True),
                     r=[("kbd", p_), ("AR", p_)], w=[pak])
                P.op("pe", lambda e, pa=pa, c=c, p_=p_, arc=arc: e.matmul(pa[:, 128:256], bbd[p_][:, c, :, :].rearrange("p a b -> p (a b)"), arc, start=True, stop=True),
                     r=[("bbd", p_), ("AR", p_)], w=[pak])
                P.op("pe", lambda e, pa=pa, c=c, p_=p_: e.matmul(pa[:, 256:320], abd[p_][:, c, :, :].rearrange("p a b -> p (a b)"), btb[p_][:, c * 64:(c + 1) * 64], start=True, stop=True),
                     r=[("abd", p_), ("btb", p_)], w=[pak])
                P.op("dve", lambda e, pa=pa, u=u: e.tensor_tensor(Aak[u], pa[:, 0:64].unsqueeze(1).broadcast_to([128, 2, 64]), m_sbd, ALU.mult),
                     r=[pak, "cf"], w=[("Aak", u)])
                P.op("dve", lambda e, pa=pa, u=u: e.tensor_tensor(Ar[u], pa[:, 64:256].rearrange("p (a b) -> p a b", b=64)[:, 0:3:2, :], tri.unsqueeze(1).broadcast_to([128, 2, 64]), ALU.mult),
                     r=[pak, "cf"], w=[("Ar", u)])
                P.op("dve", lambda e, pa=pa, bi_=bi_, ui=ui: e.tensor_tensor(QTbb[bi_][0][:, ui, :].rearrange("p (a b) -> p a b", a=2), pa[:, 128:192].unsqueeze(1).broadcast_to([128, 2, 64]), m_sbd, ALU.mult),
                     r=[pak, "cf"], w=[("QT", bi_, 0)])
                P.op("dve", lambda e, pa=pa, bi_=bi_, ui=ui: e.tensor_tensor(Qbb[bi_][0][:, ui, :].rearrange("p (a b) -> p a b", a=2), pa[:, 256:320].unsqueeze(1).broadcast_to([128, 2, 64]), m_sTbd, ALU.mult),
                     r=[pak, "cf"], w=[("Q", bi_, 0)])
                P.op("pool", lambda e, u=u, bi_=bi_, ui=ui: e.tensor_tensor(XT[u], QTbb[bi_][0][:, ui, :], ident_f, ALU.add), r=[("QT", bi_, 0), "cf"], w=[("XTb", bi_)])
            def inv_sq(jstep):
                src = (jstep - 1) % 2
                for bi_ in range(nbat):
                    nb_ = min(4, len(units) - 4 * bi_)
                    qn, qtn = QNB[bi_], QTNB[bi_]
                    for ui in range(nb_):
                        P.op("pe", lambda e, ui=ui, src=src, bi_=bi_, qn=qn: e.matmul(pb[qn][:, ui * 128:(ui + 1) * 128], QTbb[bi_][src][:, ui, :], Qbb[bi_][src][:, ui, :], start=True, stop=True),
                             r=[("QT", bi_, src), ("Q", bi_, src)], w=["pb%d" % qn])
                    if jstep < 5:
                        for ui in range(nb_):
                            P.op("pe", lambda e, ui=ui, src=src, bi_=bi_, qtn=qtn: e.matmul(pb[qtn][:, ui * 128:(ui + 1) * 128], Qbb[bi_][src][:, ui, :], QTbb[bi_][src][:, ui, :], start=True, stop=True),
                                 r=[("QT", bi_, src), ("Q", bi_, src)], w=["pb%d" % qtn])

            def inv_ev(jstep):
                dst = jstep % 2
                for bi_ in range(nbat):
                    nb_ = min(4, len(units) - 4 * bi_)
                    qn, qtn = QNB[bi_], QTNB[bi_]
                    P.op("act", lambda e, dst=dst, nb_=nb_, bi_=bi_, qn=qn: e.copy(Qbb[bi_][dst][:, 0:nb_, :], pb[qn][:, 0:nb_ * 128].rearrange("p (u c) -> p u c", c=128)),
                         r=["pb%d" % qn], w=[("Q", bi_, dst)])
                    if jstep < 5:
                        P.op("dve", lambda e, dst=dst, nb_=nb_, bi_=bi_, qtn=qtn: e.tensor_copy(QTbb[bi_][dst][:, 0:nb_, :], pb[qtn][:, 0:nb_ * 128].rearrange("p (u c) -> p u c", c=128)),
                             r=["pb%d" % qtn], w=[("QT", bi_, dst)])

            def inv_x(jstep):
                dst = jstep % 2
                for bi_ in range(nbat):
                    nb_ = min(4, len(units) - 4 * bi_)
                    xb = XB[bi_]
                    for ui in range(nb_):
                        u = 4 * bi_ + ui
                        P.op("pe", lambda e, ui=ui, u=u, dst=dst, bi_=bi_, xb=xb: e.matmul(pb[xb][:, ui * 128:(ui + 1) * 128], Qbb[bi_][dst][:, ui, :], XT[u], start=True, stop=True),
                             r=[("Q", bi_, dst), ("XTb", bi_)], w=["pb%d" % xb])

            def inv_add(jstep):
                for bi_ in range(nbat):
                    nb_ = min(4, len(units) - 4 * bi_)
                    xb = XB[bi_]
                    P.op("dve", lambda e, bi_=bi_, nb_=nb_, xb=xb: e.tensor_tensor(XTb[bi_][:, 0:nb_, :], XTb[bi_][:, 0:nb_, :],
                                                                               pb[xb][:, 0:nb_ * 128].rearrange("p (u c) -> p u c", c=128), ALU.add),
                         r=["pb%d" % xb, ("XTb", bi_)], w=[("XTb", bi_)])

            inv_sq(1)
            inv_ev(1)
            for jstep in range(1, 6):
                if jstep < 5:
                    inv_sq(jstep + 1)
                inv_x(jstep)
                if jstep < 5:
                    inv_ev(jstep + 1)
                inv_add(jstep)
            if RCUT <= 3:
                return
            for u, (c, p_) in enumerate(units):
                bk = 6 + u % 2
                for which, srcb in enumerate((vbd, Bhbd, Khbd)):
                    pst = pb[bk][:, which * 64:(which + 1) * 64].bitcast(BF16)
                    P.op("pe", lambda e, pst=pst, srcb=srcb, c=c, p_=p_: e.transpose(pst, srcb[p_][:, c, :, :].rearrange("p a b -> p (a b)"), ident_bf),
                         r=[(("vbd", "Bhbd", "Khbd")[which], p_), "cb"], w=["pb%d" % bk])
                psl = pb[bk][:, 0:192].bitcast(BF16).rearrange("p (w c) -> p w c", w=3)
                if u % 2 == 0:
                    P.op("act", lambda e, psl=psl, u=u: e.copy(TM[u], psl), r=["pb%d" % bk], w=[("TM", u)])
                else:
                    P.op("dve", lambda e, psl=psl, u=u: e.tensor_copy(TM[u], psl), r=["pb%d" % bk], w=[("TM", u)])
            if RCUT <= 4:
                return
            for c in range(nch):
                gc = a // 64 + c
                if gc >= 32:
                    j = gc - 32
                    cur["H"], cur["key"], cur["j"] = H_slot[j], "H_slot", j
                    for p_ in range(2):
                        P.op("act", lambda e, p_=p_, j=j: e.copy(H_bf[p_], H_slot[j][p_]), r=[("H_slot", j, p_)], w=[("H_bf", p_)])
                WB, UB, HB = (1, 0), (2, 6), (3, 7)
                us = [c * 2 + p_ for p_ in range(2)]
                Hfs = [cur["H"][p_] for p_ in range(2)]
                Hks = [("H_p", p_) if cur["j"] is None else ("H_slot", cur["j"], p_) for p_ in range(2)]
                for p_ in range(2):
                    u = us[p_]
                    P.op("pe", lambda e, c=c, p_=p_: e.matmul(pb[WB[p_]][:, 0:128], abd[p_][:, c, :, :].rearrange("p a b -> p (a b)"), H_bf[p_], start=True, stop=False),
                         r=[("abd", p_), ("H_bf", p_)], w=["pb%d" % WB[p_]])
                    P.op("pe", lambda e, u=u, p_=p_: e.matmul(pb[WB[p_]][:, 0:128], Aak[u].rearrange("p a b -> p (a b)"), Vtm[u], start=False, stop=True),
                         r=[("Aak", u), ("TM", u)], w=["pb%d" % WB[p_]])
                for p_ in range(2):
                    if p_ == 0:
                        P.op("act", lambda e, p_=p_: e.copy(Wsb[p_], pb[WB[p_]][:, 0:128]), r=["pb%d" % WB[p_]], w=[("Wsb", p_)])
                    else:
                        P.op("dve", lambda e, p_=p_: e.tensor_copy(Wsb[p_], pb[WB[p_]][:, 0:128]), r=["pb%d" % WB[p_]], w=[("Wsb", p_)])
                for p_ in range(2):
                    u = us[p_]
                    P.op("pe", lambda e, u=u, p_=p_: e.matmul(pb[UB[p_]][:, 0:128], XT[u], Wsb[p_], start=True, stop=True),
                         r=[("XTb", u // 4), ("Wsb", p_)], w=["pb%d" % UB[p_]])
                for p_ in range(2):
                    if p_ == 0:
                        P.op("dve", lambda e, p_=p_: e.tensor_copy(Ubf[p_], pb[UB[p_]][:, 0:128]), r=["pb%d" % UB[p_]], w=[("Ubf", p_)])
                    else:
                        P.op("act", lambda e, p_=p_: e.copy(Ubf[p_], pb[UB[p_]][:, 0:128]), r=["pb%d" % UB[p_]], w=[("Ubf", p_)])
                for p_ in range(2):
                    u = us[p_]
                    yps = pb[4 + p_][:, c * 64:(c + 1) * 64]
                    P.op("pe", lambda e, yps=yps, c=c, p_=p_: e.matmul(yps, H_bf[p_], AR[p_][:, c, 1, :], start=True, stop=False),
                         r=[("H_bf", p_), ("AR", p_)], w=["pb%d" % (4 + p_)])
                    P.op("pe", lambda e, yps=yps, u=u, p_=p_: e.matmul(yps, Ubf[p_], Ar[u][:, 1, :], start=False, stop=False),
                         r=[("Ubf", p_), ("Ar", u)], w=["pb%d" % (4 + p_)])
                    P.op("pe", lambda e, yps=yps, u=u: e.matmul(yps, Vtm[u], Ar[u][:, 0, :], start=False, stop=True),
                         r=[("TM", u), ("Ar", u)], w=["pb%d" % (4 + p_)])
                    P.op("pe", lambda e, u=u, p_=p_: e.matmul(pb[HB[p_]][:, 0:128], Bhtm[u], Ubf[p_], start=True, stop=False),
                         r=[("TM", u), ("Ubf", p_)], w=["pb%d" % HB[p_]])
                    P.op("pe", lambda e, u=u, p_=p_: e.matmul(pb[HB[p_]][:, 0:128], Khtm[u], Vtm[u], start=False, stop=True),
                         r=[("TM", u)], w=["pb%d" % HB[p_]])
                for p_ in range(2):
                    Hf, Hk = Hfs[p_], Hks[p_]
                    P.op("dve", lambda e, Hf=Hf, p_=p_, c=c: e.scalar_tensor_tensor(Hf, Hf, gC[p_][:, c:c + 1], pb[HB[p_]][:, 0:128], ALU.mult, ALU.add),
                         r=["pb%d" % HB[p_], ("gC", p_), Hk], w=[Hk])
                    P.op("act", lambda e, Hf=Hf, p_=p_: e.copy(H_bf[p_], Hf), r=[Hk], w=[("H_bf", p_)])
                if gc >= 31:
                    slot = gc - 31
                    for p_ in range(2):
                        Hf, Hk = Hfs[p_], Hks[p_]
                        P.op("pe", lambda e, Hf=Hf, p_=p_: e.transpose(pb[UB[p_]][:, 0:128], Hf, ident_f), r=[Hk, "cf"], w=["pb%d" % UB[p_]])
                        P.op("dve", lambda e, p_=p_: e.tensor_reduce(Sx[p_], pb[UB[p_]][:, 0:128].rearrange("p (h k) -> p k h", h=2), AX.X, ALU.add),
                             r=["pb%d" % UB[p_]], w=[("Sx", p_)])
                        P.dma("sp", o_rwkv[l, slot, p_ * 128:(p_ + 1) * 128, :], Sx[p_], r=[("Sx", p_)], w=["o_rwkv"])
            if RCUT <= 5:
                return
            def stageD(p_, a=a, n=n):
                PB2, PB3 = (2, 3) if p_ == 0 else (6, 7)
                tf = tmpf[p_]
                P.op("act", lambda e, p_=p_, n=n, tf=tf: e.copy(yT[p_][:, 0:n], pb[4 + p_][:, 0:n]), r=["pb%d" % (4 + p_)], w=[("yT", p_)])
                yield
                P.op("pe", lambda e, p_=p_, n=n, tf=tf: e.matmul(pb[PB2][:, 0:n], blk64_f, yT[p_][:, 0:n], start=True, stop=True), r=[("yT", p_), "cf"], w=["pb%d" % PB2])
                yield
                P.op("dve", lambda e, p_=p_, n=n, tf=tf: e.scalar_tensor_tensor(tf["Ep"][:, 0:n], pb[PB2][:, 0:n], -1.0 / 64.0, yT[p_][:, 0:n], ALU.mult, ALU.add),
                     r=["pb%d" % PB2, ("yT", p_)], w=[("Ep", p_)])
                yield
                P.op("act", lambda e, n=n, p_=p_, tf=tf: e.activation(sqk[p_][:, 0:n], tf["Ep"][:, 0:n], AF.Square), r=[("Ep", p_)], w=[("sqk", p_)])
                yield
                P.op("pe", lambda e, n=n, p_=p_, tf=tf: e.matmul(pb[PB3][:, 0:n], blk64_bf, sqk[p_][:, 0:n], start=True, stop=True), r=[("sqk", p_), "cb"], w=["pb%d" % PB3])
                yield
                P.op("act", lambda e, n=n, p_=p_, tf=tf: e.activation(tf["nrm"][:, 0:n], pb[PB3][:, 0:n], AF.Sqrt, bias=64e-5, scale=1.0 / 64.0), r=["pb%d" % PB3], w=[("nrm", p_)])
                yield
                P.op("dve", lambda e, n=n, p_=p_, tf=tf: e.reciprocal(tf["t"][:, 0:n], tf["nrm"][:, 0:n]), r=[("nrm", p_)], w=[("t", p_)])
                yield
                P.op("dve", lambda e, p_=p_, n=n, tf=tf: e.scalar_tensor_tensor(tf["Em"][:, 0:n], tf["Ep"][:, 0:n], vcol(l, V_LXG + p_), tf["t"][:, 0:n], ALU.mult, ALU.mult),
                     r=[("Ep", p_), ("t", p_), "vecs"], w=[("Em", p_)])
                yield
                P.op("dve", lambda e, p_=p_, n=n, tf=tf: e.scalar_tensor_tensor(tf["Epr"][:, 0:n], xm[:, p_, 0:n], vcol(l, V_RK + p_), k2[p_][:, 0:n], ALU.mult, ALU.mult),
                     r=["xm", ("k2", p_), "vecs"], w=[("Epr", p_)])
                yield
                P.op("pe", lambda e, n=n, p_=p_, tf=tf: e.matmul(pb[PB2][:, 0:n], blk64_f, tf["Epr"][:, 0:n], start=True, stop=True), r=[("Epr", p_), "cf"], w=["pb%d" % PB2])
                yield
                P.op("dve", lambda e, p_=p_, n=n, tf=tf: e.tensor_tensor(tf["kt"][:, 0:n], pb[PB2][:, 0:n], xm[:, 4 + p_, 0:n], ALU.mult), r=["pb%d" % PB2, "xm"], w=[("kt", p_)])
                yield
                P.op("dve", lambda e, p_=p_, n=n, tf=tf: e.scalar_tensor_tensor(tf["Em"][:, 0:n], tf["Em"][:, 0:n], vcol(l, V_LXB + p_), tf["kt"][:, 0:n], ALU.add, ALU.add),
                     r=[("Em", p_), ("kt", p_), "vecs"], w=[("Em", p_)])
                yield
                P.op("pool", lambda e, p_=p_, a=a, n=n, tf=tf: e.tensor_tensor(mixT[:, 4 + p_, a:a + n], tf["Em"][:, 0:n], gate[p_][:, 0:n], ALU.mult),
                     r=[("Em", p_), ("gate", p_)], w=tk("mix", a, n))
                yield


            gensD = [stageD(0), stageD(1)]
            aliveD = [True, True]
            while any(aliveD):
                for gi_ in range(2):
                    if aliveD[gi_]:
                        try:
                            next(gensD[gi_])
                        except StopIteration:
                            aliveD[gi_] = False

    def mixer_phase(l):
        for ti, (a, n) in enumerate(TILES):
            norm_stats(xT[:, :, a:a + n], n, tk("x", a, n), float(D), sq, sd, rstd, 6)
            for c in range(8):
                P.op("dve", lambda e, c=c, a=a, n=n: e.scalar_tensor_tensor(
                    xn[:, c, a:a + n], xT[:, c, a:a + n], gain(l, 2, c), rstd[:, 0:n], ALU.mult, ALU.mult),
                    r=tk("x", a, n) + ["rstd", "vecs"], w=tk("xn", a, n))
        for c in range(8):
            P.dma("sp", xspill[:, c * TT:(c + 1) * TT], xT[:, c, :], r=tk("x", 0, TT), w=["xspill"])
        barrier()
        if STAGE != 3:
            attention(l)
        if STAGE <= 2:
            return
        barrier()
        if "g" in os.environ.get("K_MIX", "gmr"):
            gla(l)
            barrier()
        if "m" in os.environ.get("K_MIX", "gmr"):
            gmlp(l)
            barrier()
        if "r" in os.environ.get("K_MIX", "gmr"):
            rwkv(l)
        if STAGE <= 3:
            return
        barrier()
        for j in range(NSEQ):
            P.op("pool", lambda e, j=j: e.memset(mixT[:, :, T + 64 * j + 4:T + 64 * (j + 1)], 0.0), r=tk("mix", T, 256), w=tk("mix", T, 256))
        for c in range(8):
            P.dma("sp", xT[:, c, :], xspill[:, c * TT:(c + 1) * TT], r=["xspill"], w=tk("x", 0, TT))
        wout = arena[:, XW + 8 * TT:XW + 8 * TT + 4096].bitcast(BF16).rearrange("p (k c) -> p k c", k=8)
        aos = [arena[:, XW + i * 4096:XW + (i + 1) * 4096].rearrange("p (c t) -> p c t", c=8) for i in range(2)]
        P.dma("pool", wout, w_out[l].rearrange("(k p) c -> p k c", p=128), w=["wout"])
        for ti, (a, n) in enumerate(TILES):
            ao = aos[ti % 2]
            aok = ("ao", ti % 2)
            for oc in range(8):
                bk = rr("apj")
                for k in range(8):
                    P.op("pe", lambda e, k=k, bk=bk, oc=oc, a=a, n=n: e.matmul(
                        pb[bk][:, 0:n], wout[:, k, oc * 128:(oc + 1) * 128], mixT[:, k, a:a + n], start=(k == 0), stop=(k == 7)),
                        r=["wout"] + tk("mix", a, n), w=["pb%d" % bk])
                P.op("act", lambda e, bk=bk, oc=oc, n=n, ao=ao: e.copy(ao[:, oc, 0:n], pb[bk][:, 0:n]), r=["pb%d" % bk], w=[aok])
            norm_stats(ao[:, :, 0:n], n, [aok], float(D), sq, sd, rstd, 6)
            for c in range(8):
                q = rr("tmp")
                P.op("dve", lambda e, c=c, q=q, n=n, ao=ao: e.scalar_tensor_tensor(
                    tmpa[q][:, 0:n], ao[:, c, 0:n], gain(l, 3, c), rstd[:, 0:n], ALU.mult, ALU.mult),
                    r=[aok, "rstd", "vecs"], w=[("tmpa", q)])
                P.op("pool" if c % 2 == 0 else "dve", lambda e, c=c, q=q, a=a, n=n: e.tensor_tensor(
                    xT[:, c, a:a + n], xT[:, c, a:a + n], tmpa[q][:, 0:n], ALU.add),
                    r=[("tmpa", q)] + tk("x", a, n), w=tk("x", a, n))
        barrier()
        if STAGE <= 3:
            return
        barrier()

    for l in range(2):
        ffn(l, 0, 0, 0)
        if STAGE <= 1:
            break
        mixer_phase(l)
        if STAGE >= 5:
            ffn(l, 1, 4, 1)
            continue
        if STAGE == 4:
            break
        if STAGE <= 3:
            barrier()
            dtmp = [arena[:, i * 512:(i + 1) * 512] for i in range(2)]
            for c in (range(6, 8) if STAGE == 2 else range(0, 6)):
                for ti, (a, n) in enumerate(TILES):
                    q = rr("dbg")
                    P.op("dve", lambda e, q=q, c=c, a=a, n=n: e.tensor_copy(dtmp[q][:, 0:n], mixT[:, c, a:a + n]),
                         r=tk("mix", a, n), w=[("dtmp", q)])
                    P.dma("sp", dbg[c * 128:(c + 1) * 128, a:a + n], dtmp[q][:, 0:n], r=[("dtmp", q)], w=["dbg"])
            break

    if STAGE <= 1 or STAGE >= 4:
        for c in range(8):
            for (a, n) in ((0, 1024), (1024, 1280)):
                P.dma("sp", yT_out[c * 128:(c + 1) * 128, a:a + n], xT[:, c, a:a + n], r=tk("x", a, n), w=["yT"])
    P.emit(final_keys=finals)
    return nc, P


def pack_vecs(inp):
    v = np.zeros((128, 2 * NVEC), np.float32)
    for l in range(2):
        cols = []
        for k in range(6):
            cols.append(inp["norm_gains"][l, k].reshape(8, 128).T)
        cols.append(inp["gla_b_alpha"][l].reshape(1, 128).T)
        for nm in ("gla_norm", "gmlp_ln_g", "gmlp_ln_b"):
            cols.append(inp[nm][l].reshape(2, 128).T)
        cols.append(inp["rwkv_mu"][l].reshape(8, 128).T)
        for nm in ("rwkv_w0", "rwkv_a0", "rwkv_kk", "rwkv_ka", "rwkv_rk", "rwkv_lnx_g", "rwkv_lnx_b"):
            cols.append(inp[nm][l].reshape(2, 128).T)
        m = np.concatenate(cols, axis=1)
        assert m.shape[1] == NVEC, m.shape
        v[:, l * NVEC:(l + 1) * NVEC] = m
    return v


def core_inputs(inp, i, shared):
    x = np.zeros((TT, D), np.float32)
    x[:T] = inp["x_prompt"][i]
    for j in range(NSEQ):
        x[T + j * SLOT: T + j * SLOT + 4] = inp["x_sample"][4 * i + j]
    sl = slice(4 * i, 4 * i + 4)
    m = dict(shared)
    m["xT_in"] = np.ascontiguousarray(x.T)
    m["c128"] = np.ascontiguousarray(inp["cache_win128"][:, sl].reshape(2, NSEQ, 128, 512))
    m["c512"] = np.ascontiguousarray(inp["cache_win512"][:, sl].reshape(2, NSEQ, 512, 512))
    m["c2048"] = np.ascontiguousarray(inp["cache_win2048"][:, sl].reshape(2, NSEQ, 2048, 512))
    m["state_gla"] = np.ascontiguousarray(inp["state_gla"][:, sl])
    m["state_rwkv"] = np.ascontiguousarray(inp["state_rwkv"][:, sl].reshape(2, NSEQ, 256, 64))
    m["state_shift"] = np.ascontiguousarray(inp["state_shift"][:, sl])
    return m


def shared_inputs(inp):
    return {
        "vecs": pack_vecs(inp), "cf": _CF_ARR, "cb": _CB_ARR, "ca": _CA_ARR,
        "w_ff_gate": inp["w_ff_gate"], "w_ff_up": inp["w_ff_up"], "w_ff_down": inp["w_ff_down"],
        "w_in": inp["w_in"], "w_out": inp["w_out"],
        "gla_w_alpha2": inp["gla_w_alpha2"],
        "gmlp_ln_rows": np.ascontiguousarray(np.stack([inp["gmlp_ln_g"], inp["gmlp_ln_b"]], axis=1)),
        "gmlp_wsT": np.ascontiguousarray(np.swapaxes(inp["gmlp_ws"], 2, 3)),
        "gmlp_bs": inp["gmlp_bs"],
        "rwkv_w2": inp["rwkv_w2"], "rwkv_a2": inp["rwkv_a2"], "rwkv_g2": inp["rwkv_g2"],
    }


_CACHE = {}


def run_cores(inp, cores):
    if "nc" not in _CACHE:
        _CACHE["nc"] = build_program()
    nc, P = _CACHE["nc"]
    shared = shared_inputs(inp)
    in_maps = [core_inputs(inp, i, shared) for i in cores]
    res = run_bass_kernel_spmd(nc, in_maps, core_ids=list(range(len(cores))))
    return res.results


def assemble(res):
    nco = len(res)
    y_p = np.stack([r["yT"].T[:T] for r in res])
    y_s = np.stack([r["yT"].T[T + j * SLOT: T + j * SLOT + 4] for r in res for j in range(NSEQ)])

    def per_prompt(fn):
        return np.stack([np.stack([fn(r, l) for r in res]) for l in range(2)])

    def per_sample(fn):
        return np.stack([np.stack([fn(r, l, j) for r in res for j in range(NSEQ)]) for l in range(2)])

    def kvrows(arr, lo, n):
        return np.ascontiguousarray(arr[:, :, lo:lo + n].transpose(2, 0, 1)).reshape(n, 2, 4, 64)

    p_gla = per_prompt(lambda r, l: r["o_gla"][l, 0].reshape(4, 32, 64))
    p_rwkv = per_prompt(lambda r, l: r["o_rwkv"][l, 0].reshape(4, 64, 64))
    p_shift = per_prompt(lambda r, l: r["o_shift"][l, 0].T.reshape(1024))
    p_w128 = per_prompt(lambda r, l: kvrows(r["okv01"][l, 0], 512 - 128, 128))
    p_w512 = per_prompt(lambda r, l: kvrows(r["okv01"][l, 1], 0, 512))
    p_w2048 = per_prompt(lambda r, l: kvrows(r["okv2"][l], 0, 2048))
    s_gla = per_sample(lambda r, l, j: r["o_gla"][l, 1 + j].reshape(4, 32, 64))
    s_rwkv = per_sample(lambda r, l, j: r["o_rwkv"][l, 1 + j].reshape(4, 64, 64))
    s_shift = per_sample(lambda r, l, j: r["o_shift"][l, 1 + j].T.reshape(1024))
    s_w128 = per_sample(lambda r, l, j: kvrows(r["okv01"][l, 0], 512 + 64 * j, 4))
    s_w512 = per_sample(lambda r, l, j: kvrows(r["okv01"][l, 1], 512 + 64 * j, 4))
    s_w2048 = per_sample(lambda r, l, j: kvrows(r["okv2"][l], T + 64 * j, 4))
    s_gv = per_sample(lambda r, l, j: r["o_gv"][l, j])
    outs = (y_p, y_s, p_gla, p_rwkv, p_shift, p_w128, p_w512, p_w2048, s_gla, s_rwkv, s_shift, s_w128, s_w512, s_w2048, s_gv)
    return tuple(np.ascontiguousarray(o, dtype=np.float32) for o in outs)


def kernel(**inp):
    inp = {k: np.asarray(v) for k, v in inp.items()}
    res = run_cores(inp, list(range(8)))
    return assemble(res)
```

```python
import os
import numpy as np
import concourse.bass as bass
import concourse.mybir as mybir
from contextlib import ExitStack
from concourse.bass_utils import run_bass_kernel_spmd

F32 = mybir.dt.float32
BF16 = mybir.dt.bfloat16
ALU = mybir.AluOpType
AF = mybir.ActivationFunctionType
AX = mybir.AxisListType

ENGS = ["pe", "dve", "act", "pool", "sp"]

D = 1024
T = 2048
SLOT = 64
NSEQ = 4
TT = T + NSEQ * SLOT
DFF = 2816
NFFC = DFF // 128
NCOL = 4624
EPS = 1e-6
TILES = [(0, 512), (512, 512), (1024, 512), (1536, 512), (2048, 256)]
C_GLA, C_GMLP, C_RWKV, C_DIL = 0, 784, 1296, 2320
DIL = ((128, 1), (512, 4), (2048, 16))
NEG = -30000.0
STAGE = int(os.environ.get("K_STAGE", "99"))
AW = 50000

V_GAIN = 0
V_BALPHA = 48
V_GLAN = 49
V_LNG = 51
V_LNB = 53
V_MU = 55
V_W0, V_A0, V_KK, V_KA, V_RK, V_LXG, V_LXB = 63, 65, 67, 69, 71, 73, 75
NVEC = 77


class Prog:
    def __init__(self, nc, n_dma_sems=28):
        self.nc = nc
        self.ops = []
        self.n_dma_sems = n_dma_sems
        self.es = ExitStack()

    def sb(self, name, shape, dt=F32):
        return self.es.enter_context(self.nc.sbuf_tensor(name, list(shape), dt))

    def ps(self, name, shape, dt=F32):
        return self.es.enter_context(self.nc.psum_tensor(name, list(shape), dt))

    def op(self, eng, fn, r=(), w=()):
        extra = tuple(k[0] for k in tuple(r) + tuple(w) if isinstance(k, tuple) and isinstance(k[0], str) and k[0][:2] == "pb" and k[0][2:].isdigit())
        self.ops.append(dict(eng=eng, fn=fn, r=tuple(r) + extra + ("__ph",), w=tuple(w), dma=False, bar=False))

    def dma(self, eng, out, in_, r=(), w=(), **kw):
        def fn(e, out=out, in_=in_, kw=kw):
            return e.dma_start(out=out, in_=in_, **kw)
        self.ops.append(dict(eng=eng, fn=fn, r=tuple(r) + ("__ph",), w=tuple(w), dma=True, bar=False))

    def barrier(self, fn):
        self.ops.append(dict(eng="dve", fn=fn, r=(), w=("__ph",), dma=False, bar=True))

    def emit(self, final_keys=()):
        nc = self.nc
        ops = self.ops
        n = len(ops)
        last_w = {}
        readers = {}
        deps = [set() for _ in range(n)]
        dma_rr = 0
        dma_rr_sw = 0
        dma_last = {}
        last_eng = {}
        bank_last = {}

        def bank_of(k):
            if isinstance(k, tuple):
                k = k[0]
            if isinstance(k, str) and k[:2] == "pb" and k[2:3].isdigit():
                return k[0:3]
            return None

        for i, o in enumerate(ops):
            d = deps[i]
            o["bankdeps"] = set()
            for bnk in {bank_of(k) for k in o["r"] + o["w"]} - {None}:
                if bnk in bank_last and ops[bank_last[bnk]]["eng"] != o["eng"]:
                    o["bankdeps"].add(bank_last[bnk])
                bank_last[bnk] = i
            if o["bar"]:
                d.update(last_eng.values())
                d.update(dma_last.values())
            else:
                for k in o["r"]:
                    if k in last_w:
                        d.add(last_w[k])
                for k in o["w"]:
                    if k in last_w:
                        d.add(last_w[k])
                    for j in readers.get(k, ()):
                        d.add(j)
            if o["dma"]:
                half = self.n_dma_sems // 2
                if o["eng"] == "pool":
                    s = half + dma_rr_sw % (self.n_dma_sems - half)
                    dma_rr_sw += 1
                else:
                    s = dma_rr % half
                    dma_rr += 1
                o["dsem"] = s
                if s in dma_last:
                    d.add(dma_last[s])
                dma_last[s] = i
            else:
                last_eng[o["eng"]] = i
            d.discard(i)
            if not o["bar"]:
                for k in o["r"]:
                    if k != "__ph":
                        readers.setdefault(k, []).append(i)
            for k in o["w"]:
                last_w[k] = i
                readers[k] = []
        fin = set()
        for k in final_keys:
            if k in last_w:
                fin.add(last_w[k])
        needed = set()
        for i, o in enumerate(ops):
            nd = set()
            for j in deps[i]:
                p = ops[j]
                if (not p["dma"]) and (not o["dma"]) and p["eng"] == o["eng"]:
                    if o["eng"] == "pe":
                        continue
                nd.add(j)
            nd |= o["bankdeps"]
            deps[i] = nd
            needed |= nd
        needed |= fin
        cnt = {e: 0 for e in ENGS}
        dcnt = {}
        for i, o in enumerate(ops):
            if o["dma"]:
                s = o["dsem"]
                dcnt[s] = dcnt.get(s, 0) + 16
                o["sig"] = ("d", s, dcnt[s])
            elif i in needed:
                cnt[o["eng"]] += 1
                o["sig"] = ("e", o["eng"], cnt[o["eng"]])
            else:
                o["sig"] = None
        es = self.es
        esem = {e: es.enter_context(nc.semaphore("sem_" + e)) for e in ENGS}
        dsem = [es.enter_context(nc.semaphore("dsem%d" % s)) for s in range(self.n_dma_sems)]

        def semof(k0, k1):
            return esem[k1] if k0 == "e" else dsem[k1]

        block = es.enter_context(nc.Block())
        by_eng = {e: [i for i, o in enumerate(ops) if o["eng"] == e] for e in ENGS}
        self.stats = {e: len(by_eng[e]) for e in ENGS}

        def run(e_obj, ename):
            seen = {}
            nwait = 0
            for i in by_eng[ename]:
                o = ops[i]
                waits = {}
                for j in deps[i]:
                    sig = ops[j]["sig"]
                    key = (sig[0], sig[1])
                    waits[key] = max(waits.get(key, 0), sig[2])
                for key, v in waits.items():
                    if seen.get(key, 0) >= v:
                        continue
                    seen[key] = v
                    e_obj.wait_ge(semof(*key), v)
                    nwait += 1
                ins = o["fn"](e_obj)
                sig = o["sig"]
                if sig is not None:
                    ins.then_inc(semof(sig[0], sig[1]), 16 if sig[0] == "d" else 1)
            if ename == "sp":
                waits = {}
                for j in fin:
                    sig = ops[j]["sig"]
                    key = (sig[0], sig[1])
                    waits[key] = max(waits.get(key, 0), sig[2])
                for key, v in waits.items():
                    if seen.get(key, 0) >= v:
                        continue
                    e_obj.wait_ge(semof(*key), v)
            self.stats["wait_" + ename] = nwait

        @block.tensor
        def _(e):
            run(e, "pe")

        @block.vector
        def _(e):
            run(e, "dve")

        @block.scalar
        def _(e):
            run(e, "act")

        @block.gpsimd
        def _(e):
            run(e, "pool")

        @block.sync
        def _(e):
            run(e, "sp")

        es.close()


def tk(name, a, n):
    return [(name, q) for q in range(a // 256, (a + n + 255) // 256)]


def _prod(s):
    r = 1
    for v in s:
        r *= v
    return r


class Carve:
    def __init__(self, arena, regions):
        self.arena = arena
        self.regions = [list(r) for r in regions]

    def take(self, shape, dt=F32):
        nel = _prod(shape)
        words = nel if dt == F32 else (nel + 1) // 2
        words = (words + 1) // 2 * 2
        for r in self.regions:
            if r[1] - r[0] >= words:
                off = r[0]
                r[0] += words
                break
        else:
            raise RuntimeError("arena region overflow: need %d words, free %s" % (words, self.regions))
        ap = self.arena[:, off:off + words]
        if dt != F32:
            ap = ap.bitcast(dt)
        ap = ap[:, 0:nel]
        if len(shape) == 2:
            ap = ap.rearrange("p (a b) -> p a b", a=shape[0])
        elif len(shape) == 3:
            ap = ap.rearrange("p (a b c) -> p a b c", a=shape[0], b=shape[1])
        elif len(shape) == 4:
            ap = ap.rearrange("p (a b c d) -> p a b c d", a=shape[0], b=shape[1], c=shape[2])
        return ap


def alibi_slopes():
    s = np.exp2(-8.0 * (np.arange(12, dtype=np.float32) + 1.0) / 12.0).astype(np.float32)
    return s.reshape(3, 4)


CF = {}
CA = {}
CB = {}


def build_consts():
    f_parts, b_parts, a_parts = [], [], []

    def adda(name, arr):
        arr = np.asarray(arr, np.float32).reshape(128, -1)
        CA[name] = (sum(a.shape[1] for a in a_parts), arr.shape[1])
        a_parts.append(arr)

    def addf(name, arr):
        arr = np.asarray(arr, np.float32).reshape(128, -1)
        CF[name] = (sum(a.shape[1] for a in f_parts), arr.shape[1])
        f_parts.append(arr)

    def addb(name, arr):
        arr = np.asarray(arr, np.float32).reshape(128, -1)
        CB[name] = (sum(a.shape[1] for a in b_parts), arr.shape[1])
        b_parts.append(arr)

    p = np.arange(128)
    ident = np.eye(128, dtype=np.float32)
    addf("ident", ident)
    addb("ident", ident)
    addf("ones64", np.ones((128, 64), np.float32))
    blk64 = (p[:, None] // 64 == p[None, :] // 64).astype(np.float32)
    addb("blk64", blk64)
    addf("blk64", blk64)
    tt = np.arange(256)
    addf("valid", np.broadcast_to(((tt % 64) < 4).astype(np.float32), (128, 256)))
    t5 = np.arange(512)
    addf("reset", np.broadcast_to(((t5 % 64) != 0).astype(np.float32), (128, 512)))
    s_ = p % 64
    t_ = np.arange(64)
    addf("tri_incl", (s_[:, None] <= t_[None, :]).astype(np.float32))
    addf("hm4", (p[:, None] // 32 == np.arange(4)[None, :]).astype(np.float32))
    c256 = np.arange(256)
    addf("vmask", (p[:, None] // 64 == (c256[None, :] // 64) % 2).astype(np.float32))
    addf("bdm", (p[:, None] // 32 == c256[None, :] // 64).astype(np.float32))
    c128 = np.arange(128)
    same = (p[:, None] // 64 == c128[None, :] // 64)
    addf("m_strict_bd", (same & (p[:, None] % 64 < c128[None, :] % 64)).astype(np.float32))
    addf("m_strictT_bd", (same & (p[:, None] % 64 > c128[None, :] % 64)).astype(np.float32))
    addf("hm2", (p[:, None] // 64 == np.arange(2)[None, :]).astype(np.float32))
    sl = alibi_slopes()
    k_ = p
    q_ = np.arange(256)
    for gi, (W, d) in enumerate(DIL):
        for h in range(4):
            di = q_[None, :] - k_[:, None]
            ok = (di >= 0) & (di <= 128)
            b = np.where(ok, -sl[gi, h] * d * di, NEG).astype(np.float32)
            adda("biasP%d_%d" % (gi, h), b)
    q1 = np.arange(128)
    for gi, (W, d) in enumerate(DIL):
        for h in range(4):
            sameslot = (k_[:, None] // 64) == (q1[None, :] // 64)
            pk = k_[:, None] % 64
            pq = q1[None, :] % 64
            real = (pk < 4) & (pq < 4)
            if gi == 0:
                ok = sameslot & real & (pk <= pq)
            else:
                ok = sameslot & real & (pk == pq)
            ok = ok | (k_[:, None] == q1[None, :])
            b = np.where(ok, -sl[gi, h] * (pq - pk), NEG).astype(np.float32)
            adda("biasS%d_%d" % (gi, h), b)
    pb0 = np.zeros((128, 16), np.float32)
    for i in range(4):
        for h in range(4):
            pb0[:, i * 4 + h] = np.where(p >= i, -sl[0, h] * (i + 128 - p), NEG)
    addf("pbias0", pb0)
    for gi in (1, 2):
        d = DIL[gi][1]
        addf("pbias%d" % gi, np.stack([-sl[gi, h] * d * (128 - p) for h in range(4)], axis=1))
    rows = [0, 1, 2, 3, 64, 65, 66, 67]
    sel = np.zeros((128, 8, 128), np.float32)
    for r, row in enumerate(rows):
        sel[row, r, :] = 1.0
    addb("sel8", sel)
    hs = np.zeros((128, 2, 4), np.float32)
    for pr in range(2):
        for h in range(4):
            hs[:, pr, h] = (p // 64 == h - 2 * pr)
    addf("hsel", hs)
    addf("tri128", (p[:, None] <= c128[None, :]).astype(np.float32))
    return np.concatenate(f_parts, axis=1), np.concatenate(b_parts, axis=1), np.concatenate(a_parts, axis=1)


_CF_ARR, _CB_ARR, _CA_ARR = build_consts()
NCA = _CA_ARR.shape[1]
NCF = _CF_ARR.shape[1]
NCB = _CB_ARR.shape[1]


def build_program():
    nc = bass.Bass("TRN2", target_bir_lowering=False)
    P = Prog(nc)

    def din(name, shape):
        return nc.dram_tensor(name, list(shape), F32, kind="ExternalInput").ap()

    def dout(name, shape):
        return nc.dram_tensor(name, list(shape), F32, kind="ExternalOutput").ap()

    xT_in = din("xT_in", [D, TT])
    vecs_in = din("vecs", [128, 2 * NVEC])
    cf_in = din("cf", [128, NCF])
    cb_in = din("cb", [128, NCB])
    ca_in = din("ca", [128, NCA])
    w_gate = din("w_ff_gate", [2, 2, D, DFF])
    w_up = din("w_ff_up", [2, 2, D, DFF])
    w_down = din("w_ff_down", [2, 2, DFF, D])
    w_in = din("w_in", [2, D, NCOL])
    w_out = din("w_out", [2, D, D])
    c128_in = din("c128", [2, NSEQ, 128, 512])
    c512_in = din("c512", [2, NSEQ, 512, 512])
    c2048_in = din("c2048", [2, NSEQ, 2048, 512])
    caches = (c128_in, c512_in, c2048_in)
    sgla_in = din("state_gla", [2, NSEQ, 4, 32, 64])
    srwkv_in = din("state_rwkv", [2, NSEQ, 256, 64])
    sshift_in = din("state_shift", [2, NSEQ, D])
    walpha_in = din("gla_w_alpha2", [2, 16, 128])
    lnrow_in = din("gmlp_ln_rows", [2, 2, 256])
    wsT_in = din("gmlp_wsT", [2, 4, 128, 128])
    bs_in = din("gmlp_bs", [2, 4, 128])
    w2_in = din("rwkv_w2", [2, 64, 256])
    a2_in = din("rwkv_a2", [2, 64, 256])
    g2_in = din("rwkv_g2", [2, 128, 256])

    yT_out = dout("yT", [D, TT])
    okv01 = dout("okv01", [2, 2, 2, 256, 768])
    okv2 = dout("okv2", [2, 2, 256, TT])
    o_gla = dout("o_gla", [2, 5, 128, 64])
    o_rwkv = dout("o_rwkv", [2, 5, 256, 64])
    o_shift = dout("o_shift", [2, 5, 128, 8])
    o_gv = dout("o_gv", [2, NSEQ, 4, 256])
    dbg = dout("dbg", [D, TT]) if STAGE < 4 else None
    finals = ["yT", "okv", "o_gla", "o_rwkv", "o_shift", "o_gv", "dbg"]
    xspill = nc.dram_tensor("xspill", [128, 8 * TT], F32).ap()

    vecs = P.sb("vecs_sb", [128, 2 * NVEC], F32)
    P.dma("sp", vecs[:], vecs_in, w=["vecs"])
    cf = P.sb("cf_sb", [128, NCF], F32)
    P.dma("sp", cf[:], cf_in, w=["cf"])
    cbt = P.sb("cb_sb", [128, NCB], BF16)
    P.dma("pool", cbt[:], cb_in, w=["cb"])
    ones_bf = P.sb("ones_bf", [128, 128], BF16)
    P.op("dve", lambda e: e.memset(ones_bf[:], 1.0), w=["ones_bf"])
    gh = P.sb("gh", [128, 2, 2, 8], F32)
    bar_t = P.sb("bar_t", [128, 2], F32)

    def CFv(name, shape=None):
        o, n = CF[name]
        ap = cf[:, o:o + n]
        if shape is not None and len(shape) == 2:
            ap = ap.rearrange("p (a b) -> p a b", a=shape[0])
        return ap

    def CBv(name, shape=None):
        o, n = CB[name]
        ap = cbt[:, o:o + n]
        if shape is not None and len(shape) == 2:
            ap = ap.rearrange("p (a b) -> p a b", a=shape[0])
        return ap

    ident_bf = CBv("ident")
    ident_f = CFv("ident")
    blk64_bf = CBv("blk64")
    blk64_f = CFv("blk64")
    ones64_f = CFv("ones64")

    def vcol(l, j, n=1):
        return vecs[:, l * NVEC + j: l * NVEC + j + n]

    def gain(l, k, c):
        return vcol(l, V_GAIN + k * 8 + c)

    for l in range(2):
        for f, k in ((0, 1), (1, 5)):
            P.op("dve", lambda e, l=l, f=f, k=k: e.tensor_scalar(
                gh[:, l, f, :], vcol(l, V_GAIN + k * 8, 8), 0.5, None, ALU.mult), r=["vecs"], w=["gh"])

    def barrier():
        P.barrier(lambda e: e.memset(bar_t[:], 0.0))

    arena = P.sb("arena", [128, AW], F32)
    XW = 8 * TT
    cvf = Carve(arena, [(0, AW)])
    xT = cvf.take([8, TT], F32)
    GW = 768
    ffo = cvf.take([8, GW], F32)
    xng = cvf.take([8, GW], BF16)
    hbuf = cvf.take([NFFC, GW], BF16)
    wg_sb = [cvf.take([8, 256], BF16) for i in range(2)]
    wu_sb = [cvf.take([8, 256], BF16) for i in range(2)]
    wd_sb = [cvf.take([NFFC, 256], BF16) for i in range(2)]
    sq = cvf.take([4, 512], BF16)
    sd = cvf.take([512], F32)
    rstd = cvf.take([512], F32)
    sg = [cvf.take([512], F32) for i in range(2)]
    tmpa = [cvf.take([512], F32) for i in range(2)]
    xn = arena[:, XW:XW + 4 * TT].bitcast(BF16).rearrange("p (c t) -> p c t", c=8)
    mixT = arena[:, XW + 4 * TT:XW + 8 * TT].bitcast(BF16).rearrange("p (c t) -> p c t", c=8)
    MIXER_REGIONS = [(0, XW), (XW + 8 * TT, AW)]

    for (a, n) in ((0, 768), (768, 768), (1536, 768)):
        for c in range(8):
            P.dma("sp", xT[:, c, a:a + n], xT_in[c * 128:(c + 1) * 128, a:a + n], w=tk("x", a, n))

    pb = [P.ps("pb%d" % i, [128, 512], F32) for i in range(8)]
    cnt = {}

    def rr(name, m=2):
        v = cnt.get(name, 0)
        cnt[name] = v + 1
        return v % m

    def norm_stats(src3, n, rkeys, scale_div, sq_t, sd_t, rstd_t, pbank, lhs=None, eps=EPS, nchunk=8, kp=""):
        lhs = ones_bf[:] if lhs is None else lhs
        hh = nchunk // 2
        for half in range(2):
            P.op("act", lambda e, half=half: e.activation(sq_t[:, 0:hh, 0:n], src3[:, half * hh:(half + 1) * hh, :], AF.Square),
                 r=rkeys, w=["sq" + kp])
            for c in range(hh):
                cc_ = half * hh + c
                P.op("pe", lambda e, c=c, cc_=cc_: e.matmul(pb[pbank][:, 0:n], lhs, sq_t[:, c, 0:n], start=(cc_ == 0), stop=(cc_ == nchunk - 1)),
                     r=["sq" + kp, "ones_bf", "cb"], w=["pb%d" % pbank])
        P.op("act", lambda e: e.activation(sd_t[:, 0:n], pb[pbank][:, 0:n], AF.Sqrt, bias=eps, scale=1.0 / scale_div),
             r=["pb%d" % pbank], w=["sd" + kp])
        P.op("dve", lambda e: e.reciprocal(rstd_t[:, 0:n], sd_t[:, 0:n]), r=["sd" + kp], w=["rstd" + kp])

    def ffn(l, f, k_pre, f_post):
        wg_d = w_gate[l, f].rearrange("(k p) c -> p k c", p=128)
        wu_d = w_up[l, f].rearrange("(k p) c -> p k c", p=128)
        wd_d = w_down[l, f].rearrange("(k p) c -> p k c", p=128)
        tiles = [(0, 512), (512, 256)]

        def prenorm(g):
            t0 = g * GW
            for j, (ra, n) in enumerate(tiles):
                a = t0 + ra
                norm_stats(xT[:, :, a:a + n], n, tk("x", a, n), float(D), sq, sd, rstd, 6)
                for c in range(8):
                    P.op("dve", lambda e, c=c, a=a, ra=ra, n=n: e.scalar_tensor_tensor(
                        xng[:, c, ra:ra + n], xT[:, c, a:a + n], gain(l, k_pre, c), rstd[:, 0:n], ALU.mult, ALU.mult),
                        r=tk("x", a, n) + ["rstd", "vecs"], w=[("xng", j)])

        def phase1(g):
            for fg in range(11):
                b = rr("w1")
                P.dma("pool", wg_sb[b], wg_d[:, :, fg * 256:(fg + 1) * 256], w=[("wg", b)])
                P.dma("pool", wu_sb[b], wu_d[:, :, fg * 256:(fg + 1) * 256], w=[("wu", b)])
                for jj in range(2):
                    ffc = fg * 2 + jj
                    for j, (ra, n) in enumerate(tiles):
                        q = rr("ffn1")
                        pg, pu = pb[q], pb[2 + q]
                        for k in range(8):
                            P.op("pe", lambda e, k=k, pg=pg, b=b, jj=jj, ra=ra, n=n: e.matmul(
                                pg[:, 0:n], wg_sb[b][:, k, jj * 128:(jj + 1) * 128], xng[:, k, ra:ra + n],
                                start=(k == 0), stop=(k == 7)), r=[("wg", b), ("xng", j)], w=["pb%d" % q])
                        for k in range(8):
                            P.op("pe", lambda e, k=k, pu=pu, b=b, jj=jj, ra=ra, n=n: e.matmul(
                                pu[:, 0:n], wu_sb[b][:, k, jj * 128:(jj + 1) * 128], xng[:, k, ra:ra + n],
                                start=(k == 0), stop=(k == 7)), r=[("wu", b), ("xng", j)], w=["pb%d" % (2 + q)])
                        P.op("act", lambda e, pg=pg, q=q, n=n: e.activation(sg[q][:, 0:n], pg[:, 0:n], AF.Silu),
                             r=["pb%d" % q], w=[("sg", q)])
                        P.op("dve", lambda e, pu=pu, q=q, ffc=ffc, ra=ra, n=n: e.tensor_tensor(
                            hbuf[:, ffc, ra:ra + n], sg[q][:, 0:n], pu[:, 0:n], ALU.mult),
                            r=[("sg", q), "pb%d" % (2 + q)], w=[("h", j, ffc)])

        def phase2(g):
            for og in range(4):
                b = rr("w2")
                P.dma("pool", wd_sb[b], wd_d[:, :, og * 256:(og + 1) * 256], w=[("wd", b)])
                for jj in range(2):
                    oc = og * 2 + jj
                    for j, (ra, n) in enumerate(tiles):
                        q = rr("ffn2")
                        po = pb[4 + q]
                        for ffc in range(NFFC):
                            P.op("pe", lambda e, ffc=ffc, po=po, b=b, jj=jj, ra=ra, n=n: e.matmul(
                                po[:, 0:n], wd_sb[b][:, ffc, jj * 128:(jj + 1) * 128], hbuf[:, ffc, ra:ra + n],
                                start=(ffc == 0), stop=(ffc == NFFC - 1)),
                                r=[("wd", b), ("h", j, ffc)], w=["pb%d" % (4 + q)])
                        P.op("act", lambda e, po=po, oc=oc, ra=ra, n=n: e.copy(ffo[:, oc, ra:ra + n], po[:, 0:n]),
                             r=["pb%d" % (4 + q)], w=[("ffo", j)])

        def postnorm(g):
            t0 = g * GW
            for j, (ra, n) in enumerate(tiles):
                a = t0 + ra
                norm_stats(ffo[:, :, ra:ra + n], n, [("ffo", j)], float(D), sq, sd, rstd, 7)
                for c in range(8):
                    q = rr("tmp")
                    P.op("dve", lambda e, c=c, q=q, ra=ra, n=n: e.scalar_tensor_tensor(
                        tmpa[q][:, 0:n], ffo[:, c, ra:ra + n], gh[:, l, f_post, c:c + 1], rstd[:, 0:n], ALU.mult, ALU.mult),
                        r=[("ffo", j), "rstd", "gh"], w=[("tmpa", q)])
                    P.op("dve", lambda e, c=c, q=q, a=a, n=n: e.tensor_tensor(
                        xT[:, c, a:a + n], xT[:, c, a:a + n], tmpa[q][:, 0:n], ALU.add),
                        r=[("tmpa", q)] + tk("x", a, n), w=tk("x", a, n))

        prenorm(0)
        for g in range(3):
            phase1(g)
            if g + 1 < 3:
                prenorm(g + 1)
            phase2(g)
            postnorm(g)

    def proj(wt, col0, m, a, n, pbank, pkey, wkey):
        for k in range(8):
            P.op("pe", lambda e, k=k: e.matmul(pb[pbank][0:m, 0:n], wt[:, k, col0:col0 + m], xn[:, k, a:a + n],
                                               start=(k == 0), stop=(k == 7)),
                 r=[wkey] + tk("xn", a, n), w=[pkey])

    def attention(l):
        cv = Carve(arena, MIXER_REGIONS[0:1])
        cv2 = Carve(arena, MIXER_REGIONS[1:2])
        cfa = cv.take([NCA], F32)
        P.dma("sp", cfa, ca_in, w=["cfa"])
        wdil = cv.take([8, 768], BF16)
        qT = cv.take([2, TT], BF16)
        kT = cv.take([2, TT], BF16)
        vT = cv.take([2, TT], BF16)
        Vb = cv.take([18, 4, 66], BF16)
        numacc = cv2.take([2, TT], F32)
        denacc = cv2.take([2, TT], F32)
        stg = [cv2.take([512], F32) for i in range(4)]
        qtm = cv2.take([3, 2, 256], BF16)
        st = [cv.take([256], F32) for i in range(2)]
        pT = [cv.take([256], BF16) for i in range(2)]
        kvrows = stg[0:2]
        vrows_bf = [cv.take([256], BF16) for i in range(2)]
        prod = cv.take([256], F32)
        sc = cv.take([16], F32)
        sc2 = cv.take([16], F32)
        pP = cv.take([16], BF16)
        numS = cv2.take([2, 4, 16], F32)
        denS = cv2.take([4, 16], F32)
        tmpS = cv2.take([2, 64], F32)
        redS = cv2.take([2, 16], F32)

        def CAv(name):
            o, n = CA[name]
            return cfa[:, o:o + n]

        P.op("pool", lambda e: e.memset(Vb[:, :, :, 64:65], 1.0), w=["Vb1"])
        P.op("pool", lambda e: e.memset(denacc, 0.0), w=["denacc"])
        P.op("pool", lambda e: e.memset(numS, 0.0), w=["numS"])
        P.op("pool", lambda e: e.memset(denS, 0.0), w=["denS"])

        for gi, (W, d) in enumerate(DIL):
            if str(gi) not in os.environ.get("K_GROUPS", "012"):
                continue
            cb0 = C_DIL + gi * 768
            P.dma("pool", wdil, w_in[l].rearrange("(k p) c -> p k c", p=128)[:, :, cb0:cb0 + 768], w=["wdil"])
            for ti, (a, n) in enumerate(TILES):
                for cc in range(6):
                    bk = rr("apj")
                    proj(wdil, cc * 128, 128, a, n, bk, "pb%d" % bk, "wdil")
                    which, pair = cc // 2, cc % 2
                    dstT = (qT, kT, vT)[which]
                    if a < T and d > 1:
                        dst = dstT[:, pair, 0:T].rearrange("p (r m) -> p r m", r=d)[:, :, a // d:(a + n) // d]
                        srcv = lambda ap, d=d: ap.rearrange("p (m r) -> p r m", r=d)
                    else:
                        dst = dstT[:, pair, a:a + n]
                        srcv = lambda ap: ap
                    wk = [(("qT", "kT", "vT")[which], pair)]
                    if which == 0:
                        P.op("act", lambda e, bk=bk, dst=dst, srcv=srcv, n=n: e.activation(
                            dst, srcv(pb[bk][:, 0:n]), AF.Copy, scale=0.125), r=["pb%d" % bk], w=wk)
                    else:
                        kv = which - 1
                        need_out = (gi == 2) or (a >= 1536)
                        if need_out:
                            sb_ = rr("stg", 4)
                            P.op("dve", lambda e, bk=bk, sb_=sb_, n=n: e.tensor_copy(stg[sb_][:, 0:n], pb[bk][:, 0:n]),
                                 r=["pb%d" % bk], w=[("stg", sb_)])
                            P.op("act", lambda e, bk=bk, dst=dst, srcv=srcv, n=n: e.copy(dst, srcv(pb[bk][:, 0:n])), r=["pb%d" % bk], w=wk)
                            if gi == 2:
                                P.dma("sp", okv2[l, kv, pair * 128:(pair + 1) * 128, a:a + n], stg[sb_][:, 0:n],
                                      r=[("stg", sb_)], w=["okv"])
                            else:
                                P.dma("sp", okv01[l, gi, kv, pair * 128:(pair + 1) * 128, a - 1536:a - 1536 + n], stg[sb_][:, 0:n],
                                      r=[("stg", sb_)], w=["okv"])
                        else:
                            P.op("dve", lambda e, bk=bk, dst=dst, srcv=srcv, n=n: e.tensor_copy(dst, srcv(pb[bk][:, 0:n])), r=["pb%d" % bk], w=wk)
            for blk in range(2):
                bk = rr("apj")
                for k in range(8):
                    P.op("pe", lambda e, k=k, bk=bk, blk=blk: e.matmul(
                        pb[bk][:, 0:256], xn[:, k, T + blk * 128:T + (blk + 1) * 128], wdil[:, k, 0:256],
                        start=(k == 0), stop=(k == 7)), r=["wdil"] + tk("xn", T, 256), w=["pb%d" % bk])
                P.op("act", lambda e, bk=bk, blk=blk, gi=gi: e.activation(qtm[:, gi, blk, :], pb[bk][:, 0:256], AF.Copy, scale=0.125),
                     r=["pb%d" % bk], w=["qtm"])
            for b2 in range(9):
                bk = rr("apj")
                for bb in range(2):
                    blk = b2 * 2 + bb
                    for pair in range(2):
                        slot_ = bb * 2 + pair
                        pst = pb[bk][:, slot_ * 64:(slot_ + 1) * 64].bitcast(BF16)
                        P.op("pe", lambda e, pst=pst, pair=pair, blk=blk: e.transpose(pst, vT[:, pair, blk * 128:(blk + 1) * 128], ident_bf),
                             r=[("vT", pair), "cb"], w=["pb%d" % bk])
                src4 = pb[bk][:, 0:256].bitcast(BF16).rearrange("p (b h d) -> p b h d", b=2, h=4)
                dst4 = Vb[:, b2 * 2:b2 * 2 + 2, :, 0:64]
                if b2 % 2 == 0:
                    P.op("act", lambda e, src4=src4, dst4=dst4: e.copy(dst4, src4), r=["pb%d" % bk], w=[("Vb", b2 * 2), ("Vb", b2 * 2 + 1)])
                else:
                    P.op("dve", lambda e, src4=src4, dst4=dst4: e.tensor_copy(dst4, src4), r=["pb%d" % bk], w=[("Vb", b2 * 2), ("Vb", b2 * 2 + 1)])
            cbk = 16 // d
            for h in range(4):
                p_, hin = h // 2, h % 2
                pr0 = 64 * hin
                dr = 64 if hin == 0 else 0
                biasP = CAv("biasP%d_%d" % (gi, h))
                biasS = CAv("biasS%d_%d" % (gi, h))
                def kb_info(kb):
                    if kb < 16:
                        return (256 if ((kb + 1) % cbk != 0) else 128), biasP
                    return 128, biasS

                def emit_S(kb):
                    N, bias = kb_info(kb)
                    par = kb % 2
                    P.op("pe", lambda e, par=par, kb=kb, N=N, p_=p_, pr0=pr0: e.matmul(
                        pb[5 + par][:, 0:N], kT[pr0:pr0 + 64, p_, kb * 128:(kb + 1) * 128],
                        qT[pr0:pr0 + 64, p_, kb * 128:kb * 128 + N], start=True, stop=True),
                        r=[("kT", p_), ("qT", p_)], w=["pb%d" % (5 + par)])

                def emit_rest(kb):
                    N, bias = kb_info(kb)
                    par = kb % 2
                    P.op("dve", lambda e, par=par, N=N, bias=bias: e.tensor_tensor(
                        st[par][:, 0:N], pb[5 + par][:, 0:N], bias[:, 0:N], ALU.add),
                        r=["pb%d" % (5 + par), "cfa"], w=[("st", par)])
                    P.op("act", lambda e, par=par, N=N: e.activation(pT[par][:, 0:N], st[par][:, 0:N], AF.Exp),
                         r=[("st", par)], w=[("pT", par)])
                    for half in range(N // 128):
                        qb = kb + half
                        tb, col0 = qb // 4, (qb % 4) * 128
                        if half == 1:
                            start, stop = True, False
                        else:
                            start = (qb >= 16) or (qb % cbk == 0)
                            stop = True
                        if hin == 0:
                            P.op("pe", lambda e, par=par, kb=kb, h=h, tb=tb, col0=col0, half=half, start=start, stop=stop: e.matmul(
                                pb[tb][0:65, col0:col0 + 128], Vb[:, kb, h, 0:65], pT[par][:, half * 128:(half + 1) * 128],
                                start=start, stop=stop), r=[("pT", par), ("Vb", kb), "Vb1"], w=["pb%d" % tb])
                            continue
                        P.op("pe", lambda e, par=par, kb=kb, h=h, tb=tb, col0=col0, half=half, start=start, stop=stop, pr0=pr0: e.matmul(
                            pb[tb][pr0:pr0 + 64, col0:col0 + 128], Vb[:, kb, h, 0:64], pT[par][:, half * 128:(half + 1) * 128],
                            start=start, stop=stop), r=[("pT", par), ("Vb", kb)], w=["pb%d" % tb])
                        P.op("pe", lambda e, par=par, kb=kb, h=h, tb=tb, col0=col0, half=half, start=start, stop=stop, dr=dr: e.matmul(
                            pb[tb][dr:dr + 1, col0:col0 + 128], Vb[:, kb, h, 64:65], pT[par][:, half * 128:(half + 1) * 128],
                            start=start, stop=stop), r=[("pT", par), ("Vb", kb), "Vb1"], w=["pb%d" % tb])

                def evac_bank(tb):
                    a, n = TILES[tb]
                    for (rows, acc, nm) in ((slice(pr0, pr0 + 64), numacc, "numacc"), (slice(dr, dr + 1), denacc, "denacc")):
                        if a < T and d > 1:
                            if d == 4:
                                dst = acc[rows, p_, 0:T].rearrange("p (m r) -> p r m", r=4)[:, tb, :]
                                src = pb[tb][rows, 0:512]
                            else:
                                dst = acc[rows, p_, 0:T].rearrange("p (m r) -> p r m", r=16)[:, 4 * tb:4 * tb + 4, :]
                                src = pb[tb][rows, 0:512].rearrange("p (r m) -> p r m", r=4)
                        else:
                            dst = acc[rows, p_, a:a + n]
                            src = pb[tb][rows, 0:n]
                        kk_ = (nm, p_, "P" if a < T else 4)
                        if str(gi) == os.environ.get("K_GROUPS", "012")[0] and nm == "numacc":
                            P.op("act", lambda e, dst=dst, src=src: e.copy(dst, src), r=["pb%d" % tb], w=[kk_])
                        else:
                            P.op("dve", lambda e, dst=dst, src=src: e.tensor_tensor(dst, dst, src, ALU.add),
                                 r=["pb%d" % tb, kk_, nm], w=[kk_])


                emit_S(0)
                for kb in range(18):
                    if kb + 1 < 18:
                        emit_S(kb + 1)
                    emit_rest(kb)
                    if kb in (3, 7, 11, 15, 17):
                        evac_bank(min(kb // 4, 4))
        hsel = CFv("hsel", [2, 4])
        for j in range(NSEQ):
            blk, rbase = j // 2, (j % 2) * 4
            c0 = T + j * SLOT
            for gi, (W, d) in enumerate(DIL):
                nsets = 1 if gi == 0 else 4
                for si in range(nsets):
                    b = rr("kvr")
                    if gi == 0:
                        src = caches[0][l, j, :, :]
                        qs = [0, 1, 2, 3]
                    else:
                        src = caches[gi][l, j, :, :].rearrange("(m r) c -> r m c", r=d)[si]
                        qs = [si]
                    nq = len(qs)
                    ncol = nq * 4
                    P.dma("sp", kvrows[b], src, w=[("stg", b)])
                    P.op("pool", lambda e, b=b: e.tensor_copy(vrows_bf[b], kvrows[b][:, 256:512]), r=[("stg", b)], w=[("vrb", b)])
                    for qi, i in enumerate(qs):
                        bk = 5 + rr("att")
                        P.op("pe", lambda e, bk=bk, i=i, gi=gi, rbase=rbase, blk=blk: e.matmul(
                            pb[bk][:, 0:256], CBv("sel8", [8, 128])[:, rbase + i, :], qtm[:, gi, blk, :], start=True, stop=True),
                            r=["qtm", "cb"], w=["pb%d" % bk])
                        P.op("dve", lambda e, bk=bk, b=b: e.tensor_tensor(prod, kvrows[b][:, 0:256], pb[bk][:, 0:256], ALU.mult),
                             r=["pb%d" % bk, ("stg", b)], w=["prod"])
                        P.op("dve", lambda e, qi=qi: e.tensor_reduce(
                            sc[:, qi * 4:qi * 4 + 4], prod.rearrange("p (h d) -> p h d", h=4), AX.X, ALU.add),
                            r=["prod"], w=["sc"])
                    pbias = CFv("pbias%d" % gi)
                    P.op("dve", lambda e, ncol=ncol, pbias=pbias: e.tensor_tensor(sc2[:, 0:ncol], sc[:, 0:ncol], pbias[:, 0:ncol], ALU.add),
                         r=["sc", "cf"], w=["sc2"])
                    P.op("act", lambda e, ncol=ncol: e.activation(pP[:, 0:ncol], sc2[:, 0:ncol], AF.Exp), r=["sc2"], w=["pP"])
                    c0s = qs[0] * 4
                    P.op("pe", lambda e, ncol=ncol: e.matmul(pb[7][:, 64:64 + ncol], ones_bf[:], pP[:, 0:ncol], start=True, stop=True),
                         r=["pP", "ones_bf"], w=["pb7"])
                    for p_ in range(2):
                        P.op("pe", lambda e, p_=p_, b=b, ncol=ncol: e.matmul(
                            pb[7][:, p_ * 16:p_ * 16 + ncol], vrows_bf[b][:, p_ * 128:(p_ + 1) * 128], pP[:, 0:ncol], start=True, stop=True),
                            r=["pP", ("vrb", b)], w=["pb7"])
                    P.op("dve", lambda e, j=j, c0s=c0s, ncol=ncol: e.tensor_tensor(
                        numS[:, :, j, c0s:c0s + ncol], numS[:, :, j, c0s:c0s + ncol],
                        pb[7][:, 0:32].rearrange("p (a c) -> p a c", a=2)[:, :, 0:ncol], ALU.add), r=["pb7", "numS"], w=["numS"])
                    P.op("dve", lambda e, j=j, c0s=c0s, ncol=ncol: e.tensor_tensor(
                        denS[:, j, c0s:c0s + ncol], denS[:, j, c0s:c0s + ncol], pb[7][:, 64:64 + ncol], ALU.add), r=["pb7", "denS"], w=["denS"])
        P.op("dve", lambda e: e.tensor_tensor(
            tmpS.rearrange("p a (x h) -> p a x h", h=4), numS.rearrange("p a j (q h) -> p a (j q) h", h=4),
            hsel.unsqueeze(2).broadcast_to([128, 2, 16, 4]), ALU.mult), r=["numS", "cf"], w=["tmpS"])
        P.op("dve", lambda e: e.tensor_reduce(redS, tmpS.rearrange("p a (x h) -> p a x h", h=4), AX.X, ALU.add), r=["tmpS"], w=["redS"])
        nsv = numacc[:, :, T:TT].rearrange("p a (j s) -> p a j s", s=64)[:, :, :, 0:4]
        P.op("dve", lambda e: e.tensor_tensor(nsv, nsv, redS.rearrange("p a (j q) -> p a j q", q=4), ALU.add),
             r=["redS", ("numacc", 0, 4), ("numacc", 1, 4)], w=[("numacc", 0, 4), ("numacc", 1, 4)])
        for p_ in range(2):
            for hin in range(2):
                dr = 64 if hin == 0 else 0
                h = 2 * p_ + hin
                dsv = denacc[dr:dr + 1, p_, T:TT].rearrange("p (j s) -> p j s", s=64)[:, :, 0:4]
                P.op("dve", lambda e, dsv=dsv, dr=dr, h=h: e.tensor_tensor(
                    dsv, dsv, denS[dr:dr + 1, :, :].rearrange("p j (q h) -> p j q h", h=4)[:, :, :, h], ALU.add),
                    r=["denS", ("denacc", p_, 4), "denacc"], w=[("denacc", p_, 4)])
        if STAGE == 2:
            for ci, (srcT, nm) in enumerate(((numacc, "numacc"), (denacc, "denacc"))):
                for p_ in range(2):
                    P.dma("sp", dbg[ci * 256 + p_ * 128: ci * 256 + (p_ + 1) * 128, :], srcT[:, p_, :],
                          r=[(nm, p_, "P"), (nm, p_, 4)], w=["dbg"])
            for ci, srcT in enumerate((qT, kT)):
                for (a, n) in TILES:
                    q = rr("dbgq")
                    P.op("dve", lambda e, q=q, srcT=srcT, a=a, n=n: e.tensor_copy(stg[2 + q][:, 0:n], srcT[:, 0, a:a + n]),
                         r=[("qT", 0), ("kT", 0)], w=[("stg", 2 + q)])
                    P.dma("sp", dbg[512 + ci * 128: 512 + (ci + 1) * 128, a:a + n], stg[2 + q][:, 0:n], r=[("stg", 2 + q)], w=["dbg"])
        for p_ in range(2):
            for hin in range(2):
                pr0 = 64 * hin
                dr = 64 if hin == 0 else 0
                P.op("dve", lambda e, p_=p_, dr=dr: e.reciprocal(denacc[dr:dr + 1, p_, :], denacc[dr:dr + 1, p_, :]),
                     r=[("denacc", p_, "P"), ("denacc", p_, 4), "denacc"], w=[("rden", p_, hin)])
                for tb, (a, n) in enumerate(TILES):
                    bk = rr("apj")
                    P.op("pe", lambda e, bk=bk, p_=p_, dr=dr, pr0=pr0, a=a, n=n: e.matmul(
                        pb[bk][pr0:pr0 + 64, 0:n], ones64_f[dr:dr + 1, 0:64], denacc[dr:dr + 1, p_, a:a + n], start=True, stop=True),
                        r=[("rden", p_, hin), "cf"], w=["pb%d" % bk])
                    P.op("dve", lambda e, bk=bk, p_=p_, pr0=pr0, a=a, n=n: e.tensor_tensor(
                        mixT[pr0:pr0 + 64, 6 + p_, a:a + n], numacc[pr0:pr0 + 64, p_, a:a + n], pb[bk][pr0:pr0 + 64, 0:n], ALU.mult),
                        r=["pb%d" % bk, ("numacc", p_, "P"), ("numacc", p_, 4)], w=tk("mix", a, n))

    def headstat(src, n, lhs, div, eps, sq_t, sd_t, rs_t, pbank, kp):
        P.op("act", lambda e: e.activation(sq_t[:, 0:n], src, AF.Square), r=[kp + "src"], w=[kp + "sq"])
        P.op("pe", lambda e: e.matmul(pb[pbank][:, 0:n], lhs, sq_t[:, 0:n], start=True, stop=True),
             r=[kp + "sq", "cb"], w=["pb%d" % pbank])
        P.op("act", lambda e: e.activation(sd_t[:, 0:n], pb[pbank][:, 0:n], AF.Sqrt, bias=eps, scale=1.0 / div),
             r=["pb%d" % pbank], w=[kp + "sd"])
        P.op("dve", lambda e: e.reciprocal(rs_t[:, 0:n], sd_t[:, 0:n]), r=[kp + "sd"], w=[kp + "rs"])

    def gla(l):
        cv = Carve(arena, MIXER_REGIONS)
        wgla = cv.take([8, 784], BF16)
        P.dma("pool", wgla, w_in[l].rearrange("(k p) c -> p k c", p=128)[:, :, 0:784], w=["wgla"])
        walpha = cv.take([128], BF16)
        P.dma("pool", walpha[0:16, :], walpha_in[l], w=["walpha"])
        nbal = cv.take([2], F32)
        P.op("dve", lambda e: e.tensor_scalar(nbal[:, 0:1], vcol(l, V_BALPHA), -1.0, None, ALU.mult), r=["vecs"], w=["nbal"])
        S_p = cv.take([256], F32)
        S_slot = [cv.take([256], F32) for j in range(NSEQ)]
        S_bf = cv.take([256], BF16)
        Sd = cv.take([64], F32)
        P.op("pool", lambda e: e.memset(S_p, 0.0), w=["S_p"])
        for j in range(NSEQ):
            P.op("pool", lambda e, j=j: e.memset(S_slot[j], 0.0), w=[("S_slot", j)])
            for h in range(4):
                P.dma("sp", S_slot[j][32 * h:32 * h + 32, 64 * h:64 * h + 64], sgla_in[l, j, h], r=[], w=[("S_slot", j)])
        qTt = cv.take([512], F32)
        kTt = cv.take([512], F32)
        alr = cv.take([512], BF16)
        e1 = cv.take([512], F32)
        l1 = cv.take([512], F32)
        la = cv.take([512], F32)
        bcum = cv.take([512], F32)
        eb = cv.take([512], F32)
        enb = cv.take([512], F32)
        ebl = cv.take([8], F32)
        qe = cv.take([512], BF16)
        ke = cv.take([512], BF16)
        kd = cv.take([512], BF16)
        ke_bd = cv.take([8, 4, 64], BF16)
        attm = [cv.take([2, 4, 64], BF16) for i in range(2)]
        Vbd = [cv.take([256], BF16) for i in range(4)]
        Vtm = [cv.take([256], BF16) for i in range(4)]
        kdtm = [cv.take([128], BF16) for i in range(4)]
        t1 = cv.take([256], F32)
        kdtm_all = cv.take([4, 128], BF16)
        t1x = cv.take([4, 256], F32)
        xdup = [cv.take([8, 2, 64], BF16) for i in range(4)]
        oT = cv.take([2, 512], F32)
        sqg = cv.take([512], BF16)
        sdg = cv.take([512], F32)
        rsg = [cv.take([512], F32) for i in range(2)]
        sgl = cv.take([2, 512], F32)
        t2 = cv.take([512], F32)
        tri = CFv("tri_incl")
        hm4 = CFv("hm4")
        vmask = CFv("vmask")
        bdm = CFv("bdm")
        valid = CFv("valid")
        reset = CFv("reset")
        wcols = lambda c0, m: (wgla, c0, m)

        cur = {"S": S_p, "key": "S_p"}
        P.op("act", lambda e: e.copy(S_bf, S_p), r=["S_p"], w=["S_bf"])
        for ti, (a, n) in enumerate(TILES):
            nch = n // 64
            proj(wgla, 0, 128, a, n, 0, "pb0", "wgla")
            P.op("act", lambda e, n=n: e.activation(qTt[:, 0:n], pb[0][:, 0:n], AF.Copy, scale=32 ** -0.5), r=["pb0"], w=["qTt"])
            proj(wgla, 128, 128, a, n, 1, "pb1", "wgla")
            P.op("dve", lambda e, n=n: e.tensor_copy(kTt[:, 0:n], pb[1][:, 0:n]), r=["pb1"], w=["kTt"])
            proj(wgla, 512, 16, a, n, 0, "pb0", "wgla")
            P.op("act", lambda e, n=n: e.copy(alr[0:16, 0:n], pb[0][0:16, 0:n]), r=["pb0"], w=["alr"])
            P.op("pe", lambda e, n=n: e.matmul(pb[2][:, 0:n], walpha[0:16, :], alr[0:16, 0:n], start=True, stop=True),
                 r=["alr", "walpha"], w=["pb2"])
            P.op("act", lambda e, n=n: e.activation(e1[:, 0:n], pb[2][:, 0:n], AF.Exp, bias=nbal[:, 0:1], scale=-1.0),
                 r=["pb2", "nbal"], w=["e1"])
            P.op("act", lambda e, n=n: e.activation(l1[:, 0:n], e1[:, 0:n], AF.Ln, bias=1.0), r=["e1"], w=["l1"])
            if a < T:
                P.op("dve", lambda e, n=n: e.tensor_scalar(la[:, 0:n], l1[:, 0:n], -1.0 / 16.0, None, ALU.mult), r=["l1"], w=["la"])
            else:
                P.op("dve", lambda e, n=n: e.scalar_tensor_tensor(la[:, 0:n], l1[:, 0:n], -1.0 / 16.0, valid[:, 0:n], ALU.mult, ALU.mult),
                     r=["l1", "cf"], w=["la"])
            P.op("dve", lambda e, n=n: e.tensor_tensor_scan(bcum[:, 0:n], reset[:, 0:n], la[:, 0:n], 0.0, ALU.mult, ALU.add),
                 r=["la", "cf"], w=["bcum"])
            P.op("act", lambda e, n=n: e.activation(eb[:, 0:n], bcum[:, 0:n], AF.Exp), r=["bcum"], w=["eb"])
            P.op("act", lambda e, n=n: e.activation(enb[:, 0:n], bcum[:, 0:n], AF.Exp, scale=-1.0), r=["bcum"], w=["enb"])
            P.op("act", lambda e, n=n, nch=nch: e.activation(ebl[:, 0:nch], bcum[:, 63:n:64], AF.Exp), r=["bcum"], w=["ebl"])
            CUT = int(os.environ.get("K_CUT", "99"))
            if CUT <= 1:
                return
            P.op("dve", lambda e, n=n: e.tensor_tensor(qe[:, 0:n], qTt[:, 0:n], eb[:, 0:n], ALU.mult), r=["qTt", "eb"], w=["qe"])
            P.op("pool", lambda e, n=n: e.tensor_tensor(ke[:, 0:n], kTt[:, 0:n], enb[:, 0:n], ALU.mult), r=["kTt", "enb"], w=["ke"])
            P.op("pool", lambda e, n=n, nch=nch: e.tensor_tensor(
                kd[:, 0:n].rearrange("p (c s) -> p c s", s=64), ke[:, 0:n].rearrange("p (c s) -> p c s", s=64),
                ebl[:, 0:nch].unsqueeze(2).broadcast_to([128, nch, 64]), ALU.mult), r=["ke", "ebl"], w=["kd"])
            P.op("dve", lambda e, n=n, nch=nch: e.tensor_tensor(
                ke_bd[:, 0:nch, :, :], ke[:, 0:n].rearrange("p (c s) -> p c s", s=64).unsqueeze(2).broadcast_to([128, nch, 4, 64]),
                hm4.unsqueeze(1).unsqueeze(3).broadcast_to([128, nch, 4, 64]), ALU.mult), r=["ke", "cf"], w=["ke_bd"])
            if CUT <= 2:
                return
            for cg in range(nch // 4):
                ab = rr("attm")
                for cc in range(4):
                    c = cg * 4 + cc
                    for pair in range(2):
                        P.op("pe", lambda e, c=c, cc=cc, pair=pair: e.matmul(
                            pb[4][:, (pair * 4 + cc) * 64:(pair * 4 + cc + 1) * 64],
                            ke_bd[:, c, 2 * pair:2 * pair + 2, :], qe[:, c * 64:(c + 1) * 64], start=True, stop=True),
                            r=["ke_bd", "qe"], w=["pb4"])
                P.op("dve", lambda e, ab=ab: e.tensor_tensor(
                    attm[ab].rearrange("p a b c -> p (a b) c"), pb[4][:, 0:512].rearrange("p (x s) -> p x s", s=64),
                    tri.unsqueeze(1).broadcast_to([128, 8, 64]), ALU.mult), r=["pb4", "cf"], w=[("attm", ab)])
                for cc in range(4):
                    c = cg * 4 + cc
                    vb = cc
                    vbank = 3 if cc % 2 == 0 else 2
                    P.op("dve", lambda e, c=c, a=a, vb=vb: e.tensor_copy(
                        xdup[vb], xn[:, :, a + c * 64:a + (c + 1) * 64].unsqueeze(2).broadcast_to([128, 8, 2, 64])),
                        r=tk("xn", a, n), w=[("xdup", vb)])
                    for k in range(8):
                        P.op("pe", lambda e, k=k, vb=vb, vbank=vbank: e.matmul(
                            pb[vbank][:, 0:256], xdup[vb][:, k, :, :], wgla[:, k, 256:512], start=(k == 0), stop=(k == 7)),
                            r=["wgla", ("xdup", vb)], w=["pb%d" % vbank])
                    P.op("act", lambda e, vb=vb, vbank=vbank: e.copy(Vtm[vb], pb[vbank][:, 0:256]), r=["pb%d" % vbank], w=[("Vtm", vb)])
                    P.op("pool", lambda e, vb=vb: e.tensor_tensor(Vbd[vb], Vtm[vb], vmask, ALU.mult), r=[("Vtm", vb), "cf"], w=[("Vbd", vb)])
                    kt = pb[7][0:64, cc * 64:(cc + 1) * 64].bitcast(BF16)
                    P.op("pe", lambda e, c=c, kt=kt: e.transpose(kt, kd[:, c * 64:(c + 1) * 64], ident_bf), r=["kd", "cb"], w=["pb7"])
                P.op("act", lambda e: e.copy(kdtm_all[0:64, :, :], pb[7][0:64, 0:256].bitcast(BF16).rearrange("p (c k) -> p c k", c=4)), r=["pb7"], w=["kdtm"])
                for cc in range(4):
                    ib = 4 if cc < 2 else 7
                    io = (cc % 2) * 256
                    P.op("pe", lambda e, cc=cc, ib=ib, io=io: e.matmul(pb[ib][:, io:io + 256], kdtm_all[0:64, cc, :], Vtm[cc][0:64, :], start=True, stop=True),
                         r=["kdtm", ("Vtm", cc)], w=["pb%d" % ib])
                for half_ in range(2):
                    ib = 4 if half_ == 0 else 7
                    P.op("dve", lambda e, ib=ib, half_=half_: e.tensor_tensor(
                        t1x[:, 2 * half_:2 * half_ + 2, :], pb[ib][:, 0:512].rearrange("p (c x) -> p c x", c=2),
                        bdm.unsqueeze(1).broadcast_to([128, 2, 256]), ALU.mult), r=["pb%d" % ib, "cf"], w=[("t1x", half_)])
                if CUT <= 3:
                    return
                for cc in range(4):
                    c = cg * 4 + cc
                    vb = cc
                    gc = a // 64 + c
                    if gc >= 32:
                        j = gc - 32
                        cur["S"], cur["key"] = S_slot[j], ("S_slot", j)
                        P.op("act", lambda e, S=cur["S"]: e.copy(S_bf, S), r=[cur["key"]], w=["S_bf"])
                    S, Sk = cur["S"], cur["key"]
                    for pair in range(2):
                        if os.environ.get("K_NOO"):
                            break
                        P.op("pe", lambda e, vb=vb, ab=ab, pair=pair, cc=cc, c=c: e.matmul(
                            pb[5 + pair][:, c * 64:(c + 1) * 64], Vbd[vb][:, pair * 128:(pair + 1) * 128], attm[ab][:, pair, cc, :],
                            start=True, stop=False), r=[("Vbd", vb), ("attm", ab)], w=["pb%d" % (5 + pair)])
                        P.op("pe", lambda e, pair=pair, c=c: e.matmul(
                            pb[5 + pair][:, c * 64:(c + 1) * 64], S_bf[:, pair * 128:(pair + 1) * 128], qe[:, c * 64:(c + 1) * 64],
                            start=False, stop=True), r=["S_bf", "qe"], w=["pb%d" % (5 + pair)])
                    if CUT <= 4:
                        continue
                    P.op("dve", lambda e, S=S, c=c, cc=cc: e.scalar_tensor_tensor(S, S, ebl[:, c:c + 1], t1x[:, cc, :], ALU.mult, ALU.add),
                         r=[("t1x", cc // 2), "ebl", Sk], w=[Sk])
                    P.op("act", lambda e, S=S: e.copy(S_bf, S), r=[Sk], w=["S_bf"])
                    if gc >= 31:
                        slot = gc - 31
                        P.op("dve", lambda e, S=S: e.tensor_reduce(Sd, S.rearrange("p (h v) -> p v h", h=4), AX.X, ALU.add), r=[Sk], w=["Sd"])
                        P.dma("sp", o_gla[l, slot], Sd, r=["Sd"], w=["o_gla"])
            for pair in range(2):
                P.op("act", lambda e, pair=pair, n=n: e.copy(oT[:, pair, 0:n], pb[5 + pair][:, 0:n]), r=["pb%d" % (5 + pair)], w=[("oT", pair)])
                P.op("act", lambda e, pair=pair, n=n: e.activation(sqg[:, 0:n], oT[:, pair, 0:n], AF.Square), r=[("oT", pair)], w=["sqg"])
                P.op("pe", lambda e, n=n: e.matmul(pb[2][:, 0:n], blk64_bf, sqg[:, 0:n], start=True, stop=True), r=["sqg", "cb"], w=["pb2"])
                P.op("act", lambda e, n=n: e.activation(sdg[:, 0:n], pb[2][:, 0:n], AF.Sqrt, bias=EPS, scale=1.0 / 64.0), r=["pb2"], w=["sdg"])
                P.op("dve", lambda e, pair=pair, n=n: e.reciprocal(rsg[pair][:, 0:n], sdg[:, 0:n]), r=["sdg"], w=[("rsg", pair)])
            for pair in range(2):
                bk = rr("apj")
                proj(wgla, 528 + pair * 128, 128, a, n, bk, "pb%d" % bk, "wgla")
                P.op("act", lambda e, bk=bk, pair=pair, n=n: e.activation(sgl[:, pair, 0:n], pb[bk][:, 0:n], AF.Silu), r=["pb%d" % bk], w=[("sgl", pair)])
                P.op("dve", lambda e, pair=pair, n=n: e.scalar_tensor_tensor(
                    t2[:, 0:n], oT[:, pair, 0:n], vcol(l, V_GLAN + pair), rsg[pair][:, 0:n], ALU.mult, ALU.mult),
                    r=[("oT", pair), ("rsg", pair), "vecs"], w=["t2"])
                P.op("pool", lambda e, pair=pair, a=a, n=n: e.tensor_tensor(mixT[:, pair, a:a + n], t2[:, 0:n], sgl[:, pair, 0:n], ALU.mult),
                     r=["t2", ("sgl", pair)], w=tk("mix", a, n))

    def gmlp(l):
        cv = Carve(arena, MIXER_REGIONS)
        wgm = cv.take([8, 512], BF16)
        P.dma("pool", wgm, w_in[l].rearrange("(k p) c -> p k c", p=128)[:, :, C_GMLP:C_GMLP + 512], w=["wgm"])
        wsT_f = cv.take([4, 128], F32)
        P.dma("sp", wsT_f, wsT_in[l].rearrange("g s t -> s g t"), w=["wsT_f"])
        wsS_f = cv.take([4, 128], F32)
        P.op("pool", lambda e: e.memset(wsS_f, 0.0), w=["wsS_f"])
        for g in range(4):
            P.dma("sp", wsS_f[0:4, g, 0:4], wsT_in[l, g, 0:4, 0:4], w=["wsS_f"])
            P.dma("sp", wsS_f[64:68, g, 64:68], wsT_in[l, g, 0:4, 0:4], w=["wsS_f"])
        wsTm = cv.take([4, 128], BF16)
        wsS = cv.take([4, 128], BF16)
        tri128 = CFv("tri128")
        P.op("dve", lambda e: e.tensor_tensor(wsTm, wsT_f, tri128.unsqueeze(1).broadcast_to([128, 4, 128]), ALU.mult), r=["wsT_f", "cf"], w=["wsTm"])
        P.op("dve", lambda e: e.tensor_tensor(wsS, wsS_f, tri128.unsqueeze(1).broadcast_to([128, 4, 128]), ALU.mult), r=["wsS_f", "cf"], w=["wsS"])
        bsT = cv.take([2, 128], F32)
        bsS = cv.take([2, 128], F32)
        for g in range(4):
            P.dma("sp", bsT[(g % 2) * 64:(g % 2) * 64 + 64, g // 2, :], bs_in[l, g:g + 1, :].partition_broadcast(64), w=["bsT"])
        P.op("pool", lambda e: e.tensor_copy(bsS, bsT), r=["bsT"], w=["bsS"])
        P.op("pool", lambda e: e.tensor_copy(bsS[:, :, 64:68], bsT[:, :, 0:4]), r=["bsT", "bsS"], w=["bsS"])
        lng = cv.take([256], F32)
        lnb = cv.take([256], F32)
        P.dma("sp", lng, lnrow_in[l, 0:1, :].partition_broadcast(128), w=["lng"])
        P.dma("sp", lnb, lnrow_in[l, 1:2, :].partition_broadcast(128), w=["lnb"])
        vg4 = [cv.take([256], F32) for i in range(4)]
        mvAll = cv.take([4, 2], F32)
        sdvAll = cv.take([4], F32)
        rsvAll = cv.take([4], F32)
        st6 = cv.take([6], F32)
        mv = cv.take([2], F32)
        sdv = cv.take([2], F32)
        rsv = cv.take([2], F32)
        vn0 = cv.take([256], F32)
        vn1 = cv.take([256], F32)
        vnf = [cv.take([256], F32) for i in range(2)]
        vnb = [cv.take([256], BF16) for i in range(2)]
        uT = cv.take([2, 512], F32)
        ts = cv.take([512], F32)
        for ti, (a, n) in enumerate(TILES):
            nb = n // 128
            bst = bsT if a < T else bsS
            for pair in range(2):
                proj(wgm, pair * 128, 128, a, n, 1, "pb1", "wgm")
                P.op("act", lambda e, pair=pair, n=n: e.activation(uT[:, pair, 0:n], pb[1][:, 0:n], AF.Gelu), r=["pb1"], w=[("uT", pair)])

            for bi in range(nb):
                blk = a // 128 + bi
                vbank = 0 if bi % 2 == 0 else 4
                for k in range(8):
                    P.op("pe", lambda e, k=k, blk=blk, vbank=vbank: e.matmul(pb[vbank][:, 0:256], xn[:, k, blk * 128:(blk + 1) * 128], wgm[:, k, 256:512],
                                                                            start=(k == 0), stop=(k == 7)), r=["wgm"] + tk("xn", a, n), w=["pb%d" % vbank])
                P.op("act", lambda e, bi=bi, vbank=vbank: e.activation(vg4[bi], pb[vbank][:, 0:256], AF.Gelu), r=["pb%d" % vbank], w=[("vg", bi)])
                P.op("dve", lambda e, bi=bi: e.bn_stats(st6, vg4[bi]), r=[("vg", bi)], w=["st6"])
                P.op("dve", lambda e, bi=bi: e.bn_aggr(mvAll[:, bi, :], st6), r=["st6"], w=["mvAll"])
            P.op("act", lambda e, nb=nb: e.activation(sdvAll[:, 0:nb], mvAll[:, 0:nb, 1], AF.Sqrt, bias=1e-5), r=["mvAll"], w=["sdvAll"])
            P.op("dve", lambda e, nb=nb: e.reciprocal(rsvAll[:, 0:nb], sdvAll[:, 0:nb]), r=["sdvAll"], w=["rsvAll"])
            for bi in range(nb):
                blk = a // 128 + bi
                b = bi % 2
                P.op("dve", lambda e, bi=bi: e.tensor_scalar(vn0, vg4[bi], mvAll[:, bi, 0:1], rsvAll[:, bi:bi + 1], ALU.subtract, ALU.mult),
                     r=[("vg", bi), "mvAll", "rsvAll"], w=["vn0"])
                P.op("pool", lambda e: e.tensor_tensor(vn1, vn0, lng, ALU.mult), r=["vn0", "lng"], w=["vn1"])
                P.op("pool", lambda e, b=b: e.tensor_tensor(vnf[b], vn1, lnb, ALU.add), r=["vn1", "lnb"], w=[("vnf", b)])
                P.op("act", lambda e, b=b: e.copy(vnb[b], vnf[b]), r=[("vnf", b)], w=[("vnb", b)])
                if blk >= 16:
                    for jj in range(2):
                        j = (blk - 16) * 2 + jj
                        P.dma("sp", o_gv[l, j], vnf[b][jj * 64:jj * 64 + 4, :], r=[("vnf", b)], w=["o_gv"])
                wmat = wsTm if blk < 16 else wsS
                for g in range(4):
                    P.op("pe", lambda e, g=g, b=b, bi=bi, wmat=wmat: e.matmul(
                        pb[2 + g // 2][(g % 2) * 64:(g % 2) * 64 + 64, bi * 128:(bi + 1) * 128], vnb[b][:, g * 64:(g + 1) * 64], wmat[:, g, :],
                        start=True, stop=True), r=[("vnb", b), "wsTm", "wsS"], w=["pb%d" % (2 + g // 2)])
            for pair in range(2):
                P.op("dve", lambda e, pair=pair, n=n, nb=nb, bst=bst: e.tensor_tensor(
                    ts[:, 0:n].rearrange("p (b t) -> p b t", t=128), pb[2 + pair][:, 0:n].rearrange("p (b t) -> p b t", t=128),
                    bst[:, pair, :].unsqueeze(1).broadcast_to([128, nb, 128]), ALU.add), r=["pb%d" % (2 + pair), "bsT", "bsS"], w=["ts"])
                P.op("pool", lambda e, pair=pair, a=a, n=n: e.tensor_tensor(mixT[:, 2 + pair, a:a + n], uT[:, pair, 0:n], ts[:, 0:n], ALU.mult),
                     r=["ts", ("uT", pair)], w=tk("mix", a, n))

    def rwkv(l):
        cv = Carve(arena, MIXER_REGIONS)
        NT_ = 256
        wrw = cv.take([8, 1024], BF16)
        P.dma("pool", wrw, w_in[l].rearrange("(k p) c -> p k c", p=128)[:, :, C_RWKV:C_RWKV + 1024], w=["wrw"])
        w2b = cv.take([256], BF16)
        a2b = cv.take([256], BF16)
        g2b = cv.take([256], BF16)
        P.dma("pool", w2b[0:64, :], w2_in[l], w=["w2b"])
        P.dma("pool", a2b[64:128, :], a2_in[l], w=["a2b"])
        P.dma("pool", g2b, g2_in[l], w=["g2b"])
        sst = cv.take([8, 4], F32)
        for j in range(NSEQ):
            P.dma("sp", sst[:, :, j], sshift_in[l, j].rearrange("(c p) -> p c", p=128), w=["sst"], allow_slow_non_contiguous=True)
        hm2 = CFv("hm2")
        m_sbd = CFv("m_strict_bd", [2, 64])
        m_sTbd = CFv("m_strictT_bd", [2, 64])
        tri = CFv("tri_incl")
        valid = CFv("valid")
        reset = CFv("reset")
        H_p = [cv.take([128], F32) for p_ in range(2)]
        H_slot = [[cv.take([128], F32) for p_ in range(2)] for j in range(NSEQ)]
        H_bf = [cv.take([128], BF16) for p_ in range(2)]
        snat = cv.take([64], F32)
        sbd = cv.take([2, 64], F32)
        Sx = [cv.take([64], F32) for p_ in range(2)]
        for p_ in range(2):
            P.op("pool", lambda e, p_=p_: e.memset(H_p[p_], 0.0), w=[("H_p", p_)])
            P.op("act", lambda e, p_=p_: e.copy(H_bf[p_], H_p[p_]), r=[("H_p", p_)], w=[("H_bf", p_)])
        for j in range(NSEQ):
            for p_ in range(2):
                P.dma("sp", snat, srwkv_in[l, j, p_ * 128:(p_ + 1) * 128, :], w=["snat"])
                P.op("dve", lambda e: e.tensor_tensor(sbd, snat.unsqueeze(1).broadcast_to([128, 2, 64]),
                                                      hm2.unsqueeze(2).broadcast_to([128, 2, 64]), ALU.mult), r=["snat", "cf"], w=["sbd"])
                P.op("pe", lambda e: e.transpose(pb[7][:, 0:128], sbd.rearrange("p a b -> p (a b)"), ident_f), r=["sbd", "cf"], w=["pb7"])
                P.op("act", lambda e, j=j, p_=p_: e.copy(H_slot[j][p_], pb[7][:, 0:128]), r=["pb7"], w=[("H_slot", j, p_)])
        cext = cv.take([8, NT_ + 1], F32)
        xm = cv.take([8, NT_], F32)
        tw = cv.take([NT_], BF16)
        albf = cv.take([NT_], BF16)
        sgg = cv.take([NT_], BF16)
        gate = [cv.take([NT_], F32) for p_ in range(2)]
        k2 = [cv.take([NT_], F32) for p_ in range(2)]
        AR = [cv.take([4, 2, 64], BF16) for p_ in range(2)]
        btb = [cv.take([NT_], BF16) for p_ in range(2)]
        kbd = [cv.take([4, 2, 64], BF16) for p_ in range(2)]
        bbd = [cv.take([4, 2, 64], BF16) for p_ in range(2)]
        abd = [cv.take([4, 2, 64], BF16) for p_ in range(2)]
        vbd = [cv.take([4, 2, 64], BF16) for p_ in range(2)]
        Bhbd = [cv.take([4, 2, 64], BF16) for p_ in range(2)]
        Khbd = [cv.take([4, 2, 64], BF16) for p_ in range(2)]
        gC = [cv.take([4], F32) for p_ in range(2)]
        tmpf = [{nm: cv.take([NT_], F32) for nm in ("sig", "ar", "kk0", "bv", "t", "gcum", "Ep", "Em", "Epr", "kt", "Bh", "Kh", "nrm")} for p_ in range(2)]
        sqk = [cv.take([NT_], BF16) for p_ in range(2)]
        NU = 8
        XTb = [cv.take([4, 128], F32) for b in range(NU // 4)]
        XT = [XTb[u // 4][:, u % 4, :] for u in range(NU)]
        Qbb = [[cv.take([4, 128], F32) for i in range(2)] for b in range(2)]
        QTbb = [[cv.take([4, 128], F32) for i in range(2)] for b in range(2)]
        Aak = [cv.take([2, 64], BF16) for u in range(NU)]
        Ar = [cv.take([2, 64], BF16) for u in range(NU)]
        TM = [cv.take([3, 128], BF16) for u in range(NU)]
        Vtm = [TM[u][:, 0, :] for u in range(NU)]
        Bhtm = [TM[u][:, 1, :] for u in range(NU)]
        Khtm = [TM[u][:, 2, :] for u in range(NU)]
        Wsb = [cv.take([128], F32) for p_ in range(2)]
        Ubf = [cv.take([128], BF16) for p_ in range(2)]
        yT = [cv.take([NT_], F32) for p_ in range(2)]
        C1 = -float(np.exp(-0.5))

        rtiles = [(a, NT_) for a in range(0, TT, NT_)]
        cur = {"H": H_p, "key": "H_p", "j": None}
        for ti, (a, n) in enumerate(rtiles):
            nch = n // 64
            smp = a >= T
            if ti == 0:
                P.op("pool", lambda e: e.memset(cext[:, :, 0:1], 0.0), w=["cext0"])
            else:
                P.op("pool", lambda e, n=n: e.tensor_copy(cext[:, :, 0:1], cext[:, :, n:n + 1]), r=["cext"], w=["cext0"])
            for cc in range(8):
                bk = rr("apj")
                proj(wrw, cc * 128, 128, a, n, bk, "pb%d" % bk, "wrw")
                if cc % 2 == 0:
                    P.op("act", lambda e, cc=cc, bk=bk, n=n: e.copy(cext[:, cc, 1:n + 1], pb[bk][:, 0:n]), r=["pb%d" % bk, "cext0"], w=["cext"])
                else:
                    P.op("dve", lambda e, cc=cc, bk=bk, n=n: e.tensor_copy(cext[:, cc, 1:n + 1], pb[bk][:, 0:n]), r=["pb%d" % bk, "cext0"], w=["cext"])
            if a + n == T:
                P.dma("sp", o_shift[l, 0], cext[:, :, n], r=["cext"], w=["o_shift"], allow_slow_non_contiguous=True)
            if smp:
                for j in range(NSEQ):
                    P.dma("sp", o_shift[l, 1 + j], cext[:, :, 1 + 64 * j + 3], r=["cext"], w=["o_shift"], allow_slow_non_contiguous=True)
            P.op("pool", lambda e, n=n: e.tensor_tensor(xm[:, 0:4, 0:n], cext[:, 0:4, 0:n], cext[:, 0:4, 1:n + 1], ALU.subtract),
                 r=["cext", "cext0"], w=["xm"])
            P.op("dve", lambda e, n=n: e.tensor_tensor(xm[:, 4:8, 0:n], cext[:, 4:8, 0:n], cext[:, 4:8, 1:n + 1], ALU.subtract),
                 r=["cext", "cext0"], w=["xm"])
            if smp:
                P.op("dve", lambda e, n=n: e.tensor_tensor(xm[:, :, 0:n:64], sst, cext[:, :, 1:n + 1:64], ALU.subtract),
                     r=["cext", "sst", "xm"], w=["xm"])
            for cc in range(8):
                P.op("dve", lambda e, cc=cc, n=n: e.scalar_tensor_tensor(
                    xm[:, cc, 0:n], xm[:, cc, 0:n], vcol(l, V_MU + cc), cext[:, cc, 1:n + 1], ALU.mult, ALU.add),
                    r=["xm", "cext", "vecs"], w=["xm"])
            if smp:
                P.op("pool", lambda e, n=n: e.tensor_tensor(xm[:, :, 0:n], xm[:, :, 0:n], valid[:, 0:n].unsqueeze(1).broadcast_to([128, 8, n]), ALU.mult),
                     r=["xm", "cf"], w=["xm"])
            RCUT = int(os.environ.get("K_RCUT", "99"))
            if RCUT <= 1:
                return
            P.op("act", lambda e, n=n: e.activation(tw[0:64, 0:n], xm[0:64, 6, 0:n], AF.Tanh), r=["xm"], w=["tw"])
            P.op("act", lambda e, n=n: e.copy(albf[64:128, 0:n], xm[64:128, 6, 0:n]), r=["xm"], w=["albf"])
            P.op("act", lambda e, n=n: e.activation(sgg[:, 0:n], xm[:, 7, 0:n], AF.Sigmoid), r=["xm"], w=["sgg"])
            def stageB(p_, n=n, nch=nch, smp=smp):
                PB2, PB3 = (2, 3) if p_ == 0 else (6, 7)
                tf = tmpf[p_]
                r_ = xm[:, p_, 0:n]
                k_ = xm[:, 2 + p_, 0:n]
                v_ = xm[:, 4 + p_, 0:n]
                P.op("pe", lambda e, p_=p_, n=n: e.matmul(pb[PB2][:, 0:n], w2b[0:64, p_ * 128:(p_ + 1) * 128], tw[0:64, 0:n], start=True, stop=True),
                     r=["tw", "w2b"], w=["pb%d" % PB2])
                P.op("act", lambda e, p_=p_, n=n: e.activation(tf["sig"][:, 0:n], pb[PB2][:, 0:n], AF.Sigmoid, bias=vcol(l, V_W0 + p_)),
                     r=["pb%d" % PB2, "vecs"], w=[("sig", p_)])
                if smp:
                    P.op("dve", lambda e, n=n: e.scalar_tensor_tensor(tf["sig"][:, 0:n], tf["sig"][:, 0:n], C1, valid[:, 0:n], ALU.mult, ALU.mult),
                         r=[("sig", p_), "cf"], w=[("sig", p_)])
                else:
                    P.op("dve", lambda e, n=n: e.tensor_scalar(tf["sig"][:, 0:n], tf["sig"][:, 0:n], C1, None, ALU.mult), r=[("sig", p_)], w=[("sig", p_)])
                P.op("pe", lambda e, p_=p_, n=n: e.matmul(pb[PB3][:, 0:n], a2b[64:128, p_ * 128:(p_ + 1) * 128], albf[64:128, 0:n], start=True, stop=True),
                     r=["albf", "a2b"], w=["pb%d" % PB3])
                P.op("act", lambda e, p_=p_, n=n: e.activation(tf["ar"][:, 0:n], pb[PB3][:, 0:n], AF.Sigmoid, bias=vcol(l, V_A0 + p_)),
                     r=["pb%d" % PB3, "vecs"], w=[("ar", p_)])
                P.op("pe", lambda e, p_=p_, n=n: e.matmul(pb[PB2][:, 0:n], g2b[:, p_ * 128:(p_ + 1) * 128], sgg[:, 0:n], start=True, stop=True),
                     r=["sgg", "g2b"], w=["pb%d" % PB2])
                P.op("act", lambda e, p_=p_, n=n: e.copy(gate[p_][:, 0:n], pb[PB2][:, 0:n]), r=["pb%d" % PB2], w=[("gate", p_)])
                yield
                P.op("dve", lambda e, p_=p_, n=n, k_=k_: e.tensor_scalar(tf["kk0"][:, 0:n], k_, vcol(l, V_KK + p_), None, ALU.mult),
                     r=["xm", "vecs"], w=[("kk0", p_)])
                P.op("act", lambda e, n=n: e.activation(sqk[p_][:, 0:n], tf["kk0"][:, 0:n], AF.Square), r=[("kk0", p_)], w=[("sqk", p_)])
                yield
                P.op("pe", lambda e, n=n: e.matmul(pb[PB3][:, 0:n], blk64_bf, sqk[p_][:, 0:n], start=True, stop=True), r=[("sqk", p_), "cb"], w=["pb%d" % PB3])
                yield
                P.op("act", lambda e, n=n: e.activation(tf["nrm"][:, 0:n], pb[PB3][:, 0:n], AF.Sqrt), r=["pb%d" % PB3], w=[("nrm", p_)])
                yield
                P.op("dve", lambda e, n=n: e.tensor_scalar(tf["nrm"][:, 0:n], tf["nrm"][:, 0:n], 1e-12, None, ALU.max), r=[("nrm", p_)], w=[("nrm", p_)])
                yield
                P.op("dve", lambda e, n=n: e.reciprocal(tf["t"][:, 0:n], tf["nrm"][:, 0:n]), r=[("nrm", p_)], w=[("t", p_)])
                yield
                P.op("dve", lambda e, n=n: e.tensor_tensor(tf["kk0"][:, 0:n], tf["kk0"][:, 0:n], tf["t"][:, 0:n], ALU.mult), r=[("kk0", p_), ("t", p_)], w=[("kk0", p_)])
                yield
                P.op("pool", lambda e, n=n: e.tensor_tensor(tf["bv"][:, 0:n], tf["kk0"][:, 0:n], tf["ar"][:, 0:n], ALU.mult), r=[("kk0", p_), ("ar", p_)], w=[("bv", p_)])
                yield
                P.op("dve", lambda e, p_=p_, n=n: e.tensor_scalar(tf["t"][:, 0:n], tf["ar"][:, 0:n], vcol(l, V_KA + p_), vcol(l, V_KA + p_), ALU.mult, ALU.subtract),
                     r=[("ar", p_), "vecs", ("kk0", p_)], w=[("t", p_)])
                P.op("dve", lambda e, p_=p_, n=n, k_=k_: e.scalar_tensor_tensor(k2[p_][:, 0:n], tf["t"][:, 0:n], 1.0, k_, ALU.add, ALU.mult),
                     r=[("t", p_), "xm"], w=[("k2", p_)])
                P.op("dve", lambda e, n=n: e.tensor_tensor_scan(tf["gcum"][:, 0:n], reset[:, 0:n], tf["sig"][:, 0:n], 0.0, ALU.mult, ALU.add),
                     r=[("sig", p_), "cf"], w=[("gcum", p_)])
                P.op("pool", lambda e, n=n: e.tensor_tensor(tf["nrm"][:, 0:n], tf["gcum"][:, 0:n], tf["sig"][:, 0:n], ALU.subtract), r=[("gcum", p_), ("sig", p_)], w=[("nrm", p_)])
                yield
                P.op("act", lambda e, n=n: e.activation(tf["Ep"][:, 0:n], tf["gcum"][:, 0:n], AF.Exp), r=[("gcum", p_)], w=[("Ep", p_)])
                yield
                P.op("act", lambda e, n=n: e.activation(tf["Em"][:, 0:n], tf["gcum"][:, 0:n], AF.Exp, scale=-1.0), r=[("gcum", p_)], w=[("Em", p_)])
                yield
                P.op("act", lambda e, n=n: e.activation(tf["Epr"][:, 0:n], tf["nrm"][:, 0:n], AF.Exp), r=[("nrm", p_)], w=[("Epr", p_)])
                yield
                P.op("act", lambda e, p_=p_, n=n, nch=nch: e.copy(gC[p_][:, 0:nch], tf["Ep"][:, 63:n:64]), r=[("Ep", p_)], w=[("gC", p_)])
                yield
                c3 = lambda ap: ap.rearrange("p (c s) -> p c s", s=64)
                P.op("dve", lambda e, p_=p_, n=n, nch=nch: e.scalar_tensor_tensor(
                    AR[p_][:, 0:nch, 0, :], c3(tf["kk0"][:, 0:n]), -1.0, c3(tf["Epr"][:, 0:n]), ALU.mult, ALU.mult),
                    r=[("kk0", p_), ("Epr", p_)], w=[("AR", p_)])
                P.op("pool", lambda e, p_=p_, n=n, nch=nch, r_=r_: e.tensor_tensor(AR[p_][:, 0:nch, 1, :], c3(r_), c3(tf["Ep"][:, 0:n]), ALU.mult),
                     r=["xm", ("Ep", p_)], w=[("AR", p_)])
                P.op("dve", lambda e, n=n: e.tensor_tensor(tf["bv"][:, 0:n], tf["bv"][:, 0:n], tf["Em"][:, 0:n], ALU.mult), r=[("bv", p_), ("Em", p_)], w=[("bv", p_)])
                yield
                P.op("pool", lambda e, p_=p_, n=n: e.tensor_tensor(tf["kt"][:, 0:n], k2[p_][:, 0:n], tf["Em"][:, 0:n], ALU.mult), r=[("k2", p_), ("Em", p_)], w=[("kt", p_)])
                yield
                P.op("act", lambda e, p_=p_, n=n: e.copy(btb[p_][:, 0:n], tf["bv"][:, 0:n]), r=[("bv", p_)], w=[("btb", p_)])
                yield
                gcb = lambda p_, nch: gC[p_][:, 0:nch].unsqueeze(2).broadcast_to([128, nch, 64])
                P.op("dve", lambda e, p_=p_, n=n, nch=nch: e.tensor_tensor(c3(tf["Bh"][:, 0:n]), c3(tf["bv"][:, 0:n]), gcb(p_, nch), ALU.mult),
                     r=[("bv", p_), ("gC", p_)], w=[("Bh", p_)])
                P.op("pool", lambda e, p_=p_, n=n, nch=nch: e.tensor_tensor(c3(tf["Kh"][:, 0:n]), c3(tf["kt"][:, 0:n]), gcb(p_, nch), ALU.mult),
                     r=[("kt", p_), ("gC", p_)], w=[("Kh", p_)])

                def expand(dst, src3, eng, rk_, wk_, nch=nch):
                    if eng == "act":
                        for hh_ in range(2):
                            P.op("act", lambda e, dst=dst, src3=src3, nch=nch, hh_=hh_: e.activation(
                                dst[:, 0:nch, hh_, :], src3, AF.Copy, scale=hm2[:, hh_:hh_ + 1]), r=rk_ + ["cf"], w=[wk_])
                        return
                    P.op(eng, lambda e, dst=dst, src3=src3, nch=nch: e.tensor_tensor(
                        dst[:, 0:nch, :, :], src3.unsqueeze(2).broadcast_to([128, nch, 2, 64]),
                        hm2.unsqueeze(1).unsqueeze(3).broadcast_to([128, nch, 2, 64]), ALU.mult), r=rk_ + ["cf"], w=[wk_])
                expand(kbd[p_], c3(tf["kt"][:, 0:n]), "act", [("kt", p_)], ("kbd", p_))
                yield
                expand(bbd[p_], c3(tf["bv"][:, 0:n]), "dve", [("bv", p_)], ("bbd", p_))
                yield
                expand(abd[p_], AR[p_][:, 0:nch, 0, :], "act", [("AR", p_)], ("abd", p_))
                yield
                expand(vbd[p_], c3(v_), "dve", ["xm"], ("vbd", p_))
                yield
                expand(Bhbd[p_], c3(tf["Bh"][:, 0:n]), "act", [("Bh", p_)], ("Bhbd", p_))
                yield
                expand(Khbd[p_], c3(tf["Kh"][:, 0:n]), "dve", [("Kh", p_)], ("Khbd", p_))
                yield
            gens = [stageB(0), stageB(1)]
            alive = [True, True]
            while any(alive):
                for gi_ in range(2):
                    if alive[gi_]:
                        try:
                            next(gens[gi_])
                        except StopIteration:
                            alive[gi_] = False
            if RCUT <= 2:
                return
            units = [(c, p_) for c in range(nch) for p_ in range(2)]
            nbat = (len(units) + 3) // 4
            QNB, QTNB, XB = (6, 2), (7, 3), (0, 1)
            for u, (c, p_) in enumerate(units):
                bi_, ui = u // 4, u % 4
                pa = pb[4 + u % 2]
                pak = "pb%d" % (4 + u % 2)
                arc = AR[p_][:, c, :, :].rearrange("p a b -> p (a b)")
                P.op("pe", lambda e, pa=pa, c=c, p_=p_, arc=arc: e.matmul(pa[:, 0:128], kbd[p_][:, c, :, :].rearrange("p a b -> p (a b)"), arc, start=True, stop=True),
                     r=[("kbd", p_), ("AR", p_)], w=[pak])
                P.op("pe", lambda e, pa=pa, c=c, p_=p_, arc=arc: e.matmul(pa[:, 128:256], bbd[p_][:, c, :, :].rearrange("p a b -> p (a b)"), arc, start=True, stop=True),
                     r=[("bbd", p_), ("AR", p_)], w=[pak])
                P.op("pe", lambda e, pa=pa, c=c, p_=p_: e.matmul(pa[:, 256:320], abd[p_][:, c, :, :].rearrange("p a b -> p (a b)"), btb[p_][:, c * 64:(c + 1) * 64], start=True, stop=True),
                     r=[("abd", p_), ("btb", p_)], w=[pak])
                P.op("dve", lambda e, pa=pa, u=u: e.tensor_tensor(Aak[u], pa[:, 0:64].unsqueeze(1).broadcast_to([128, 2, 64]), m_sbd, ALU.mult),
                     r=[pak, "cf"], w=[("Aak", u)])
                P.op("dve", lambda e, pa=pa, u=u: e.tensor_tensor(Ar[u], pa[:, 64:256].rearrange("p (a b) -> p a b", b=64)[:, 0:3:2, :], tri.unsqueeze(1).broadcast_to([128, 2, 64]), ALU.mult),
                     r=[pak, "cf"], w=[("Ar", u)])
                P.op("dve", lambda e, pa=pa, bi_=bi_, ui=ui: e.tensor_tensor(QTbb[bi_][0][:, ui, :].rearrange("p (a b) -> p a b", a=2), pa[:, 128:192].unsqueeze(1).broadcast_to([128, 2, 64]), m_sbd, ALU.mult),
                     r=[pak, "cf"], w=[("QT", bi_, 0)])
                P.op("dve", lambda e, pa=pa, bi_=bi_, ui=ui: e.tensor_tensor(Qbb[bi_][0][:, ui, :].rearrange("p (a b) -> p a b", a=2), pa[:, 256:320].unsqueeze(1).broadcast_to([128, 2, 64]), m_sTbd, ALU.mult),
                     r=[pak, "cf"], w=[("Q", bi_, 0)])
                P.op("pool", lambda e, u=u, bi_=bi_, ui=ui: e.tensor_tensor(XT[u], QTbb[bi_][0][:, ui, :], ident_f, ALU.add), r=[("QT", bi_, 0), "cf"], w=[("XTb", bi_)])
            def inv_sq(jstep):
                src = (jstep - 1) % 2
                for bi_ in range(nbat):
                    nb_ = min(4, len(units) - 4 * bi_)
                    qn, qtn = QNB[bi_], QTNB[bi_]
                    for ui in range(nb_):
                        P.op("pe", lambda e, ui=ui, src=src, bi_=bi_, qn=qn: e.matmul(pb[qn][:, ui * 128:(ui + 1) * 128], QTbb[bi_][src][:, ui, :], Qbb[bi_][src][:, ui, :], start=True, stop=True),
                             r=[("QT", bi_, src), ("Q", bi_, src)], w=["pb%d" % qn])
                    if jstep < 5:
                        for ui in range(nb_):
                            P.op("pe", lambda e, ui=ui, src=src, bi_=bi_, qtn=qtn: e.matmul(pb[qtn][:, ui * 128:(ui + 1) * 128], Qbb[bi_][src][:, ui, :], QTbb[bi_][src][:, ui, :], start=True, stop=True),
                                 r=[("QT", bi_, src), ("Q", bi_, src)], w=["pb%d" % qtn])

            def inv_ev(jstep):
                dst = jstep % 2
                for bi_ in range(nbat):
                    nb_ = min(4, len(units) - 4 * bi_)
                    qn, qtn = QNB[bi_], QTNB[bi_]
                    P.op("act", lambda e, dst=dst, nb_=nb_, bi_=bi_, qn=qn: e.copy(Qbb[bi_][dst][:, 0:nb_, :], pb[qn][:, 0:nb_ * 128].rearrange("p (u c) -> p u c", c=128)),
                         r=["pb%d" % qn], w=[("Q", bi_, dst)])
                    if jstep < 5:
                        P.op("dve", lambda e, dst=dst, nb_=nb_, bi_=bi_, qtn=qtn: e.tensor_copy(QTbb[bi_][dst][:, 0:nb_, :], pb[qtn][:, 0:nb_ * 128].rearrange("p (u c) -> p u c", c=128)),
                             r=["pb%d" % qtn], w=[("QT", bi_, dst)])

            def inv_x(jstep):
                dst = jstep % 2
                for bi_ in range(nbat):
                    nb_ = min(4, len(units) - 4 * bi_)
                    xb = XB[bi_]
                    for ui in range(nb_):
                        u = 4 * bi_ + ui
                        P.op("pe", lambda e, ui=ui, u=u, dst=dst, bi_=bi_, xb=xb: e.matmul(pb[xb][:, ui * 128:(ui + 1) * 128], Qbb[bi_][dst][:, ui, :], XT[u], start=True, stop=True),
                             r=[("Q", bi_, dst), ("XTb", bi_)], w=["pb%d" % xb])

            def inv_add(jstep):
                for bi_ in range(nbat):
                    nb_ = min(4, len(units) - 4 * bi_)
                    xb = XB[bi_]
                    P.op("dve", lambda e, bi_=bi_, nb_=nb_, xb=xb: e.tensor_tensor(XTb[bi_][:, 0:nb_, :], XTb[bi_][:, 0:nb_, :],
                                                                               pb[xb][:, 0:nb_ * 128].rearrange("p (u c) -> p u c", c=128), ALU.add),
                         r=["pb%d" % xb, ("XTb", bi_)], w=[("XTb", bi_)])

            inv_sq(1)
            inv_ev(1)
            for jstep in range(1, 6):
                if jstep < 5:
                    inv_sq(jstep + 1)
                inv_x(jstep)
                if jstep < 5:
                    inv_ev(jstep + 1)
                inv_add(jstep)
            if RCUT <= 3:
                return
            for u, (c, p_) in enumerate(units):
                bk = 6 + u % 2
                for which, srcb in enumerate((vbd, Bhbd, Khbd)):
                    pst = pb[bk][:, which * 64:(which + 1) * 64].bitcast(BF16)
                    P.op("pe", lambda e, pst=pst, srcb=srcb, c=c, p_=p_: e.transpose(pst, srcb[p_][:, c, :, :].rearrange("p a b -> p (a b)"), ident_bf),
                         r=[(("vbd", "Bhbd", "Khbd")[which], p_), "cb"], w=["pb%d" % bk])
                psl = pb[bk][:, 0:192].bitcast(BF16).rearrange("p (w c) -> p w c", w=3)
                if u % 2 == 0:
                    P.op("act", lambda e, psl=psl, u=u: e.copy(TM[u], psl), r=["pb%d" % bk], w=[("TM", u)])
                else:
                    P.op("dve", lambda e, psl=psl, u=u: e.tensor_copy(TM[u], psl), r=["pb%d" % bk], w=[("TM", u)])
            if RCUT <= 4:
                return
            for c in range(nch):
                gc = a // 64 + c
                if gc >= 32:
                    j = gc - 32
                    cur["H"], cur["key"], cur["j"] = H_slot[j], "H_slot", j
                    for p_ in range(2):
                        P.op("act", lambda e, p_=p_, j=j: e.copy(H_bf[p_], H_slot[j][p_]), r=[("H_slot", j, p_)], w=[("H_bf", p_)])
                WB, UB, HB = (1, 0), (2, 6), (3, 7)
                us = [c * 2 + p_ for p_ in range(2)]
                Hfs = [cur["H"][p_] for p_ in range(2)]
                Hks = [("H_p", p_) if cur["j"] is None else ("H_slot", cur["j"], p_) for p_ in range(2)]
                for p_ in range(2):
                    u = us[p_]
                    P.op("pe", lambda e, c=c, p_=p_: e.matmul(pb[WB[p_]][:, 0:128], abd[p_][:, c, :, :].rearrange("p a b -> p (a b)"), H_bf[p_], start=True, stop=False),
                         r=[("abd", p_), ("H_bf", p_)], w=["pb%d" % WB[p_]])
                    P.op("pe", lambda e, u=u, p_=p_: e.matmul(pb[WB[p_]][:, 0:128], Aak[u].rearrange("p a b -> p (a b)"), Vtm[u], start=False, stop=True),
                         r=[("Aak", u), ("TM", u)], w=["pb%d" % WB[p_]])
                for p_ in range(2):
                    if p_ == 0:
                        P.op("act", lambda e, p_=p_: e.copy(Wsb[p_], pb[WB[p_]][:, 0:128]), r=["pb%d" % WB[p_]], w=[("Wsb", p_)])
                    else:
                        P.op("dve", lambda e, p_=p_: e.tensor_copy(Wsb[p_], pb[WB[p_]][:, 0:128]), r=["pb%d" % WB[p_]], w=[("Wsb", p_)])
                for p_ in range(2):
                    u = us[p_]
                    P.op("pe", lambda e, u=u, p_=p_: e.matmul(pb[UB[p_]][:, 0:128], XT[u], Wsb[p_], start=True, stop=True),
                         r=[("XTb", u // 4), ("Wsb", p_)], w=["pb%d" % UB[p_]])
                for p_ in range(2):
                    if p_ == 0:
                        P.op("dve", lambda e, p_=p_: e.tensor_copy(Ubf[p_], pb[UB[p_]][:, 0:128]), r=["pb%d" % UB[p_]], w=[("Ubf", p_)])
                    else:
                        P.op("act", lambda e, p_=p_: e.copy(Ubf[p_], pb[UB[p_]][:, 0:128]), r=["pb%d" % UB[p_]], w=[("Ubf", p_)])
                for p_ in range(2):
                    u = us[p_]
                    yps = pb[4 + p_][:, c * 64:(c + 1) * 64]
                    P.op("pe", lambda e, yps=yps, c=c, p_=p_: e.matmul(yps, H_bf[p_], AR[p_][:, c, 1, :], start=True, stop=False),
                         r=[("H_bf", p_), ("AR", p_)], w=["pb%d" % (4 + p_)])
                    P.op("pe", lambda e, yps=yps, u=u, p_=p_: e.matmul(yps, Ubf[p_], Ar[u][:, 1, :], start=False, stop=False),
                         r=[("Ubf", p_), ("Ar", u)], w=["pb%d" % (4 + p_)])
                    P.op("pe", lambda e, yps=yps, u=u: e.matmul(yps, Vtm[u], Ar[u][:, 0, :], start=False, stop=True),
                         r=[("TM", u), ("Ar", u)], w=["pb%d" % (4 + p_)])
                    P.op("pe", lambda e, u=u, p_=p_: e.matmul(pb[HB[p_]][:, 0:128], Bhtm[u], Ubf[p_], start=True, stop=False),
                         r=[("TM", u), ("Ubf", p_)], w=["pb%d" % HB[p_]])
                    P.op("pe", lambda e, u=u, p_=p_: e.matmul(pb[HB[p_]][:, 0:128], Khtm[u], Vtm[u], start=False, stop=True),
                         r=[("TM", u)], w=["pb%d" % HB[p_]])
                for p_ in range(2):
                    Hf, Hk = Hfs[p_], Hks[p_]
                    P.op("dve", lambda e, Hf=Hf, p_=p_, c=c: e.scalar_tensor_tensor(Hf, Hf, gC[p_][:, c:c + 1], pb[HB[p_]][:, 0:128], ALU.mult, ALU.add),
                         r=["pb%d" % HB[p_], ("gC", p_), Hk], w=[Hk])
                    P.op("act", lambda e, Hf=Hf, p_=p_: e.copy(H_bf[p_], Hf), r=[Hk], w=[("H_bf", p_)])
                if gc >= 31:
                    slot = gc - 31
                    for p_ in range(2):
                        Hf, Hk = Hfs[p_], Hks[p_]
                        P.op("pe", lambda e, Hf=Hf, p_=p_: e.transpose(pb[UB[p_]][:, 0:128], Hf, ident_f), r=[Hk, "cf"], w=["pb%d" % UB[p_]])
                        P.op("dve", lambda e, p_=p_: e.tensor_reduce(Sx[p_], pb[UB[p_]][:, 0:128].rearrange("p (h k) -> p k h", h=2), AX.X, ALU.add),
                             r=["pb%d" % UB[p_]], w=[("Sx", p_)])
                        P.dma("sp", o_rwkv[l, slot, p_ * 128:(p_ + 1) * 128, :], Sx[p_], r=[("Sx", p_)], w=["o_rwkv"])
            if RCUT <= 5:
                return
            def stageD(p_, a=a, n=n):
                PB2, PB3 = (2, 3) if p_ == 0 else (6, 7)
                tf = tmpf[p_]
                P.op("act", lambda e, p_=p_, n=n, tf=tf: e.copy(yT[p_][:, 0:n], pb[4 + p_][:, 0:n]), r=["pb%d" % (4 + p_)], w=[("yT", p_)])
                yield
                P.op("pe", lambda e, p_=p_, n=n, tf=tf: e.matmul(pb[PB2][:, 0:n], blk64_f, yT[p_][:, 0:n], start=True, stop=True), r=[("yT", p_), "cf"], w=["pb%d" % PB2])
                yield
                P.op("dve", lambda e, p_=p_, n=n, tf=tf: e.scalar_tensor_tensor(tf["Ep"][:, 0:n], pb[PB2][:, 0:n], -1.0 / 64.0, yT[p_][:, 0:n], ALU.mult, ALU.add),
                     r=["pb%d" % PB2, ("yT", p_)], w=[("Ep", p_)])
                yield
                P.op("act", lambda e, n=n, p_=p_, tf=tf: e.activation(sqk[p_][:, 0:n], tf["Ep"][:, 0:n], AF.Square), r=[("Ep", p_)], w=[("sqk", p_)])
                yield
                P.op("pe", lambda e, n=n, p_=p_, tf=tf: e.matmul(pb[PB3][:, 0:n], blk64_bf, sqk[p_][:, 0:n], start=True, stop=True), r=[("sqk", p_), "cb"], w=["pb%d" % PB3])
                yield
                P.op("act", lambda e, n=n, p_=p_, tf=tf: e.activation(tf["nrm"][:, 0:n], pb[PB3][:, 0:n], AF.Sqrt, bias=64e-5, scale=1.0 / 64.0), r=["pb%d" % PB3], w=[("nrm", p_)])
                yield
                P.op("dve", lambda e, n=n, p_=p_, tf=tf: e.reciprocal(tf["t"][:, 0:n], tf["nrm"][:, 0:n]), r=[("nrm", p_)], w=[("t", p_)])
                yield
                P.op("dve", lambda e, p_=p_, n=n, tf=tf: e.scalar_tensor_tensor(tf["Em"][:, 0:n], tf["Ep"][:, 0:n], vcol(l, V_LXG + p_), tf["t"][:, 0:n], ALU.mult, ALU.mult),
                     r=[("Ep", p_), ("t", p_), "vecs"], w=[("Em", p_)])
                yield
                P.op("dve", lambda e, p_=p_, n=n, tf=tf: e.scalar_tensor_tensor(tf["Epr"][:, 0:n], xm[:, p_, 0:n], vcol(l, V_RK + p_), k2[p_][:, 0:n], ALU.mult, ALU.mult),
                     r=["xm", ("k2", p_), "vecs"], w=[("Epr", p_)])
                yield
                P.op("pe", lambda e, n=n, p_=p_, tf=tf: e.matmul(pb[PB2][:, 0:n], blk64_f, tf["Epr"][:, 0:n], start=True, stop=True), r=[("Epr", p_), "cf"], w=["pb%d" % PB2])
                yield
                P.op("dve", lambda e, p_=p_, n=n, tf=tf: e.tensor_tensor(tf["kt"][:, 0:n], pb[PB2][:, 0:n], xm[:, 4 + p_, 0:n], ALU.mult), r=["pb%d" % PB2, "xm"], w=[("kt", p_)])
                yield
                P.op("dve", lambda e, p_=p_, n=n, tf=tf: e.scalar_tensor_tensor(tf["Em"][:, 0:n], tf["Em"][:, 0:n], vcol(l, V_LXB + p_), tf["kt"][:, 0:n], ALU.add, ALU.add),
                     r=[("Em", p_), ("kt", p_), "vecs"], w=[("Em", p_)])
                yield
                P.op("pool", lambda e, p_=p_, a=a, n=n, tf=tf: e.tensor_tensor(mixT[:, 4 + p_, a:a + n], tf["Em"][:, 0:n], gate[p_][:, 0:n], ALU.mult),
                     r=[("Em", p_), ("gate", p_)], w=tk("mix", a, n))
                yield


            gensD = [stageD(0), stageD(1)]
            aliveD = [True, True]
            while any(aliveD):
                for gi_ in range(2):
                    if aliveD[gi_]:
                        try:
                            next(gensD[gi_])
                        except StopIteration:
                            aliveD[gi_] = False

    def mixer_phase(l):
        for ti, (a, n) in enumerate(TILES):
            norm_stats(xT[:, :, a:a + n], n, tk("x", a, n), float(D), sq, sd, rstd, 6)
            for c in range(8):
                P.op("dve", lambda e, c=c, a=a, n=n: e.scalar_tensor_tensor(
                    xn[:, c, a:a + n], xT[:, c, a:a + n], gain(l, 2, c), rstd[:, 0:n], ALU.mult, ALU.mult),
                    r=tk("x", a, n) + ["rstd", "vecs"], w=tk("xn", a, n))
        for c in range(8):
            P.dma("sp", xspill[:, c * TT:(c + 1) * TT], xT[:, c, :], r=tk("x", 0, TT), w=["xspill"])
        barrier()
        if STAGE != 3:
            attention(l)
        if STAGE <= 2:
            return
        barrier()
        if "g" in os.environ.get("K_MIX", "gmr"):
            gla(l)
            barrier()
        if "m" in os.environ.get("K_MIX", "gmr"):
            gmlp(l)
            barrier()
        if "r" in os.environ.get("K_MIX", "gmr"):
            rwkv(l)
        if STAGE <= 3:
            return
        barrier()
        for j in range(NSEQ):
            P.op("pool", lambda e, j=j: e.memset(mixT[:, :, T + 64 * j + 4:T + 64 * (j + 1)], 0.0), r=tk("mix", T, 256), w=tk("mix", T, 256))
        for c in range(8):
            P.dma("sp", xT[:, c, :], xspill[:, c * TT:(c + 1) * TT], r=["xspill"], w=tk("x", 0, TT))
        wout = arena[:, XW + 8 * TT:XW + 8 * TT + 4096].bitcast(BF16).rearrange("p (k c) -> p k c", k=8)
        aos = [arena[:, XW + i * 4096:XW + (i + 1) * 4096].rearrange("p (c t) -> p c t", c=8) for i in range(2)]
        P.dma("pool", wout, w_out[l].rearrange("(k p) c -> p k c", p=128), w=["wout"])
        for ti, (a, n) in enumerate(TILES):
            ao = aos[ti % 2]
            aok = ("ao", ti % 2)
            for oc in range(8):
                bk = rr("apj")
                for k in range(8):
                    P.op("pe", lambda e, k=k, bk=bk, oc=oc, a=a, n=n: e.matmul(
                        pb[bk][:, 0:n], wout[:, k, oc * 128:(oc + 1) * 128], mixT[:, k, a:a + n], start=(k == 0), stop=(k == 7)),
                        r=["wout"] + tk("mix", a, n), w=["pb%d" % bk])
                P.op("act", lambda e, bk=bk, oc=oc, n=n, ao=ao: e.copy(ao[:, oc, 0:n], pb[bk][:, 0:n]), r=["pb%d" % bk], w=[aok])
            norm_stats(ao[:, :, 0:n], n, [aok], float(D), sq, sd, rstd, 6)
            for c in range(8):
                q = rr("tmp")
                P.op("dve", lambda e, c=c, q=q, n=n, ao=ao: e.scalar_tensor_tensor(
                    tmpa[q][:, 0:n], ao[:, c, 0:n], gain(l, 3, c), rstd[:, 0:n], ALU.mult, ALU.mult),
                    r=[aok, "rstd", "vecs"], w=[("tmpa", q)])
                P.op("pool" if c % 2 == 0 else "dve", lambda e, c=c, q=q, a=a, n=n: e.tensor_tensor(
                    xT[:, c, a:a + n], xT[:, c, a:a + n], tmpa[q][:, 0:n], ALU.add),
                    r=[("tmpa", q)] + tk("x", a, n), w=tk("x", a, n))
        barrier()
        if STAGE <= 3:
            return
        barrier()

    for l in range(2):
        ffn(l, 0, 0, 0)
        if STAGE <= 1:
            break
        mixer_phase(l)
        if STAGE >= 5:
            ffn(l, 1, 4, 1)
            continue
        if STAGE == 4:
            break
        if STAGE <= 3:
            barrier()
            dtmp = [arena[:, i * 512:(i + 1) * 512] for i in range(2)]
            for c in (range(6, 8) if STAGE == 2 else range(0, 6)):
                for ti, (a, n) in enumerate(TILES):
                    q = rr("dbg")
                    P.op("dve", lambda e, q=q, c=c, a=a, n=n: e.tensor_copy(dtmp[q][:, 0:n], mixT[:, c, a:a + n]),
                         r=tk("mix", a, n), w=[("dtmp", q)])
                    P.dma("sp", dbg[c * 128:(c + 1) * 128, a:a + n], dtmp[q][:, 0:n], r=[("dtmp", q)], w=["dbg"])
            break

    if STAGE <= 1 or STAGE >= 4:
        for c in range(8):
            for (a, n) in ((0, 1024), (1024, 1280)):
                P.dma("sp", yT_out[c * 128:(c + 1) * 128, a:a + n], xT[:, c, a:a + n], r=tk("x", a, n), w=["yT"])
    P.emit(final_keys=finals)
    return nc, P


def pack_vecs(inp):
    v = np.zeros((128, 2 * NVEC), np.float32)
    for l in range(2):
        cols = []
        for k in range(6):
            cols.append(inp["norm_gains"][l, k].reshape(8, 128).T)
        cols.append(inp["gla_b_alpha"][l].reshape(1, 128).T)
        for nm in ("gla_norm", "gmlp_ln_g", "gmlp_ln_b"):
            cols.append(inp[nm][l].reshape(2, 128).T)
        cols.append(inp["rwkv_mu"][l].reshape(8, 128).T)
        for nm in ("rwkv_w0", "rwkv_a0", "rwkv_kk", "rwkv_ka", "rwkv_rk", "rwkv_lnx_g", "rwkv_lnx_b"):
            cols.append(inp[nm][l].reshape(2, 128).T)
        m = np.concatenate(cols, axis=1)
        assert m.shape[1] == NVEC, m.shape
        v[:, l * NVEC:(l + 1) * NVEC] = m
    return v


def core_inputs(inp, i, shared):
    x = np.zeros((TT, D), np.float32)
    x[:T] = inp["x_prompt"][i]
    for j in range(NSEQ):
        x[T + j * SLOT: T + j * SLOT + 4] = inp["x_sample"][4 * i + j]
    sl = slice(4 * i, 4 * i + 4)
    m = dict(shared)
    m["xT_in"] = np.ascontiguousarray(x.T)
    m["c128"] = np.ascontiguousarray(inp["cache_win128"][:, sl].reshape(2, NSEQ, 128, 512))
    m["c512"] = np.ascontiguousarray(inp["cache_win512"][:, sl].reshape(2, NSEQ, 512, 512))
    m["c2048"] = np.ascontiguousarray(inp["cache_win2048"][:, sl].reshape(2, NSEQ, 2048, 512))
    m["state_gla"] = np.ascontiguousarray(inp["state_gla"][:, sl])
    m["state_rwkv"] = np.ascontiguousarray(inp["state_rwkv"][:, sl].reshape(2, NSEQ, 256, 64))
    m["state_shift"] = np.ascontiguousarray(inp["state_shift"][:, sl])
    return m


def shared_inputs(inp):
    return {
        "vecs": pack_vecs(inp), "cf": _CF_ARR, "cb": _CB_ARR, "ca": _CA_ARR,
        "w_ff_gate": inp["w_ff_gate"], "w_ff_up": inp["w_ff_up"], "w_ff_down": inp["w_ff_down"],
        "w_in": inp["w_in"], "w_out": inp["w_out"],
        "gla_w_alpha2": inp["gla_w_alpha2"],
        "gmlp_ln_rows": np.ascontiguousarray(np.stack([inp["gmlp_ln_g"], inp["gmlp_ln_b"]], axis=1)),
        "gmlp_wsT": np.ascontiguousarray(np.swapaxes(inp["gmlp_ws"], 2, 3)),
        "gmlp_bs": inp["gmlp_bs"],
        "rwkv_w2": inp["rwkv_w2"], "rwkv_a2": inp["rwkv_a2"], "rwkv_g2": inp["rwkv_g2"],
    }


_CACHE = {}


def run_cores(inp, cores):
    if "nc" not in _CACHE:
        _CACHE["nc"] = build_program()
    nc, P = _CACHE["nc"]
    shared = shared_inputs(inp)
    in_maps = [core_inputs(inp, i, shared) for i in cores]
    res = run_bass_kernel_spmd(nc, in_maps, core_ids=list(range(len(cores))))
    return res.results


def assemble(res):
    nco = len(res)
    y_p = np.stack([r["yT"].T[:T] for r in res])
    y_s = np.stack([r["yT"].T[T + j * SLOT: T + j * SLOT + 4] for r in res for j in range(NSEQ)])

    def per_prompt(fn):
        return np.stack([np.stack([fn(r, l) for r in res]) for l in range(2)])

    def per_sample(fn):
        return np.stack([np.stack([fn(r, l, j) for r in res for j in range(NSEQ)]) for l in range(2)])

    def kvrows(arr, lo, n):
        return np.ascontiguousarray(arr[:, :, lo:lo + n].transpose(2, 0, 1)).reshape(n, 2, 4, 64)

    p_gla = per_prompt(lambda r, l: r["o_gla"][l, 0].reshape(4, 32, 64))
    p_rwkv = per_prompt(lambda r, l: r["o_rwkv"][l, 0].reshape(4, 64, 64))
    p_shift = per_prompt(lambda r, l: r["o_shift"][l, 0].T.reshape(1024))
    p_w128 = per_prompt(lambda r, l: kvrows(r["okv01"][l, 0], 512 - 128, 128))
    p_w512 = per_prompt(lambda r, l: kvrows(r["okv01"][l, 1], 0, 512))
    p_w2048 = per_prompt(lambda r, l: kvrows(r["okv2"][l], 0, 2048))
    s_gla = per_sample(lambda r, l, j: r["o_gla"][l, 1 + j].reshape(4, 32, 64))
    s_rwkv = per_sample(lambda r, l, j: r["o_rwkv"][l, 1 + j].reshape(4, 64, 64))
    s_shift = per_sample(lambda r, l, j: r["o_shift"][l, 1 + j].T.reshape(1024))
    s_w128 = per_sample(lambda r, l, j: kvrows(r["okv01"][l, 0], 512 + 64 * j, 4))
    s_w512 = per_sample(lambda r, l, j: kvrows(r["okv01"][l, 1], 512 + 64 * j, 4))
    s_w2048 = per_sample(lambda r, l, j: kvrows(r["okv2"][l], T + 64 * j, 4))
    s_gv = per_sample(lambda r, l, j: r["o_gv"][l, j])
    outs = (y_p, y_s, p_gla, p_rwkv, p_shift, p_w128, p_w512, p_w2048, s_gla, s_rwkv, s_shift, s_w128, s_w512, s_w2048, s_gv)
    return tuple(np.ascontiguousarray(o, dtype=np.float32) for o in outs)


def kernel(**inp):
    inp = {k: np.asarray(v) for k, v in inp.items()}
    res = run_cores(inp, list(range(8)))
    return assemble(res)
```
